# Optimizing a Trainium2 kernel written in Bass

```python
import math
import jax
import jax.numpy as jnp
from jax import lax
import numpy as np

D_MODEL = 1024
BATCH = 4
SEQ = 8192
DEPTH = 2
DEC_BATCH = 16
DEC_SEQ = 4096
PAST_LEN = 128

GRID_W = 64
CHUNK = 128
D_A = 1024
N_GROUPS_A = 8
GROUP_A = D_A // N_GROUPS_A
N_HEADS_B = 16
HEAD_DIM_B = 64
D_B = N_HEADS_B * HEAD_DIM_B
WIN_H_MAX = 8
WIN_W = 16
RPB_H = 2 * WIN_H_MAX - 1
RPB_W = 2 * WIN_W - 1
D_FF = 2816
D_IN = 2 * D_A + 3 * D_B + 2 * D_MODEL
EPS = 1e-6

kernel_name = "hybrid_gmlp_natten_macaron_encoder"


def rmsnorm(x, g):
    xf = x.astype(jnp.float32)
    y = xf * lax.rsqrt(jnp.mean(xf * xf, axis=-1, keepdims=True) + EPS)
    return (y * g.astype(jnp.float32)).astype(x.dtype)


def layernorm(x, g, b):
    xf = x.astype(jnp.float32)
    mu = jnp.mean(xf, axis=-1, keepdims=True)
    var = jnp.mean(jnp.square(xf - mu), axis=-1, keepdims=True)
    y = (xf - mu) * lax.rsqrt(var + EPS)
    return (y * g.astype(jnp.float32) + b.astype(jnp.float32)).astype(x.dtype)


def swiglu(x, w_gate, w_up, w_down):
    return (jax.nn.silu(x @ w_gate) * (x @ w_up)) @ w_down


def spatial_gating(u, v, ln_g, ln_b, w_s, b_s):
    bsz, s, _ = v.shape
    v = layernorm(v, ln_g, ln_b)
    v = v.reshape(bsz, s // CHUNK, CHUNK, N_GROUPS_A, GROUP_A)
    mixed = jnp.einsum('gpq,bnqgc->bnpgc', w_s, v) + jnp.transpose(b_s)[None, None, :, :, None]
    return u * mixed.reshape(bsz, s, D_A)


def neighbourhood_attention(q, k, v, rpb):
    bsz, s, _ = q.shape
    rows = s // GRID_W
    kh = min(WIN_H_MAX, rows)
    scale = HEAD_DIM_B ** -0.5
    qg = (q * scale).reshape(bsz, rows, GRID_W, N_HEADS_B, HEAD_DIM_B)
    kg = k.reshape(bsz, rows, GRID_W, N_HEADS_B, HEAD_DIM_B)
    vg = v.reshape(bsz, rows, GRID_W, N_HEADS_B, HEAD_DIM_B)

    col = jnp.arange(GRID_W, dtype=jnp.int32)
    col_start = jnp.clip(col - WIN_W // 2, 0, GRID_W - WIN_W)
    col_idx = col_start[:, None] + jnp.arange(WIN_W, dtype=jnp.int32)[None, :]
    col_off = col_idx - col[:, None] + (WIN_W - 1)
    rpb_cols = rpb[:, :, col_off]

    row = jnp.arange(rows, dtype=jnp.int32)
    row_start = jnp.clip(row - kh // 2, 0, rows - kh)
    q_rows = jnp.transpose(qg, (1, 0, 2, 3, 4))

    def one_row(args):
        q_r, r, rs = args
        k_band = lax.dynamic_slice_in_dim(kg, rs, kh, axis=1)
        v_band = lax.dynamic_slice_in_dim(vg, rs, kh, axis=1)
        k_win = k_band[:, :, col_idx]
        v_win = v_band[:, :, col_idx]
        scores = jnp.einsum('bqhd,bkqwhd->bhqkw', q_r, k_win).astype(jnp.float32)
        row_off = rs + jnp.arange(kh, dtype=jnp.int32) - r + (WIN_H_MAX - 1)
        bias = jnp.transpose(rpb_cols[:, row_off], (0, 2, 1, 3))
        scores = scores + bias[None].astype(jnp.float32)
        p = jax.nn.softmax(scores.reshape(bsz, N_HEADS_B, GRID_W, kh * WIN_W), axis=-1)
        p = p.reshape(bsz, N_HEADS_B, GRID_W, kh, WIN_W).astype(v.dtype)
        return jnp.einsum('bhqkw,bkqwhd->bqhd', p, v_win)

    out = lax.map(one_row, (q_rows, row, row_start))
    return jnp.transpose(out, (1, 0, 2, 3, 4)).reshape(bsz, s, D_B)


def trunk(x, ffn1_norm, ffn1_w_gate, ffn1_w_up, ffn1_w_down, mix_norm, w_in, b_gate,
          sgu_ln_g, sgu_ln_b, sgu_w_s, sgu_b_s, nat_rpb, w_branch_a, w_branch_b, w_out,
          ffn2_norm, ffn2_w_gate, ffn2_w_up, ffn2_w_down, final_norm):
    splits = [D_A, 2 * D_A, 2 * D_A + D_B, 2 * D_A + 2 * D_B, 2 * D_A + 3 * D_B,
              2 * D_A + 3 * D_B + D_MODEL]
    h = x
    for l in range(DEPTH):
        h = h + 0.5 * swiglu(rmsnorm(h, ffn1_norm[l]), ffn1_w_gate[l], ffn1_w_up[l], ffn1_w_down[l])
        n = rmsnorm(h, mix_norm[l])
        z = n @ w_in[l]
        u, v, q, k, vv, ga, gb = jnp.split(z, splits, axis=-1)
        ya = spatial_gating(jax.nn.gelu(u, approximate=False), jax.nn.gelu(v, approximate=False),
                            sgu_ln_g[l], sgu_ln_b[l], sgu_w_s[l], sgu_b_s[l])
        yb = neighbourhood_attention(q, k, vv, nat_rpb[l])
        gate_a = jax.nn.sigmoid(ga + b_gate[l, 0])
        gate_b = jax.nn.sigmoid(gb + b_gate[l, 1])
        merged = gate_a * (ya @ w_branch_a[l]) + gate_b * (yb @ w_branch_b[l])
        h = h + merged @ w_out[l]
        h = h + 0.5 * swiglu(rmsnorm(h, ffn2_norm[l]), ffn2_w_gate[l], ffn2_w_up[l], ffn2_w_down[l])
    return rmsnorm(h, final_norm)


def setup_inputs(seed: int = 0) -> dict:
    key = jax.random.key(seed)
    ks = jax.random.split(key, 24)
    f32 = jnp.float32

    def nrm(k, shape, scale):
        return jax.random.normal(k, shape, f32) * scale

    def gain(k, shape):
        return 1.0 + 0.05 * jax.random.normal(k, shape, f32)

    return {
        "x_prompt": jax.random.normal(ks[0], (BATCH, SEQ, D_MODEL), f32),
        "x_sample": jax.random.normal(ks[1], (DEC_BATCH, DEC_SEQ, D_MODEL), f32),
        "ffn1_norm": gain(ks[2], (DEPTH, D_MODEL)),
        "ffn1_w_gate": nrm(ks[3], (DEPTH, D_MODEL, D_FF), D_MODEL ** -0.5),
        "ffn1_w_up": nrm(ks[4], (DEPTH, D_MODEL, D_FF), D_MODEL ** -0.5),
        "ffn1_w_down": nrm(ks[5], (DEPTH, D_FF, D_MODEL), D_FF ** -0.5),
        "mix_norm": gain(ks[6], (DEPTH, D_MODEL)),
        "w_in": nrm(ks[7], (DEPTH, D_MODEL, D_IN), D_MODEL ** -0.5),
        "b_gate": nrm(ks[8], (DEPTH, 2, D_MODEL), 0.01),
        "sgu_ln_g": gain(ks[9], (DEPTH, D_A)),
        "sgu_ln_b": nrm(ks[10], (DEPTH, D_A), 0.02),
        "sgu_w_s": nrm(ks[11], (DEPTH, N_GROUPS_A, CHUNK, CHUNK), CHUNK ** -0.5),
        "sgu_b_s": 1.0 + nrm(ks[12], (DEPTH, N_GROUPS_A, CHUNK), 0.1),
        "nat_rpb": nrm(ks[13], (DEPTH, N_HEADS_B, RPB_H, RPB_W), 0.1),
        "w_branch_a": nrm(ks[14], (DEPTH, D_A, D_MODEL), D_A ** -0.5),
        "w_branch_b": nrm(ks[15], (DEPTH, D_B, D_MODEL), D_B ** -0.5),
        "w_out": nrm(ks[16], (DEPTH, D_MODEL, D_MODEL), D_MODEL ** -0.5),
        "ffn2_norm": gain(ks[17], (DEPTH, D_MODEL)),
        "ffn2_w_gate": nrm(ks[18], (DEPTH, D_MODEL, D_FF), D_MODEL ** -0.5),
        "ffn2_w_up": nrm(ks[19], (DEPTH, D_MODEL, D_FF), D_MODEL ** -0.5),
        "ffn2_w_down": nrm(ks[20], (DEPTH, D_FF, D_MODEL), D_FF ** -0.5),
        "final_norm": gain(ks[21], (D_MODEL,)),
    }


def reference(x_prompt, x_sample, ffn1_norm, ffn1_w_gate, ffn1_w_up, ffn1_w_down, mix_norm, w_in,
              b_gate, sgu_ln_g, sgu_ln_b, sgu_w_s, sgu_b_s, nat_rpb, w_branch_a, w_branch_b, w_out,
              ffn2_norm, ffn2_w_gate, ffn2_w_up, ffn2_w_down, final_norm):
    y_prompt = trunk(x_prompt, ffn1_norm, ffn1_w_gate, ffn1_w_up, ffn1_w_down, mix_norm, w_in, b_gate,
                     sgu_ln_g, sgu_ln_b, sgu_w_s, sgu_b_s, nat_rpb, w_branch_a, w_branch_b, w_out,
                     ffn2_norm, ffn2_w_gate, ffn2_w_up, ffn2_w_down, final_norm)
    y_sample = trunk(x_sample, ffn1_norm, ffn1_w_gate, ffn1_w_up, ffn1_w_down, mix_norm, w_in, b_gate,
                     sgu_ln_g, sgu_ln_b, sgu_w_s, sgu_b_s, nat_rpb, w_branch_a, w_branch_b, w_out,
                     ffn2_norm, ffn2_w_gate, ffn2_w_up, ffn2_w_down, final_norm)
    return (y_prompt, y_sample)
```

```python
import bisect
from contextlib import ExitStack

import numpy as np
import concourse.bass as bass
import concourse.mybir as mybir
from concourse.bass_utils import run_bass_kernel_spmd

F32 = mybir.dt.float32
BF16 = mybir.dt.bfloat16
AF = mybir.ActivationFunctionType
ALU = mybir.AluOpType

D = 1024
DFF = 2816
NFC = DFF // 128
NWSLOT = 6
DIN = 7168
NL = 2
NH = 16
EPS = 1e-6
ENG = ("pe", "act", "dve", "pool", "sp")

PROD_CFG = dict(
    nrow=192,
    segs=(((0, 128), (128, 192)), ((0, 64), (64, 128), (128, 192))),
)


class Buf:
    __slots__ = ("w", "r")

    def __init__(self):
        self.w = {}
        self.r = {}


def bufs(n):
    return [Buf() for _ in range(n)]


class Prog:
    def __init__(self, nc):
        self.nc = nc
        self.q = {e: [] for e in ENG}
        self.sigidx = {e: [] for e in ENG}
        self.seen = {e: {} for e in ENG}
        self.dmasem = {}

    def _token(self, ref):
        if ref[0] == "s":
            return ref[1], ref[2]
        _, eng, idx = ref
        si = self.sigidx[eng]
        j = bisect.bisect_left(si, idx)
        if j < len(si):
            return eng, j + 1
        q = self.q[eng]
        last = len(q) - 1
        while q[last][0] != "op":
            last -= 1
        assert last >= idx
        q[last][3] = (eng, 1)
        si.append(last)
        return eng, len(si)

    def _resolve(self, eng, refs):
        wl = []
        for ref in refs:
            if ref is None:
                continue
            key, val = self._token(ref)
            if self.seen[eng].get(key, 0) >= val:
                continue
            self.seen[eng][key] = val
            wl.append((key, val))
        return wl

    def _deps(self, eng, reads, writes, waits):
        refs = list(waits)
        for b in reads:
            refs += list(b.w.values())
        for b in writes:
            for k, r in b.w.items():
                if not (r[0] == "e" and r[1] == eng):
                    refs.append(r)
            for k, r in b.r.items():
                if not (r[0] == "e" and r[1] == eng):
                    refs.append(r)
        if eng == "pe":
            refs = [r for r in refs if not (r[0] == "e" and r[1] == "pe")]
        return refs

    def op(self, eng, fn, reads=(), writes=(), waits=()):
        wl = self._resolve(eng, self._deps(eng, reads, writes, waits))
        idx = len(self.q[eng])
        self.q[eng].append(["op", fn, wl, None])
        ref = ("e", eng, idx)
        for b in reads:
            b.r[eng] = ref
        for b in writes:
            b.w = {eng: ref}
            b.r = {}
        return ref

    def dma(self, queue, sem, pairs, reads=(), writes=(), waits=()):
        wl = self._resolve(queue, self._deps(queue, reads, writes, waits))
        cur = self.dmasem.get(sem, 0)
        first = True
        for o, i in pairs:
            cur += 16
            self.q[queue].append(["dma", (lambda e, o=o, i=i: e.dma_start(out=o, in_=i)), wl if first else [], (sem, 16)])
            first = False
        self.dmasem[sem] = cur
        ref = ("s", sem, cur)
        for b in reads:
            b.r[sem] = ref
        for b in writes:
            b.w = {sem: ref}
            b.r = {}
        return ref

    def barrier(self):
        refs = []
        for e in ENG:
            q = self.q[e]
            for i in range(len(q) - 1, -1, -1):
                if q[i][0] == "op":
                    refs.append(("e", e, i))
                    break
        for s, v in self.dmasem.items():
            refs.append(("s", s, v))
        toks = [self._token(r) for r in refs]
        for e in ENG:
            wl = []
            for key, val in toks:
                if key == e:
                    continue
                if self.seen[e].get(key, 0) >= val:
                    continue
                self.seen[e][key] = val
                wl.append((key, val))
            if wl:
                self.q[e].append(["wait", None, wl, None])

    def run(self):
        nc = self.nc
        with ExitStack() as st:
            sems = {}
            for n in list(ENG) + list(self.dmasem.keys()):
                sems[n] = st.enter_context(nc.semaphore("s_" + n))
            block = st.enter_context(nc.Block())

            def replay(name):
                def f(e):
                    for kind, fn, wl, sig in self.q[name]:
                        for key, val in wl:
                            e.wait_ge(sems[key], val)
                        if fn is None:
                            continue
                        ins = fn(e)
                        if sig is not None:
                            ins.then_inc(sems[sig[0]], sig[1])

                return f

            block.tensor(replay("pe"))
            block.scalar(replay("act"))
            block.vector(replay("dve"))
            block.gpsimd(replay("pool"))
            block.sync(replay("sp"))


def _window(r, segs):
    for (s, e) in segs:
        if s <= r < e:
            rows = e - s
            rs = s + min(max((r - s) - 4, 0), rows - 8)
            return rs, rs + 8
    raise AssertionError


def attn_plan(r0, nq, segs_modes):
    allowed = {}
    for qr in range(r0, r0 + nq):
        wins = [_window(qr, segs) for segs in segs_modes]
        for kr in range(min(w[0] for w in wins), max(w[1] for w in wins)):
            inm = [w[0] <= kr < w[1] for w in wins]
            if any(inm):
                allowed[(kr, qr)] = "b" if all(inm) else (0 if inm[0] else 1)
    chunks = sorted(set(kr // 2 for (kr, qr) in allowed))
    plan = []
    for c in chunks:
        qrs = [qr for (kr, qr) in allowed if kr // 2 == c]
        qa, qb = min(qrs), max(qrs)
        runs, zero = [], []
        for hf in range(2):
            kr = 2 * c + hf
            cur = None
            for qr in range(qa, qb + 1):
                cls = allowed.get((kr, qr))
                if cls is None:
                    zero.append((hf, qr))
                    cur = None
                    continue
                if cur is not None and cur[3] == cls and cur[2] == qr - 1:
                    cur[2] = qr
                else:
                    cur = [hf, qr, qr, cls]
                    runs.append(cur)
        plan.append(dict(c=c, qa=qa, qb=qb, runs=[tuple(r) for r in runs], zero=zero))
    full = [p for p in plan if p["qa"] == r0 and p["qb"] == r0 + nq - 1]
    assert full
    pref = [p for p in full if p["c"] == r0 // 2]
    f0 = pref[0] if pref else full[0]
    plan.remove(f0)
    plan.insert(0, f0)
    return plan


def build_program(cfg, debug=False):
    NROW = cfg["nrow"]
    NTOK = NROW * 64
    NCH = NTOK // 128
    segs_modes = cfg["segs"]
    nc = bass.Bass("TRN2", target_bir_lowering=False)
    P = Prog(nc)

    _uid = [0]
    _sbuf_tensor, _psum_tensor = nc.sbuf_tensor, nc.psum_tensor

    def _sb(name, shape, dt):
        _uid[0] += 1
        return _sbuf_tensor("%s_u%d" % (name, _uid[0]), shape, dt)

    def _pp(name, shape, dt):
        _uid[0] += 1
        return _psum_tensor("%s_u%d" % (name, _uid[0]), shape, dt)

    def din(name, shape, dt=F32):
        return nc.dram_tensor(name, list(shape), dt, kind="ExternalInput").ap()

    def dscr(name, shape, dt, out=False):
        return nc.dram_tensor(name, list(shape), dt, kind=("ExternalOutput" if (out and debug) else "Internal")).ap()

    x_in = din("x", [NTOK, D])
    y_out = nc.dram_tensor("y", [NTOK, D], F32, kind="ExternalOutput").ap()
    wsrc = {}
    for pre in ("ffn1", "ffn2"):
        wsrc[pre + "_w_gate"] = din(pre + "_w_gate", [NL, D, DFF])
        wsrc[pre + "_w_up"] = din(pre + "_w_up", [NL, D, DFF])
        wsrc[pre + "_w_down"] = din(pre + "_w_down", [NL, DFF, D])
    wsrc["w_in"] = din("w_in", [NL, D, DIN])
    for n in ("w_branch_a", "w_branch_b", "w_out"):
        wsrc[n] = din(n, [NL, D, D])
    norms = din("norms", [7, D])
    lngT = din("lngT", [NL, 128, 8])
    lnb = din("lnb", [NL, D])
    bgT = din("bgT", [NL, 2, 128, 8])
    wsT = din("wsT", [NL, 8, 128, 128])
    bs = din("bs", [NL, 8, 128])
    rpbpad = din("rpbpad", [NL, 7536])
    ident_in = din("ident", [128, 128])
    colmask_in = din("colmask", [128, 64])
    mm_in = din("modemask", [128, 2])

    wb = {k: dscr(k + "_b", v.shape, BF16) for k, v in wsrc.items()}
    h1s = dscr("h1s", [NTOK, D], F32, out=True)
    h2s = dscr("h2s", [NTOK, D], F32, out=True)
    qs = dscr("qs", [8, 128, NTOK], BF16, out=True)
    gbs = dscr("gbs", [8, 128, NTOK], BF16, out=True)
    mas = dscr("mas", [8, 128, NTOK], BF16, out=True)
    ks = dscr("ks", [NCH, 128, 1024], BF16, out=True)
    vs = dscr("vs", [NCH, 128, 1024], BF16, out=True)

    _dbg_done = [False]
    if debug:
        dbg_nT = nc.dram_tensor("dbg_nT", [128, 8 * 512], BF16, kind="ExternalOutput").ap()
        dbg_aT = nc.dram_tensor("dbg_aT", [128, NFC * 512], BF16, kind="ExternalOutput").ap()
        dbg_nT2 = nc.dram_tensor("dbg_nT2", [128, 8 * 512], BF16, kind="ExternalOutput").ap()
        dbg_uT = nc.dram_tensor("dbg_uT", [128, 8 * 512], BF16, kind="ExternalOutput").ap()
        dbg_rs = nc.dram_tensor("dbg_rs", [128, 4], F32, kind="ExternalOutput").ap()
    cast_pairs = []
    for k, src in wsrc.items():
        K = src.shape[1]
        for l in range(NL):
            for r in range(0, K, 128):
                cast_pairs.append((wb[k][l, r:r + 128, :], src[l, r:r + 128, :]))
    cast_ref = P.dma("pool", "cast", cast_pairs)

    class PS:
        def __init__(self, st):
            self.t = [st.enter_context(_pp("psb%d" % i, [128, 512], F32)) for i in range(8)]
            self.b = bufs(8)
            self.n = 0

        def get(self):
            i = self.n % 8
            self.n += 1
            return self.t[i], self.b[i]

    class WStream:
        def __init__(self, st, nslot, name):
            self.ns = nslot
            self.t = [st.enter_context(_sb("%s%d" % (name, i), [128, 4096], BF16)) for i in range(nslot)]
            self.b = bufs(nslot)
            self.name = name
            self.plan = []
            self.issued = 0
            self.cur = 0

        def add(self, kind, src):
            self.plan.append((kind, src))

        def view(self, i, kind):
            t = self.t[i % self.ns]
            if kind == "up":
                return t[:].rearrange("p (a n) -> p a n", a=8)
            return t[:].rearrange("p (a n) -> p a n", a=4)

        def ensure(self, upto):
            upto = min(len(self.plan), upto)
            while self.issued < upto:
                j = self.issued
                kind, src = self.plan[j]
                v = self.view(j, kind)
                if src.shape[1] != v.shape[1]:
                    v = v[:, 0:src.shape[1], :]
                if src.shape[2] != v.shape[2]:
                    v = v[:, :, 0:src.shape[2]]
                P.dma("sp", "%s%d" % (self.name, j % self.ns), [(v, src)], writes=[self.b[j % self.ns]], waits=[cast_ref])
                self.issued += 1

        def next(self, kind):
            i = self.cur
            self.cur += 1
            assert self.plan[i][0] == kind, (i, kind, self.plan[i][0])
            self.ensure(i + self.ns - 1)
            return self.view(i, kind), self.b[i % self.ns]

    def plan_ffn(ws, wg, wu, wd):
        for fg in range(6):
            ncol = 512 if fg < 5 else 256
            ws.add("up", w_up_src(wg, fg * 512, ncol))
            ws.add("up", w_up_src(wu, fg * 512, ncol))
        for dg in range(6):
            nj = 4 if dg < 5 else 2
            ws.add("down", w_down_src(wd, dg * 4, nj))

    def w_up_src(w_l, c0, ncol):
        return w_l[:, c0:c0 + ncol].rearrange("(a p) n -> p a n", p=128)

    def w_down_src(w_l, j0, nj):
        return w_l[j0 * 128:(j0 + nj) * 128, :].rearrange("(a p) n -> p a n", p=128)

    def norm_transpose(E, xt, XB, g_rep, GB, nT, NTB, ntiles, pre=None):
        ps, ident, IB = E["ps"], E["ident"], E["IB"]
        ss, SSB, rstd, RSB = E["ss"], E["SSB"], E["rstd"], E["RSB"]
        junk, JB = E["junk"], E["JB"]
        state = {}

        def stage1(t):
            if pre is not None:
                pre(t)
            P.op("act", lambda e, t=t: e.activation(out=junk[:], in_=xt[:, t, :], func=AF.Square, accum_out=ss[:, t:t + 1]),
                 reads=[XB[t]], writes=[JB, SSB[t]])
            P.op("act", lambda e, t=t: e.activation(out=rstd[:, t:t + 1], in_=ss[:, t:t + 1], func=AF.Sqrt, scale=1.0 / D, bias=E["epsc"][:, 0:1]),
                 reads=[SSB[t], E["EPSB"]], writes=[RSB[t]])

        def stage2(t):
            P.op("dve", lambda e, t=t: e.reciprocal(out=rstd[:, t:t + 1], in_=rstd[:, t:t + 1]), reads=[RSB[t]], writes=[RSB[t]])
            nt, NB = E["nt"][t % 2], E["NB"][t % 2]
            P.op("dve", lambda e, t=t, nt=nt: e.scalar_tensor_tensor(out=nt[:], in0=xt[:, t, :], scalar=rstd[:, t:t + 1], in1=g_rep,
                                                                     op0=ALU.mult, op1=ALU.mult),
                 reads=[XB[t], RSB[t], GB], writes=[NB])
            bank, BB = ps.get()
            bv = bank[:].bitcast(BF16)
            for c in range(8):
                P.op("pe", lambda e, c=c, bv=bv, nt=nt: e.transpose(bv[:, c * 128:(c + 1) * 128], nt[:, c * 128:(c + 1) * 128], ident[:]),
                     reads=[NB, IB], writes=[BB] if c == 0 else [])
            BB.w = {"pe": ("e", "pe", len(P.q["pe"]) - 1)}
            state[t] = (bv, BB)

        def stage3(t):
            bv, BB = state.pop(t)
            P.op("act", lambda e, t=t, bv=bv: e.activation(out=nT[:, :, t * 128:(t + 1) * 128], in_=bv.rearrange("p (c k) -> p c k", c=8), func=AF.Copy),
                 reads=[BB], writes=[NTB[t]])

        for t in range(ntiles):
            stage1(t)
            if t >= 1:
                stage2(t - 1)
            if t >= 2:
                stage3(t - 2)
        stage2(ntiles - 1)
        if ntiles >= 2:
            stage3(ntiles - 2)
        stage3(ntiles - 1)

    def ffn(E, xt, XB, g_rep, GB, ws, ntiles, skip_norm=False, pre=None, defer=False, mid=None):
        ps = E["ps"]
        nT, NTB, aT, AB = E["nT"], E["NTB"], E["aT"], E["AB"]
        T = ntiles * 128
        if not skip_norm:
            norm_transpose(E, xt, XB, g_rep, GB, nT, NTB, ntiles, pre=pre)
        else:
            assert pre is None
        for fg in range(6):
            nchk = 4 if fg < 5 else 2
            sg_v, SGB = ws.next("up")
            su_v, SUB = ws.next("up")
            for jj in range(nchk):
                j = fg * 4 + jj
                bg, BG = ps.get()
                for dc in range(8):
                    P.op("pe", lambda e, dc=dc, jj=jj, bg=bg, sg_v=sg_v: e.matmul(bg[:, 0:T], lhsT=sg_v[:, dc, jj * 128:(jj + 1) * 128], rhs=nT[:, dc, 0:T],
                                                                                  start=(dc == 0), stop=(dc == 7)),
                         reads=NTB[0:ntiles] + [SGB], writes=[BG] if dc == 0 else [])
                BG.w = {"pe": ("e", "pe", len(P.q["pe"]) - 1)}
                bu, BU = ps.get()
                for dc in range(8):
                    P.op("pe", lambda e, dc=dc, jj=jj, bu=bu, su_v=su_v: e.matmul(bu[:, 0:T], lhsT=su_v[:, dc, jj * 128:(jj + 1) * 128], rhs=nT[:, dc, 0:T],
                                                                                  start=(dc == 0), stop=(dc == 7)),
                         reads=NTB[0:ntiles] + [SUB], writes=[BU] if dc == 0 else [])
                BU.w = {"pe": ("e", "pe", len(P.q["pe"]) - 1)}
                sgt, SB_ = E["sg"][j % 2], E["SGB"][j % 2]
                P.op("act", lambda e, bg=bg, sgt=sgt: e.activation(out=sgt[:, 0:T], in_=bg[:, 0:T], func=AF.Silu), reads=[BG], writes=[SB_])
                P.op("dve", lambda e, j=j, bu=bu, sgt=sgt: e.tensor_tensor(out=aT[:, j, 0:T], in0=bu[:, 0:T], in1=sgt[:, 0:T], op=ALU.mult),
                     reads=[BU, SB_], writes=[AB[j]])
        if debug and not _dbg_done[0]:
            _dbg_done[0] = True
            P.dma("pool", "dbg", [(dbg_nT[:, :], nT[:, :, :].rearrange("p c t -> p (c t)")), (dbg_aT[:, :], aT[:, :, :].rearrange("p c t -> p (c t)")),
                                  (dbg_rs[:, :], E["rstd"][:, :])], reads=NTB[0:ntiles] + AB + E["RSB"])
        if mid is not None:
            mid()
        banks = [[ps.get() for _ in range(2)] for _ in range(ntiles)]
        for dg in range(6):
            nj = 4 if dg < 5 else 2
            sd_v, SDB = ws.next("down")
            for jj in range(nj):
                j = dg * 4 + jj
                for t in range(ntiles):
                    for hf in range(2):
                        bk, BK = banks[t][hf]
                        P.op("pe", lambda e, j=j, jj=jj, t=t, hf=hf, bk=bk, sd_v=sd_v: e.matmul(
                            bk[:, :], lhsT=aT[:, j, t * 128:(t + 1) * 128], rhs=sd_v[:, jj, hf * 512:(hf + 1) * 512], start=(j == 0), stop=(j == NFC - 1)),
                             reads=[AB[j], SDB], writes=[BK] if j == 0 else [])
        lastpe = ("e", "pe", len(P.q["pe"]) - 1)
        for t in range(ntiles):
            for hf in range(2):
                banks[t][hf][1].w = {"pe": lastpe}

        def evac(t):
            for hf in range(2):
                bk, BK = banks[t][hf]
                P.op("dve", lambda e, t=t, hf=hf, bk=bk: e.scalar_tensor_tensor(out=xt[:, t, hf * 512:(hf + 1) * 512], in0=bk[:, :], scalar=0.5,
                                                                                 in1=xt[:, t, hf * 512:(hf + 1) * 512], op0=ALU.mult, op1=ALU.add),
                     reads=[BK, XB[t]], writes=[XB[t]])
        if defer:
            return evac
        for t in range(ntiles):
            evac(t)
        return None

    def common_env(st, ntiles, with_ffn=True):
        E = {}
        sb = lambda n, s, d: st.enter_context(_sb(n, s, d))
        E["ps"] = PS(st)
        E["ident"] = sb("ident", [128, 128], BF16)
        E["IB"] = Buf()
        identf = sb("identf", [128, 128], F32)
        IFB = Buf()
        P.dma("sp", "c_ident", [(identf[:], ident_in[:, :])], writes=[IFB])
        P.op("dve", lambda e: e.tensor_copy(out=E["ident"][:], in_=identf[:]), reads=[IFB], writes=[E["IB"]])
        E["ss"] = sb("ss", [128, 4], F32)
        E["SSB"] = bufs(4)
        E["rstd"] = sb("rstd", [128, 4], F32)
        E["RSB"] = bufs(4)
        E["junk"] = sb("junk", [128, 1024], BF16)
        E["JB"] = Buf()
        E["epsc"] = sb("epsc", [128, 1], F32)
        E["EPSB"] = Buf()
        P.op("dve", lambda e: e.memset(E["epsc"][:], EPS), writes=[E["EPSB"]])
        E["nt"] = [sb("nt%d" % i, [128, 1024], BF16) for i in range(2)]
        E["NB"] = bufs(2)
        E["nT"] = sb("nT", [128, 8, ntiles * 128], BF16)
        E["NTB"] = bufs(ntiles)
        if with_ffn:
            E["aT"] = sb("aT", [128, NFC, ntiles * 128], BF16)
            E["AB"] = bufs(NFC)
            E["sg"] = [sb("sg%d" % i, [128, ntiles * 128], BF16) for i in range(2)]
            E["SGB"] = bufs(2)
        return E, sb

    def load_grep(sb, idxs):
        out = []
        for i in idxs:
            t = sb("grep%d" % i, [128, 1024], F32)
            B = Buf()
            src = bass.AP(norms.tensor, norms[i:i + 1, :].offset, [[0, 128], [1, 1024]])
            P.dma("sp", "c_grep%d" % i, [(t[:], src)], writes=[B])
            out.append((t[:], B))
        return out

    def loop_A(l, hin, pre_ffn2):
        with ExitStack() as st:
            NT = 4
            T = 512
            E, sb = common_env(st, NT)
            ps = E["ps"]
            ws = WStream(st, NWSLOT, "wA")
            nsteps = NTOK // T
            gi = [3 * l, 3 * l + 1] + ([3 * (l - 1) + 2] if pre_ffn2 else [])
            greps = load_grep(sb, gi)
            xt = [sb("xt%d" % i, [128, NT, 1024], F32) for i in range(2)]
            XB = [bufs(NT) for _ in range(2)]
            aT, AB = E["aT"], E["AB"]
            uT = aT[:, 0:8, :]; UB = AB[0:8]
            gaT = aT[:, 8:16, :]; GAB = AB[8:16]
            yaT = sb("yaT", [128, 8, T], BF16); YAB = bufs(NT)
            vg = [sb("vg%d" % i, [128, 1024], F32) for i in range(2)]; VGB = bufs(2)
            vh = sb("vh", [128, NT, 1024], BF16); VHB = bufs(NT)
            NSTG = 3
            stg = [sb("stg%d" % i, [128, 8 * T], BF16) for i in range(NSTG)]; STB = bufs(NSTG)
            stg_n = [0]
            tmp = [sb("sgut%d" % i, [128, 4, 128], F32) for i in range(2)]; TMB = bufs(2)
            bst = sb("bst", [128, 12], F32); BSTB = bufs(2)
            mv = sb("mv", [128, 4], F32); MVB = bufs(2)
            lng_t = sb("lng_t", [128, 8], F32); LNGB = Buf()
            P.dma("sp", "c_lng", [(lng_t[:], lngT[l])], writes=[LNGB])
            bg_t = sb("bg_t", [128, 2, 8], F32); BGB = Buf()
            P.dma("sp", "c_bg", [(bg_t[:, 0, :], bgT[l, 0]), (bg_t[:, 1, :], bgT[l, 1])], writes=[BGB])
            wsf = sb("wsf", [128, 8, 128], F32); WSFB = Buf()
            P.dma("sp", "c_wsf", [(wsf[:], wsT[l].rearrange("g q p -> q g p"))], writes=[WSFB])
            wst = sb("wst", [128, 8, 128], BF16); WSTB = Buf()
            P.op("dve", lambda e: e.tensor_copy(out=wst[:], in_=wsf[:]), reads=[WSFB], writes=[WSTB])
            onesf = sb("onesf", [128, 1], F32); ONB = Buf()
            P.op("dve", lambda e: e.memset(onesf[:], 1.0), writes=[ONB])
            l2 = sb("l2", [2, 1024], F32); L2B = Buf()
            P.op("dve", lambda e: e.memset(l2[:], 1.0), writes=[L2B])
            P.dma("sp", "c_l2", [(l2[0:1, :], lnb[l:l + 1, :])], writes=[L2B])
            r2 = sb("r2", [2, 8, 128], F32); R2B = Buf()
            P.dma("sp", "c_r2", [(r2[1:2, :, :], bs[l:l + 1, :, :])], writes=[R2B])
            bias2 = sb("bias2", [128, 8, 128], F32); B2B = Buf()
            for half in range(2):
                bk, BK = ps.get()
                for g4 in range(4):
                    g = half * 4 + g4
                    P.op("pe", lambda e, g=g, g4=g4, bk=bk: e.matmul(bk[0:1, g4 * 128:(g4 + 1) * 128], lhsT=onesf[:, 0:1], rhs=wsf[:, g, :], start=True, stop=True),
                         reads=[ONB, WSFB], writes=[BK] if g4 == 0 else [])
                BK.w = {"pe": ("e", "pe", len(P.q["pe"]) - 1)}
                P.op("dve", lambda e, half=half, bk=bk: e.tensor_copy(out=r2[0:1, half * 4:(half + 1) * 4, :], in_=bk[0:1, :].rearrange("p (a b) -> p a b", a=4)),
                     reads=[BK], writes=[])
            R2B.w["dve"] = ("e", "dve", len(P.q["dve"]) - 1)
            for half in range(2):
                bk, BK = ps.get()
                for g4 in range(4):
                    g = half * 4 + g4
                    P.op("pe", lambda e, g=g, g4=g4, bk=bk: e.matmul(bk[:, g4 * 128:(g4 + 1) * 128], lhsT=l2[0:2, g * 128:(g + 1) * 128], rhs=r2[0:2, g, :], start=True, stop=True),
                         reads=[L2B, R2B], writes=[BK] if g4 == 0 else [])
                BK.w = {"pe": ("e", "pe", len(P.q["pe"]) - 1)}
                P.op("dve", lambda e, half=half, bk=bk: e.tensor_copy(out=bias2[:, half * 4:(half + 1) * 4, :], in_=bk[:, :].rearrange("p (a b) -> p a b", a=4)),
                     reads=[BK], writes=[])
            B2B.w = {"dve": ("e", "dve", len(P.q["dve"]) - 1)}

            win_l = wb["w_in"][l]
            for s in range(nsteps):
                if pre_ffn2:
                    plan_ffn(ws, wb["ffn2_w_gate"][l - 1], wb["ffn2_w_up"][l - 1], wb["ffn2_w_down"][l - 1])
                plan_ffn(ws, wb["ffn1_w_gate"][l], wb["ffn1_w_up"][l], wb["ffn1_w_down"][l])
                for cg in range(14):
                    ws.add("up", w_up_src(win_l, cg * 512, 512))
                for cg in range(2):
                    ws.add("up", w_up_src(wb["w_branch_a"][l], cg * 512, 512))

            def load_x(s):
                sl = s % 2
                tok0 = s * T
                pairs = [(xt[sl][:, t, :], hin[tok0 + t * 128: tok0 + (t + 1) * 128, :]) for t in range(NT)]
                P.dma("sp", "ldx%d" % sl, pairs, writes=XB[sl])

            def stage():
                i = stg_n[0] % NSTG
                stg_n[0] += 1
                return stg[i], STB[i], "st_stg%d" % i

            def lastref(eng):
                return ("e", eng, len(P.q[eng]) - 1)

            load_x(0)
            for s in range(nsteps):
                sl = s % 2
                tok0 = s * T
                x, XBs = xt[sl], XB[sl]
                if s + 1 < nsteps:
                    load_x(s + 1)
                nT, NTB = E["nT"], E["NTB"]
                hoisted = s > 0
                pv = None
                if pre_ffn2:
                    pv = ffn(E, x, XBs, greps[2][0], greps[2][1], ws, NT, skip_norm=hoisted, defer=True)
                    pv = ffn(E, x, XBs, greps[0][0], greps[0][1], ws, NT, pre=pv, defer=True)
                else:
                    pv = ffn(E, x, XBs, greps[0][0], greps[0][1], ws, NT, skip_norm=hoisted, defer=True)
                norm_transpose(E, x, XBs, greps[1][0], greps[1][1], nT, NTB, NT, pre=pv)
                P.dma("pool", "st_h1_%d" % sl, [(h1s[tok0 + t * 128: tok0 + (t + 1) * 128, :], x[:, t, :]) for t in range(NT)], reads=XBs)

                def fm_group(cgi, evac):
                    sv, SVB = ws.next("up")
                    for jj in range(4):
                        bk, BK = ps.get()
                        for dc in range(8):
                            P.op("pe", lambda e, dc=dc, jj=jj, bk=bk, sv=sv: e.matmul(bk[:, :], lhsT=sv[:, dc, jj * 128:(jj + 1) * 128], rhs=nT[:, dc, :],
                                                                                      start=(dc == 0), stop=(dc == 7)),
                                 reads=NTB + [SVB], writes=[BK] if dc == 0 else [])
                        BK.w = {"pe": lastref("pe")}
                        evac(cgi * 4 + jj, bk, BK)

                def tm_group(half, evac):
                    sv, SVB = ws.next("up")
                    for t in range(NT):
                        bk, BK = ps.get()
                        for dc in range(8):
                            P.op("pe", lambda e, dc=dc, t=t, bk=bk, sv=sv: e.matmul(bk[:, :], lhsT=nT[:, dc, t * 128:(t + 1) * 128], rhs=sv[:, dc, :],
                                                                                    start=(dc == 0), stop=(dc == 7)),
                                 reads=[NTB[t], SVB], writes=[BK] if dc == 0 else [])
                        BK.w = {"pe": lastref("pe")}
                        evac(t, half, bk, BK)

                for cgi in (0, 1):
                    fm_group(cgi, lambda c, bk, BK: P.op("act", lambda e, c=c, bk=bk: e.activation(out=uT[:, c, :], in_=bk[:, :], func=AF.Gelu),
                                                         reads=[BK], writes=[UB[c]]))
                if debug and s == 0 and l == 0:
                    P.dma("pool", "dbg2", [(dbg_nT2[:, :], nT[:, :, :].rearrange("p c t -> p (c t)")), (dbg_uT[:, :], aT[:, 0:8, :].rearrange("p c t -> p (c t)"))], reads=NTB + UB)
                sv0, SVB0 = ws.next("up")
                sv1, SVB1 = ws.next("up")
                for t in range(NT):
                    i = t % 2
                    for half, (sv, SVB) in enumerate(((sv0, SVB0), (sv1, SVB1))):
                        bk, BK = ps.get()
                        for dc in range(8):
                            P.op("pe", lambda e, dc=dc, t=t, bk=bk, sv=sv: e.matmul(bk[:, :], lhsT=nT[:, dc, t * 128:(t + 1) * 128], rhs=sv[:, dc, :],
                                                                                    start=(dc == 0), stop=(dc == 7)),
                                 reads=[NTB[t], SVB], writes=[BK] if dc == 0 else [])
                        BK.w = {"pe": lastref("pe")}
                        if half == 0:
                            P.op("act", lambda e, i=i, bk=bk: e.activation(out=vg[i][:, 0:512], in_=bk[:, :], func=AF.Gelu), reads=[BK], writes=[VGB[i]])
                        else:
                            P.op("act", lambda e, i=i, bk=bk: e.activation(out=vg[i][:, 512:1024], in_=bk[:, :], func=AF.Gelu), reads=[BK], writes=[])
                            VGB[i].w = {"act": lastref("act")}
                    P.op("dve", lambda e, i=i: e.bn_stats(out=bst[:, 0:6], in_=vg[i][:, 0:512]), reads=[VGB[i]], writes=[BSTB[0]])
                    P.op("dve", lambda e, i=i: e.bn_stats(out=bst[:, 6:12], in_=vg[i][:, 512:1024]), reads=[VGB[i]], writes=[BSTB[1]])
                    P.op("dve", lambda e: e.bn_aggr(out=mv[:, 0:2], in_=bst[:, 0:12]), reads=BSTB, writes=[MVB[0]])
                    P.op("act", lambda e: e.activation(out=mv[:, 2:3], in_=mv[:, 1:2], func=AF.Sqrt, scale=1.0, bias=E["epsc"][:, 0:1]),
                         reads=[MVB[0], E["EPSB"]], writes=[MVB[1]])
                    P.op("dve", lambda e: e.reciprocal(out=mv[:, 3:4], in_=mv[:, 2:3]), reads=[MVB[1]], writes=[MVB[1]])
                    P.op("dve", lambda e, t=t, i=i: e.tensor_scalar(out=vh[:, t, :], in0=vg[i][:, :], scalar1=mv[:, 0:1], scalar2=mv[:, 3:4],
                                                                    op0=ALU.subtract, op1=ALU.mult),
                         reads=[VGB[i], MVB[0], MVB[1]], writes=[VHB[t]])
                qst, QSB, qsem = stage()
                qv = qst[:].rearrange("p (c t) -> p c t", c=8)
                for cgi in (0, 1):
                    fm_group(cgi, lambda c, bk, BK: P.op("dve", lambda e, c=c, bk=bk, qv=qv: e.tensor_scalar(out=qv[:, c, :], in0=bk[:, :], scalar1=0.125, scalar2=None, op0=ALU.mult),
                                                         reads=[BK], writes=[QSB] if c == 0 else []))
                QSB.w = {"dve": lastref("dve")}
                P.dma("pool", qsem, [(qs[:, :, tok0:tok0 + T].rearrange("c p t -> p c t"), qv)], reads=[QSB])
                kst, KSB, ksem = stage()
                kv = kst[:].rearrange("p (t c k) -> p t c k", t=NT, c=8)
                for cgi in (0, 1):
                    fm_group(cgi, lambda c, bk, BK: P.op("dve", lambda e, c=c, bk=bk, kv=kv: e.tensor_copy(out=kv[:, :, c, :], in_=bk[:, :].rearrange("p (t k) -> p t k", t=NT)),
                                                         reads=[BK], writes=[KSB] if c == 0 else []))
                KSB.w = {"dve": lastref("dve")}
                P.dma("pool", ksem, [(ks[tok0 // 128: tok0 // 128 + NT, :, :].rearrange("t p f -> p t f"), kst[:].rearrange("p (t f) -> p t f", t=NT))], reads=[KSB])
                vst, VSB, vsem = stage()
                vv = vst[:].rearrange("p (t f) -> p t f", t=NT)
                first = [True]

                def evac_vv(t, half, bk, BK):
                    P.op("dve", lambda e, t=t, half=half, bk=bk, vv=vv: e.tensor_copy(out=vv[:, t, half * 512:(half + 1) * 512], in_=bk[:, :]),
                         reads=[BK], writes=[VSB] if first[0] else [])
                    first[0] = False
                for half in (0, 1):
                    tm_group(half, evac_vv)
                VSB.w = {"dve": lastref("dve")}
                P.dma("pool", vsem, [(vs[tok0 // 128: tok0 // 128 + NT, :, :].rearrange("t p f -> p t f"), vv)], reads=[VSB])
                for cgi in (0, 1):
                    fm_group(cgi, lambda c, bk, BK: P.op("act", lambda e, c=c, bk=bk: e.activation(out=gaT[:, c, :], in_=bk[:, :], func=AF.Sigmoid, bias=bg_t[:, 0, c:c + 1], scale=1.0),
                                                         reads=[BK, BGB], writes=[GAB[c]]))
                gst, GSB, gsem = stage()
                gv = gst[:].rearrange("p (c t) -> p c t", c=8)
                for cgi in (0, 1):
                    fm_group(cgi, lambda c, bk, BK: P.op("act", lambda e, c=c, bk=bk, gv=gv: e.activation(out=gv[:, c, :], in_=bk[:, :], func=AF.Sigmoid, bias=bg_t[:, 1, c:c + 1], scale=1.0),
                                                         reads=[BK, BGB], writes=[GSB] if c == 0 else []))
                GSB.w = {"act": lastref("act")}
                P.dma("pool", gsem, [(gbs[:, :, tok0:tok0 + T].rearrange("c p t -> p c t"), gv)], reads=[GSB])
                for t in range(NT):
                    for half in range(2):
                        bk, BK = ps.get()
                        for g4 in range(4):
                            g = half * 4 + g4
                            P.op("pe", lambda e, g=g, g4=g4, t=t, bk=bk: e.matmul(bk[:, g4 * 128:(g4 + 1) * 128], lhsT=vh[:, t, g * 128:(g + 1) * 128], rhs=wst[:, g, :],
                                                                                  start=True, stop=True),
                                 reads=[VHB[t], WSTB], writes=[BK] if g4 == 0 else [])
                        BK.w = {"pe": lastref("pe")}
                        tm, TB = tmp[half], TMB[half]
                        for g4 in range(4):
                            g = half * 4 + g4
                            P.op("dve", lambda e, g=g, g4=g4, bk=bk, tm=tm: e.scalar_tensor_tensor(out=tm[:, g4, :], in0=bk[:, g4 * 128:(g4 + 1) * 128], scalar=lng_t[:, g:g + 1],
                                                                                                  in1=bias2[:, g, :], op0=ALU.mult, op1=ALU.add),
                                 reads=[BK, LNGB, B2B], writes=[TB] if g4 == 0 else [])
                        TB.w = {"dve": lastref("dve")}
                        P.op("dve", lambda e, half=half, t=t, tm=tm: e.tensor_tensor(out=yaT[:, half * 4:(half + 1) * 4, t * 128:(t + 1) * 128], in0=tm[:, :, :],
                                                                                     in1=uT[:, half * 4:(half + 1) * 4, t * 128:(t + 1) * 128], op=ALU.mult),
                             reads=[TB] + UB[half * 4:(half + 1) * 4], writes=[YAB[t]] if half == 0 else [])
                    YAB[t].w = {"dve": lastref("dve")}
                if s + 1 < nsteps:
                    gfirst = greps[2] if pre_ffn2 else greps[0]
                    norm_transpose(E, xt[(s + 1) % 2], XB[(s + 1) % 2], gfirst[0], gfirst[1], nT, NTB, NT)
                mst, MSB, msem = stage()
                mvv = mst[:].rearrange("p (c t) -> p c t", c=8)
                for cg in range(2):
                    sv, SVB = ws.next("up")
                    for jj in range(4):
                        c = cg * 4 + jj
                        bk, BK = ps.get()
                        for cc in range(8):
                            P.op("pe", lambda e, cc=cc, jj=jj, bk=bk, sv=sv: e.matmul(bk[:, :], lhsT=sv[:, cc, jj * 128:(jj + 1) * 128], rhs=yaT[:, cc, :],
                                                                                      start=(cc == 0), stop=(cc == 7)),
                                 reads=YAB + [SVB], writes=[BK] if cc == 0 else [])
                        BK.w = {"pe": lastref("pe")}
                        P.op("dve", lambda e, c=c, bk=bk, mvv=mvv: e.tensor_tensor(out=mvv[:, c, :], in0=bk[:, :], in1=gaT[:, c, :], op=ALU.mult),
                             reads=[BK, GAB[c]], writes=[MSB] if c == 0 else [])
                MSB.w = {"dve": lastref("dve")}
                P.dma("pool", msem, [(mas[:, :, tok0:tok0 + T].rearrange("c p t -> p c t"), mvv)], reads=[MSB])
            assert ws.cur == len(ws.plan)
            P.barrier()

    def loop_B(l, hout):
        with ExitStack() as st:
            sb = lambda n, s, d: st.enter_context(_sb(n, s, d))
            ps = PS(st)
            NQR = 4
            T = 256
            nsteps = NROW // NQR
            NSLOT = 8
            wbt = sb("wbt", [128, 8, 1024], BF16); WBB = Buf()
            wot = sb("wot", [128, 8, 1024], BF16); WOB = Buf()
            P.dma("sp", "c_wb", [(wbt[:], wb["w_branch_b"][l].rearrange("(a p) n -> p a n", p=128))], writes=[WBB], waits=[cast_ref])
            P.dma("sp", "c_wo", [(wot[:], wb["w_out"][l].rearrange("(a p) n -> p a n", p=128))], writes=[WOB], waits=[cast_ref])
            mm_t = sb("mm_t", [128, 2], F32); MMB = Buf()
            P.dma("sp", "c_mm", [(mm_t[:], mm_in[:, :])], writes=[MMB])
            cm_t = sb("cm_t", [128, 64], F32); CMB = Buf()
            P.dma("sp", "c_cm", [(cm_t[:], colmask_in[:, :])], writes=[CMB])
            etab = sb("etab", [128, NH * 15 * 64], BF16); ETB = Buf()
            with ExitStack() as st2:
                hk = [st2.enter_context(_sb("hk%d" % i, [128, 4 * 15 * 64], F32)) for i in range(2)]; HKB = bufs(2)
                for hg in range(4):
                    i = hg % 2
                    base = rpbpad[l:l + 1, hg * 4 * 465: hg * 4 * 465 + 1].offset
                    src = bass.AP(rpbpad.tensor, base, [[1, 64], [465, 4], [31, 15], [1, 64]])
                    dst0 = hk[i][0:64, :].rearrange("p (h r q) -> p h r q", h=4, r=15)
                    dst1 = hk[i][64:128, :].rearrange("p (h r q) -> p h r q", h=4, r=15)
                    P.dma("sp", "c_hk%d" % i, [(dst0, src), (dst1, src)], writes=[HKB[i]])
                    P.op("act", lambda e, i=i: e.activation(out=hk[i][:, :], in_=hk[i][:, :], func=AF.Exp), reads=[HKB[i]], writes=[HKB[i]])
                    a = hk[i][:, 0:64]
                    rev = bass.AP(a.tensor, a.offset + 63, [list(a.ap[0]), [64, 60], [-1, 64]])
                    c0_ = cm_t[:, :]
                    cmb = bass.AP(c0_.tensor, c0_.offset, [list(c0_.ap[0]), [0, 60], [1, 64]])
                    P.op("dve", lambda e, hg=hg, rev=rev, cmb=cmb: e.tensor_tensor(out=etab[:, hg * 3840:(hg + 1) * 3840].rearrange("p (a q) -> p a q", q=64), in0=rev, in1=cmb, op=ALU.mult),
                         reads=[HKB[i], CMB], writes=[])
                    HKB[i].r["dve"] = ("e", "dve", len(P.q["dve"]) - 1)
                ETB.w = {"dve": ("e", "dve", len(P.q["dve"]) - 1)}
                P.barrier()
            kring = sb("kring", [128, NSLOT, 1024], BF16); KRB = bufs(NSLOT)
            vring = sb("vring", [128, NSLOT, 8, 192], BF16); VRB = bufs(NSLOT); VONE = Buf()
            P.op("dve", lambda e: e.memset(vring[:, :, :, 64:128], 1.0), writes=[VONE])
            qt = [sb("qt%d" % i, [128, 8, T], BF16) for i in range(2)]; QB = bufs(2)
            gbt = [sb("gbt%d" % i, [128, 8, T], BF16) for i in range(2)]; GBB = bufs(2)
            mat = [sb("mat%d" % i, [128, 8, T], BF16) for i in range(2)]; MAB = bufs(2)
            ht = [sb("ht%d" % i, [128, 2, 1024], F32) for i in range(2)]; HB = [bufs(2) for _ in range(2)]
            NG = 3
            NPX = 2 * NG
            pexp = [sb("pexp%d" % i, [128, 512], BF16) for i in range(NPX)]; PXB = bufs(NPX)
            pm = [sb("pm%d" % i, [128, 512], BF16) for i in range(NPX)]; PMB = bufs(NPX)
            pmzero = [set() for _ in range(NPX)]
            rr = [sb("rr%d" % i, [128, T], F32) for i in range(2)]; RRB = bufs(2)
            ybT = sb("ybT", [128, 8, T], BF16); YBB = bufs(8)
            mT = sb("mT", [128, 8, T], BF16); MTB = bufs(8)
            loaded = [-1]
            plans = [attn_plan(s * NQR, NQR, segs_modes) for s in range(nsteps)]

            def load_chunks(upto):
                while loaded[0] < upto:
                    c = loaded[0] + 1
                    sl = c % NSLOT
                    P.dma("sp", "ldk%d" % sl, [(kring[:, sl, :], ks[c])], writes=[KRB[sl]])
                    vsrc = vs[c].rearrange("p (a b f) -> p a b f", a=8, b=2)
                    P.dma("sp", "ldv%d" % sl, [(vring[:, sl, :, 0:64], vsrc[:, :, 0, :]), (vring[:, sl, :, 128:192], vsrc[:, :, 1, :])], writes=[VRB[sl]])
                    loaded[0] = c

            def load_step(s):
                i = s % 2
                tok0 = s * T
                P.dma("sp", "ldq%d" % i, [(qt[i][:], qs[:, :, tok0:tok0 + T].rearrange("c p t -> p c t"))], writes=[QB[i]])
                load_chunks(max(p["c"] for p in plans[s]))
                P.dma("sp", "ldgb%d" % i, [(gbt[i][:], gbs[:, :, tok0:tok0 + T].rearrange("c p t -> p c t"))], writes=[GBB[i]])
                P.dma("sp", "ldma%d" % i, [(mat[i][:], mas[:, :, tok0:tok0 + T].rearrange("c p t -> p c t"))], writes=[MAB[i]])
                P.dma("sp", "ldh%d" % i, [(ht[i][:, t, :], h1s[tok0 + t * 128: tok0 + (t + 1) * 128, :]) for t in range(2)], writes=HB[i])

            load_step(0)
            pxn = [0]
            for s in range(nsteps):
                i = s % 2
                tok0 = s * T
                r0 = s * NQR
                if s + 1 < nsteps:
                    load_step(s + 1)
                plan = plans[s]
                groups = []
                cur, used = [], 0
                for p in plan:
                    nqc = (p["qb"] - p["qa"] + 1) * 64
                    if used + nqc > 512:
                        groups.append(cur)
                        cur, used = [], 0
                    cur.append((p, used, nqc))
                    used += nqc
                groups.append(cur)

                has_special = any(r[3] != "b" for p in plan for r in p["runs"])

                def emit_scores(h):
                    hp, dc = h % 2, h // 2
                    res = []
                    for grp in groups:
                        bk, BK = ps.get()
                        for gi_, (p, off, nqc) in enumerate(grp):
                            sl = p["c"] % NSLOT
                            c0 = (p["qa"] - r0) * 64
                            P.op("pe", lambda e, bk=bk, off=off, nqc=nqc, sl=sl, c0=c0, hp=hp, dc=dc, i=i: e.matmul(
                                bk[:, off:off + nqc], lhsT=kring[hp * 64:(hp + 1) * 64, sl, dc * 128:(dc + 1) * 128],
                                rhs=qt[i][hp * 64:(hp + 1) * 64, dc, c0:c0 + nqc], start=True, stop=True),
                                 reads=[KRB[sl], QB[i]], writes=[BK] if gi_ == 0 else [])
                        BK.w = {"pe": ("e", "pe", len(P.q["pe"]) - 1)}
                        res.append((bk, BK, grp))
                    return res

                def emit_x(h, sc):
                    eng = "pool" if (h % 2 == 1 and not has_special) else "dve"
                    assert len(sc) <= NG
                    for gix, (bk, BK, grp) in enumerate(sc):
                        k = (h % 2) * NG + gix
                        tot = grp[-1][1] + grp[-1][2]
                        P.op("act", lambda e, bk=bk, k=k, tot=tot: e.activation(out=pexp[k][:, 0:tot], in_=bk[:, 0:tot], func=AF.Exp), reads=[BK], writes=[PXB[k]])
                        nops = 0
                        for (p, off, nqc) in grp:
                            for (hf, qf, ql, cls) in p["runs"]:
                                kr = 2 * p["c"] + hf
                                n = ql - qf + 1
                                ri0 = kr - qf + 7
                                co = off + (qf - p["qa"]) * 64
                                eb = etab[hf * 64:(hf + 1) * 64, (h * 15 + ri0) * 64:(h * 15 + ri0) * 64 + 64]
                                ein = bass.AP(eb.tensor, eb.offset, [list(eb.ap[0]), [-64, n], [1, 64]])
                                o_ap = pm[k][hf * 64:(hf + 1) * 64, co:co + n * 64].rearrange("p (a q) -> p a q", q=64)
                                for bb in range(n):
                                    pmzero[k].discard((hf, co // 64 + bb))
                                i_ap = pexp[k][hf * 64:(hf + 1) * 64, co:co + n * 64].rearrange("p (a q) -> p a q", q=64)
                                if cls == "b":
                                    P.op(eng, lambda e, o_ap=o_ap, i_ap=i_ap, ein=ein: e.tensor_tensor(out=o_ap, in0=i_ap, in1=ein, op=ALU.mult),
                                         reads=[PXB[k], ETB], writes=[PMB[k]] if nops == 0 else [])
                                else:
                                    P.op(eng, lambda e, o_ap=o_ap, i_ap=i_ap, ein=ein, cls=cls, hf=hf: e.scalar_tensor_tensor(
                                        out=o_ap, in0=i_ap, scalar=mm_t[hf * 64:(hf + 1) * 64, cls:cls + 1], in1=ein, op0=ALU.mult, op1=ALU.mult),
                                         reads=[PXB[k], ETB, MMB], writes=[PMB[k]] if nops == 0 else [])
                                nops += 1
                            for (hf, qr) in p["zero"]:
                                co = off + (qr - p["qa"]) * 64
                                if (hf, co // 64) in pmzero[k]:
                                    continue
                                pmzero[k].add((hf, co // 64))
                                P.op(eng, lambda e, k=k, hf=hf, co=co: e.memset(pm[k][hf * 64:(hf + 1) * 64, co:co + 64], 0.0),
                                     reads=[], writes=[PMB[k]] if nops == 0 else [])
                                nops += 1
                        PMB[k].w = {eng: ("e", eng, len(P.q[eng]) - 1)}
                        PXB[k].r[eng] = ("e", eng, len(P.q[eng]) - 1)

                def emit_v(h, sc):
                    hp, dc = h % 2, h // 2
                    ob, OB = ps.get()
                    firstpv = True
                    for gix, (bk, BK, grp) in enumerate(sc):
                        k = (h % 2) * NG + gix
                        for gi_, (p, off, nqc) in enumerate(grp):
                            sl = p["c"] % NSLOT
                            lh = vring[:, sl, dc, 0:128] if hp == 0 else vring[:, sl, dc, 64:192]
                            c0 = (p["qa"] - r0) * 64
                            lastpv = (gix == len(sc) - 1) and (gi_ == len(grp) - 1)
                            P.op("pe", lambda e, ob=ob, c0=c0, nqc=nqc, lh=lh, k=k, off=off, fp=firstpv, lp=lastpv: e.matmul(
                                ob[:, c0:c0 + nqc], lhsT=lh, rhs=pm[k][:, off:off + nqc], start=fp, stop=lp),
                                 reads=[PMB[k], VRB[sl], VONE], writes=[OB] if firstpv else [])
                            firstpv = False
                    OB.w = {"pe": ("e", "pe", len(P.q["pe"]) - 1)}
                    return ob, OB

                def emit_n(h, ob, OB):
                    hp, dc = h % 2, h // 2
                    j = h % 2
                    slo, shi = (64, 128) if hp == 0 else (0, 64)
                    nlo, nhi = (0, 64) if hp == 0 else (64, 128)
                    P.op("act", lambda e, ob=ob, j=j, slo=slo, shi=shi: e.activation(out=rr[j][slo:shi, :], in_=ob[slo:shi, 0:T], func=AF.Ln), reads=[OB], writes=[RRB[j]])
                    P.op("act", lambda e, j=j, slo=slo, shi=shi: e.activation(out=rr[j][slo:shi, :], in_=rr[j][slo:shi, :], func=AF.Exp, scale=-1.0), reads=[RRB[j]], writes=[RRB[j]])
                    P.op("dve", lambda e, ob=ob, j=j, slo=slo, shi=shi, nlo=nlo, nhi=nhi, dc=dc: e.tensor_tensor(
                        out=ybT[nlo:nhi, dc, :], in0=ob[nlo:nhi, 0:T], in1=rr[j][slo:shi, :], op=ALU.mult),
                         reads=[OB, RRB[j]], writes=[YBB[dc]] if hp == 0 else [])
                    YBB[dc].w = {"dve": ("e", "dve", len(P.q["dve"]) - 1)}

                scs = {0: emit_scores(0)}
                if NH > 1:
                    scs[1] = emit_scores(1)
                emit_x(0, scs[0])
                for h in range(NH):
                    if h + 2 < NH:
                        scs[h + 2] = emit_scores(h + 2)
                    if h + 1 < NH:
                        emit_x(h + 1, scs[h + 1])
                    ob, OB = emit_v(h, scs[h])
                    emit_n(h, ob, OB)
                    del scs[h]
                for c in range(8):
                    bk, BK = ps.get()
                    for cc in range(8):
                        P.op("pe", lambda e, c=c, cc=cc, bk=bk: e.matmul(bk[:, 0:T], lhsT=wbt[:, cc, c * 128:(c + 1) * 128], rhs=ybT[:, cc, :], start=(cc == 0), stop=(cc == 7)),
                             reads=YBB + [WBB], writes=[BK] if cc == 0 else [])
                    BK.w = {"pe": ("e", "pe", len(P.q["pe"]) - 1)}
                    P.op("dve", lambda e, c=c, bk=bk, i=i: e.tensor_tensor(out=mT[:, c, :], in0=bk[:, 0:T], in1=gbt[i][:, c, :], op=ALU.mult), reads=[BK, GBB[i]], writes=[MTB[c]])
                    P.op("dve", lambda e, c=c, i=i: e.tensor_tensor(out=mT[:, c, :], in0=mT[:, c, :], in1=mat[i][:, c, :], op=ALU.add), reads=[MTB[c], MAB[i]], writes=[MTB[c]])
                for t in range(2):
                    for hf in range(2):
                        bk, BK = ps.get()
                        for cc in range(8):
                            P.op("pe", lambda e, t=t, hf=hf, cc=cc, bk=bk: e.matmul(bk[:, :], lhsT=mT[:, cc, t * 128:(t + 1) * 128], rhs=wot[:, cc, hf * 512:(hf + 1) * 512],
                                                                                    start=(cc == 0), stop=(cc == 7)),
                                 reads=MTB + [WOB], writes=[BK] if cc == 0 else [])
                        BK.w = {"pe": ("e", "pe", len(P.q["pe"]) - 1)}
                        P.op("dve", lambda e, t=t, hf=hf, bk=bk, i=i: e.tensor_tensor(out=ht[i][:, t, hf * 512:(hf + 1) * 512], in0=bk[:, :], in1=ht[i][:, t, hf * 512:(hf + 1) * 512], op=ALU.add),
                             reads=[BK, HB[i][t]], writes=[HB[i][t]])
                P.dma("pool", "st_h2_%d" % i, [(hout[tok0 + t * 128: tok0 + (t + 1) * 128, :], ht[i][:, t, :]) for t in range(2)], reads=HB[i])
            P.barrier()

    def loop_C(hin):
        with ExitStack() as st:
            NT = 4
            T = 512
            E, sb = common_env(st, NT)
            ws = WStream(st, NWSLOT, "wC")
            greps = load_grep(sb, [5, 6])
            xt = [sb("xc%d" % i, [128, NT, 1024], F32) for i in range(2)]
            XB = [bufs(NT) for _ in range(2)]
            yt = [sb("yc%d" % i, [128, NT, 1024], F32) for i in range(2)]
            YB = [bufs(NT) for _ in range(2)]
            nsteps = NTOK // T
            for s in range(nsteps):
                plan_ffn(ws, wb["ffn2_w_gate"][NL - 1], wb["ffn2_w_up"][NL - 1], wb["ffn2_w_down"][NL - 1])

            def load_x(s):
                sl = s % 2
                tok0 = s * T
                P.dma("sp", "ldc%d" % sl, [(xt[sl][:, t, :], hin[tok0 + t * 128: tok0 + (t + 1) * 128, :]) for t in range(NT)], writes=XB[sl])
            load_x(0)
            for s in range(nsteps):
                sl = s % 2
                tok0 = s * T
                if s + 1 < nsteps:
                    load_x(s + 1)
                x, XBs = xt[sl], XB[sl]
                nT, NTB = E["nT"], E["NTB"]
                mid = None
                if s + 1 < nsteps:
                    mid = (lambda s=s: norm_transpose(E, xt[(s + 1) % 2], XB[(s + 1) % 2], greps[0][0], greps[0][1], nT, NTB, NT))
                pv = ffn(E, x, XBs, greps[0][0], greps[0][1], ws, NT, skip_norm=(s > 0), defer=True, mid=mid)
                ss, SSB, rstd, RSB, junk, JB = E["ss"], E["SSB"], E["rstd"], E["RSB"], E["junk"], E["JB"]
                for t in range(NT):
                    pv(t)
                    P.op("act", lambda e, t=t, x=x: e.activation(out=junk[:], in_=x[:, t, :], func=AF.Square, accum_out=ss[:, t:t + 1]), reads=[XBs[t]], writes=[JB, SSB[t]])
                    P.op("act", lambda e, t=t: e.activation(out=rstd[:, t:t + 1], in_=ss[:, t:t + 1], func=AF.Sqrt, scale=1.0 / D, bias=E["epsc"][:, 0:1]),
                         reads=[SSB[t], E["EPSB"]], writes=[RSB[t]])
                for t in range(NT):
                    P.op("dve", lambda e, t=t: e.reciprocal(out=rstd[:, t:t + 1], in_=rstd[:, t:t + 1]), reads=[RSB[t]], writes=[RSB[t]])
                    P.op("dve", lambda e, t=t, x=x, sl=sl: e.scalar_tensor_tensor(out=yt[sl][:, t, :], in0=x[:, t, :], scalar=rstd[:, t:t + 1], in1=greps[1][0], op0=ALU.mult, op1=ALU.mult),
                         reads=[XBs[t], RSB[t], greps[1][1]], writes=[YB[sl][t]])
                P.dma("pool", "st_y%d" % sl, [(y_out[tok0 + t * 128: tok0 + (t + 1) * 128, :], yt[sl][:, t, :]) for t in range(NT)], reads=YB[sl])
            P.barrier()

    P.barrier()
    stop = cfg.get("stop", 5)
    loop_A(0, x_in, False)
    if stop >= 2:
        loop_B(0, h2s)
    if stop >= 3:
        loop_A(1, h2s, True)
    if stop >= 4:
        loop_B(1, h2s)
    if stop >= 5:
        loop_C(h2s)
    P.run()
    return nc


def _consts():
    ident = np.eye(128, dtype=np.float32)
    qc = np.arange(64)
    cs = np.clip(qc - 8, 0, 48)
    kc = np.arange(64)
    cm = ((kc[:, None] >= cs[None, :]) & (kc[:, None] < cs[None, :] + 16)).astype(np.float32)
    colmask = np.concatenate([cm, cm], axis=0)
    return ident, colmask


def _shared_inputs(inp):
    f = lambda a: np.ascontiguousarray(np.asarray(a, dtype=np.float32))
    sh = {}
    for k in ("ffn1_w_gate", "ffn1_w_up", "ffn1_w_down", "ffn2_w_gate", "ffn2_w_up", "ffn2_w_down", "w_in", "w_branch_a", "w_branch_b", "w_out"):
        sh[k] = f(inp[k])
    n = [inp["ffn1_norm"][0], inp["mix_norm"][0], inp["ffn2_norm"][0], inp["ffn1_norm"][1], inp["mix_norm"][1], inp["ffn2_norm"][1], inp["final_norm"]]
    sh["norms"] = f(np.stack([np.asarray(a) for a in n]))
    sh["lngT"] = f(np.asarray(inp["sgu_ln_g"]).reshape(NL, 8, 128).transpose(0, 2, 1))
    sh["lnb"] = f(inp["sgu_ln_b"])
    sh["bgT"] = f(np.asarray(inp["b_gate"]).reshape(NL, 2, 8, 128).transpose(0, 1, 3, 2))
    sh["wsT"] = f(np.asarray(inp["sgu_w_s"]).transpose(0, 1, 3, 2))
    sh["bs"] = f(inp["sgu_b_s"])
    rp = np.asarray(inp["nat_rpb"], dtype=np.float32).reshape(NL, -1)
    sh["rpbpad"] = f(np.pad(rp, ((0, 0), (48, 48))))
    ident, colmask = _consts()
    sh["ident"] = ident
    sh["colmask"] = colmask
    return sh


_NC_CACHE = {}


def run_cfg(cfg, xs, modes, inp, debug=False):
    key = (cfg["nrow"], cfg["segs"], debug, cfg.get("stop", 5))
    if key not in _NC_CACHE:
        _NC_CACHE[key] = build_program(cfg, debug=debug)
    nc = _NC_CACHE[key]
    sh = _shared_inputs(inp)
    in_maps = []
    for c in range(8):
        m = dict(sh)
        m["x"] = np.ascontiguousarray(xs[c], dtype=np.float32)
        mm = np.zeros((128, 2), np.float32)
        mm[:, modes[c]] = 1.0
        m["modemask"] = mm
        in_maps.append(m)
    res = run_bass_kernel_spmd(nc, in_maps, core_ids=list(range(8)))
    return res.results


def kernel(**inp):
    xp = np.asarray(inp["x_prompt"], dtype=np.float32)
    xsm = np.asarray(inp["x_sample"], dtype=np.float32)
    xs, modes = [], []
    for c in range(4):
        xs.append(np.concatenate([xp[c], xsm[c]], axis=0))
        modes.append(0)
    for c in range(4):
        xs.append(np.concatenate([xsm[4 + 3 * c + i] for i in range(3)], axis=0))
        modes.append(1)
    res = run_cfg(PROD_CFG, xs, modes, inp)
    yp = np.empty_like(xp)
    ys = np.empty_like(xsm)
    for c in range(4):
        y = res[c]["y"]
        yp[c] = y[0:8192]
        ys[c] = y[8192:]
    for c in range(4):
        y = res[4 + c]["y"]
        for i in range(3):
            ys[4 + 3 * c + i] = y[i * 4096:(i + 1) * 4096]
    return yp, ys
```

```python
import bisect
from contextlib import ExitStack

import numpy as np
import concourse.bass as bass
import concourse.mybir as mybir
from concourse.bass_utils import run_bass_kernel_spmd

F32 = mybir.dt.float32
BF16 = mybir.dt.bfloat16
AF = mybir.ActivationFunctionType
ALU = mybir.AluOpType

D = 1024
DFF = 2816
NFC = DFF // 128
NWSLOT = 5
DIN = 7168
NL = 2
NH = 16
EPS = 1e-6
ENG = ("pe", "act", "dve", "pool", "sp")

PROD_CFG = dict(
    nrow=192,
    segs=(((0, 128), (128, 192)), ((0, 64), (64, 128), (128, 192))),
)


class Buf:
    __slots__ = ("w", "r")

    def __init__(self):
        self.w = {}
        self.r = {}


def bufs(n):
    return [Buf() for _ in range(n)]


class Prog:
    def __init__(self, nc):
        self.nc = nc
        self.q = {e: [] for e in ENG}
        self.sigidx = {e: [] for e in ENG}
        self.seen = {e: {} for e in ENG}
        self.dmasem = {}

    def _token(self, ref):
        if ref[0] == "s":
            return ref[1], ref[2]
        _, eng, idx = ref
        si = self.sigidx[eng]
        j = bisect.bisect_left(si, idx)
        if j < len(si):
            return eng, j + 1
        q = self.q[eng]
        last = len(q) - 1
        while q[last][0] != "op":
            last -= 1
        assert last >= idx
        q[last][3] = (eng, 1)
        si.append(last)
        return eng, len(si)

    def _resolve(self, eng, refs):
        wl = []
        for ref in refs:
            if ref is None:
                continue
            key, val = self._token(ref)
            if self.seen[eng].get(key, 0) >= val:
                continue
            self.seen[eng][key] = val
            wl.append((key, val))
        return wl

    def _deps(self, eng, reads, writes, waits):
        refs = list(waits)
        for b in reads:
            refs += list(b.w.values())
        for b in writes:
            for k, r in b.w.items():
                if not (r[0] == "e" and r[1] == eng):
                    refs.append(r)
            for k, r in b.r.items():
                if not (r[0] == "e" and r[1] == eng):
                    refs.append(r)
        if eng == "pe":
            refs = [r for r in refs if not (r[0] == "e" and r[1] == "pe")]
        return refs

    def op(self, eng, fn, reads=(), writes=(), waits=()):
        wl = self._resolve(eng, self._deps(eng, reads, writes, waits))
        idx = len(self.q[eng])
        self.q[eng].append(["op", fn, wl, None])
        ref = ("e", eng, idx)
        for b in reads:
            b.r[eng] = ref
        for b in writes:
            b.w = {eng: ref}
            b.r = {}
        return ref

    def dma(self, queue, sem, pairs, reads=(), writes=(), waits=()):
        wl = self._resolve(queue, self._deps(queue, reads, writes, waits))
        cur = self.dmasem.get(sem, 0)
        first = True
        for o, i in pairs:
            cur += 16
            self.q[queue].append(["dma", (lambda e, o=o, i=i: e.dma_start(out=o, in_=i)), wl if first else [], (sem, 16)])
            first = False
        self.dmasem[sem] = cur
        ref = ("s", sem, cur)
        for b in reads:
            b.r[sem] = ref
        for b in writes:
            b.w = {sem: ref}
            b.r = {}
        return ref

    def barrier(self):
        refs = []
        for e in ENG:
            q = self.q[e]
            for i in range(len(q) - 1, -1, -1):
                if q[i][0] == "op":
                    refs.append(("e", e, i))
                    break
        for s, v in self.dmasem.items():
            refs.append(("s", s, v))
        toks = [self._token(r) for r in refs]
        for e in ENG:
            wl = []
            for key, val in toks:
                if key == e:
                    continue
                if self.seen[e].get(key, 0) >= val:
                    continue
                self.seen[e][key] = val
                wl.append((key, val))
            if wl:
                self.q[e].append(["wait", None, wl, None])

    def run(self):
        nc = self.nc
        with ExitStack() as st:
            sems = {}
            for n in list(ENG) + list(self.dmasem.keys()):
                sems[n] = st.enter_context(nc.semaphore("s_" + n))
            block = st.enter_context(nc.Block())

            def replay(name):
                def f(e):
                    for kind, fn, wl, sig in self.q[name]:
                        for key, val in wl:
                            e.wait_ge(sems[key], val)
                        if fn is None:
                            continue
                        ins = fn(e)
                        if sig is not None:
                            ins.then_inc(sems[sig[0]], sig[1])

                return f

            block.tensor(replay("pe"))
            block.scalar(replay("act"))
            block.vector(replay("dve"))
            block.gpsimd(replay("pool"))
            block.sync(replay("sp"))


def _window(r, segs):
    for (s, e) in segs:
        if s <= r < e:
            rows = e - s
            rs = s + min(max((r - s) - 4, 0), rows - 8)
            return rs, rs + 8
    raise AssertionError


def attn_plan(r0, nq, segs_modes):
    allowed = {}
    for qr in range(r0, r0 + nq):
        wins = [_window(qr, segs) for segs in segs_modes]
        for kr in range(min(w[0] for w in wins), max(w[1] for w in wins)):
            inm = [w[0] <= kr < w[1] for w in wins]
            if any(inm):
                allowed[(kr, qr)] = "b" if all(inm) else (0 if inm[0] else 1)
    chunks = sorted(set(kr // 2 for (kr, qr) in allowed))
    plan = []
    for c in chunks:
        qrs = [qr for (kr, qr) in allowed if kr // 2 == c]
        qa, qb = min(qrs), max(qrs)
        runs, zero = [], []
        for hf in range(2):
            kr = 2 * c + hf
            cur = None
            for qr in range(qa, qb + 1):
                cls = allowed.get((kr, qr))
                if cls is None:
                    zero.append((hf, qr))
                    cur = None
                    continue
                if cur is not None and cur[3] == cls and cur[2] == qr - 1:
                    cur[2] = qr
                else:
                    cur = [hf, qr, qr, cls]
                    runs.append(cur)
        plan.append(dict(c=c, qa=qa, qb=qb, runs=[tuple(r) for r in runs], zero=zero))
    full = [p for p in plan if p["qa"] == r0 and p["qb"] == r0 + nq - 1]
    assert full
    pref = [p for p in full if p["c"] == r0 // 2]
    f0 = pref[0] if pref else full[0]
    plan.remove(f0)
    plan.insert(0, f0)
    return plan


def build_program(cfg, debug=False):
    NROW = cfg["nrow"]
    NTOK = NROW * 64
    NCH = NTOK // 128
    segs_modes = cfg["segs"]
    nc = bass.Bass("TRN2", target_bir_lowering=False)
    P = Prog(nc)

    _uid = [0]
    _sbuf_tensor, _psum_tensor = nc.sbuf_tensor, nc.psum_tensor

    def _sb(name, shape, dt):
        _uid[0] += 1
        return _sbuf_tensor("%s_u%d" % (name, _uid[0]), shape, dt)

    def _pp(name, shape, dt):
        _uid[0] += 1
        return _psum_tensor("%s_u%d" % (name, _uid[0]), shape, dt)

    def din(name, shape, dt=F32):
        return nc.dram_tensor(name, list(shape), dt, kind="ExternalInput").ap()

    def dscr(name, shape, dt, out=False):
        return nc.dram_tensor(name, list(shape), dt, kind=("ExternalOutput" if (out and debug) else "Internal")).ap()

    x_in = din("x", [NTOK, D])
    y_out = nc.dram_tensor("y", [NTOK, D], F32, kind="ExternalOutput").ap()
    wsrc = {}
    for pre in ("ffn1", "ffn2"):
        wsrc[pre + "_w_gate"] = din(pre + "_w_gate", [NL, D, DFF])
        wsrc[pre + "_w_up"] = din(pre + "_w_up", [NL, D, DFF])
        wsrc[pre + "_w_down"] = din(pre + "_w_down", [NL, DFF, D])
    wsrc["w_in"] = din("w_in", [NL, D, DIN])
    for n in ("w_branch_a", "w_branch_b", "w_out"):
        wsrc[n] = din(n, [NL, D, D])
    norms = din("norms", [7, D])
    lngT = din("lngT", [NL, 128, 8])
    lnb = din("lnb", [NL, D])
    bgT = din("bgT", [NL, 2, 128, 8])
    wsT = din("wsT", [NL, 8, 128, 128])
    bs = din("bs", [NL, 8, 128])
    rpbpad = din("rpbpad", [NL, 7600])
    ident_in = din("ident", [128, 128])
    colmask_in = din("colmask", [128, 64])
    mm_in = din("modemask", [128, 2])

    wb = {k: dscr(k + "_b", v.shape, BF16) for k, v in wsrc.items()}
    h1s = dscr("h1s", [NTOK, D], F32, out=True)
    h2s = dscr("h2s", [NTOK, D], F32, out=True)
    qs = dscr("qs", [8, 128, NTOK], BF16, out=True)
    gbs = dscr("gbs", [8, 128, NTOK], BF16, out=True)
    mas = dscr("mas", [8, 128, NTOK], BF16, out=True)
    ks = dscr("ks", [NCH, 128, 1024], BF16, out=True)
    vs = dscr("vs", [NCH, 128, 1024], BF16, out=True)

    _dbg_done = [False]
    if debug:
        dbg_nT = nc.dram_tensor("dbg_nT", [128, 8 * 512], BF16, kind="ExternalOutput").ap()
        dbg_aT = nc.dram_tensor("dbg_aT", [128, NFC * 512], BF16, kind="ExternalOutput").ap()
        dbg_nT2 = nc.dram_tensor("dbg_nT2", [128, 8 * 512], BF16, kind="ExternalOutput").ap()
        dbg_uT = nc.dram_tensor("dbg_uT", [128, 8 * 512], BF16, kind="ExternalOutput").ap()
        dbg_rs = nc.dram_tensor("dbg_rs", [128, 4], F32, kind="ExternalOutput").ap()
    cast_pairs = []
    for k, src in wsrc.items():
        K = src.shape[1]
        for l in range(NL):
            for r in range(0, K, 128):
                cast_pairs.append((wb[k][l, r:r + 128, :], src[l, r:r + 128, :]))
    cast_ref = P.dma("pool", "cast", cast_pairs)

    class PS:
        def __init__(self, st):
            self.t = [st.enter_context(_pp("psb%d" % i, [128, 512], F32)) for i in range(8)]
            self.b = bufs(8)
            self.n = 0

        def get(self):
            i = self.n % 8
            self.n += 1
            return self.t[i], self.b[i]

    class WStream:
        def __init__(self, st, nslot, name):
            self.ns = nslot
            self.t = [st.enter_context(_sb("%s%d" % (name, i), [128, 4096], BF16)) for i in range(nslot)]
            self.b = bufs(nslot)
            self.name = name
            self.plan = []
            self.issued = 0
            self.cur = 0

        def add(self, kind, src):
            self.plan.append((kind, src))

        def view(self, i, kind):
            t = self.t[i % self.ns]
            if kind == "up":
                return t[:].rearrange("p (a n) -> p a n", a=8)
            return t[:].rearrange("p (a n) -> p a n", a=4)

        def ensure(self, upto):
            upto = min(len(self.plan), upto)
            while self.issued < upto:
                j = self.issued
                kind, src = self.plan[j]
                v = self.view(j, kind)
                if src.shape[1] != v.shape[1]:
                    v = v[:, 0:src.shape[1], :]
                if src.shape[2] != v.shape[2]:
                    v = v[:, :, 0:src.shape[2]]
                P.dma("sp", "%s%d" % (self.name, j % self.ns), [(v, src)], writes=[self.b[j % self.ns]], waits=[cast_ref])
                self.issued += 1

        def next(self, kind):
            i = self.cur
            self.cur += 1
            assert self.plan[i][0] == kind, (i, kind, self.plan[i][0])
            self.ensure(i + self.ns - 1)
            return self.view(i, kind), self.b[i % self.ns]

    def plan_ffn(ws, wg, wu, wd):
        for fg in range(6):
            ncol = 512 if fg < 5 else 256
            ws.add("up", w_up_src(wg, fg * 512, ncol))
            ws.add("up", w_up_src(wu, fg * 512, ncol))
        for dg in range(6):
            nj = 4 if dg < 5 else 2
            ws.add("down", w_down_src(wd, dg * 4, nj))

    def w_up_src(w_l, c0, ncol):
        return w_l[:, c0:c0 + ncol].rearrange("(a p) n -> p a n", p=128)

    def w_down_src(w_l, j0, nj):
        return w_l[j0 * 128:(j0 + nj) * 128, :].rearrange("(a p) n -> p a n", p=128)

    def norm_transpose(E, xt, XB, g_rep, GB, nT, NTB, ntiles, pre=None):
        ps, ident, IB = E["ps"], E["ident"], E["IB"]
        ss, SSB, rstd, RSB = E["ss"], E["SSB"], E["rstd"], E["RSB"]
        junk, JB = E["junk"], E["JB"]
        state = {}

        def stage1(t):
            if pre is not None:
                pre(t)
            P.op("act", lambda e, t=t: e.activation(out=junk[:], in_=xt[:, t, :], func=AF.Square, accum_out=ss[:, t:t + 1]),
                 reads=[XB[t]], writes=[JB, SSB[t]])
            P.op("act", lambda e, t=t: e.activation(out=rstd[:, t:t + 1], in_=ss[:, t:t + 1], func=AF.Sqrt, scale=1.0 / D, bias=E["epsc"][:, 0:1]),
                 reads=[SSB[t], E["EPSB"]], writes=[RSB[t]])

        def stage2(t):
            P.op("dve", lambda e, t=t: e.reciprocal(out=rstd[:, t:t + 1], in_=rstd[:, t:t + 1]), reads=[RSB[t]], writes=[RSB[t]])
            nt, NB = E["nt"][t % 2], E["NB"][t % 2]
            P.op("dve", lambda e, t=t, nt=nt: e.scalar_tensor_tensor(out=nt[:], in0=xt[:, t, :], scalar=rstd[:, t:t + 1], in1=g_rep,
                                                                     op0=ALU.mult, op1=ALU.mult),
                 reads=[XB[t], RSB[t], GB], writes=[NB])
            bank, BB = ps.get()
            bv = bank[:].bitcast(BF16)
            for c in range(8):
                P.op("pe", lambda e, c=c, bv=bv, nt=nt: e.transpose(bv[:, c * 128:(c + 1) * 128], nt[:, c * 128:(c + 1) * 128], ident[:]),
                     reads=[NB, IB], writes=[BB] if c == 0 else [])
            BB.w = {"pe": ("e", "pe", len(P.q["pe"]) - 1)}
            state[t] = (bv, BB)

        def stage3(t):
            bv, BB = state.pop(t)
            P.op("act", lambda e, t=t, bv=bv: e.activation(out=nT[:, :, t * 128:(t + 1) * 128], in_=bv.rearrange("p (c k) -> p c k", c=8), func=AF.Copy),
                 reads=[BB], writes=[NTB[t]])

        for t in range(ntiles):
            stage1(t)
            if t >= 1:
                stage2(t - 1)
            if t >= 2:
                stage3(t - 2)
        stage2(ntiles - 1)
        if ntiles >= 2:
            stage3(ntiles - 2)
        stage3(ntiles - 1)

    def norm_stats_early(E, xt, XB, g_rep, GB, ntiles):
        ss, rstd, junk, JB = E["ss"], E["rstd"], E["junk"], E["JB"]
        for t in range(ntiles):
            P.op("act", lambda e, t=t: e.activation(out=junk[:], in_=xt[:, t, :], func=AF.Square, accum_out=ss[:, 4 + t:5 + t]),
                 reads=[XB[t]], writes=[JB, E["SSB2"][t]])
            P.op("act", lambda e, t=t: e.activation(out=rstd[:, 4 + t:5 + t], in_=ss[:, 4 + t:5 + t], func=AF.Sqrt, scale=1.0 / D, bias=E["epsc"][:, 0:1]),
                 reads=[E["SSB2"][t], E["EPSB"]], writes=[E["RSB2"][t]])
        for t in range(ntiles):
            P.op("dve", lambda e, t=t: e.reciprocal(out=rstd[:, 4 + t:5 + t], in_=rstd[:, 4 + t:5 + t]), reads=[E["RSB2"][t]], writes=[E["RSB2"][t]])
            P.op("dve", lambda e, t=t: e.scalar_tensor_tensor(out=E["nth"][t][:], in0=xt[:, t, :], scalar=rstd[:, 4 + t:5 + t], in1=g_rep, op0=ALU.mult, op1=ALU.mult),
                 reads=[XB[t], E["RSB2"][t], GB], writes=[E["NHB"][t]])

    def norm_finish_late(E, nT, NTB, ntiles):
        ps, ident, IB = E["ps"], E["ident"], E["IB"]
        st_ = []
        for t in range(ntiles):
            bank, BB = ps.get()
            bv = bank[:].bitcast(BF16)
            nth = E["nth"][t]
            for c in range(8):
                P.op("pe", lambda e, c=c, bv=bv, nth=nth: e.transpose(bv[:, c * 128:(c + 1) * 128], nth[:, c * 128:(c + 1) * 128], ident[:]),
                     reads=[E["NHB"][t], IB], writes=[BB] if c == 0 else [])
            BB.w = {"pe": ("e", "pe", len(P.q["pe"]) - 1)}
            st_.append((bv, BB))
        for t in range(ntiles):
            bv, BB = st_[t]
            P.op("act", lambda e, t=t, bv=bv: e.activation(out=nT[:, :, t * 128:(t + 1) * 128], in_=bv.rearrange("p (c k) -> p c k", c=8), func=AF.Copy),
                 reads=[BB], writes=[NTB[t]])

    def ffn(E, xt, XB, g_rep, GB, ws, ntiles, skip_norm=False, pre=None, defer=False, mid=None):
        ps = E["ps"]
        nT, NTB, aT, AB = E["nT"], E["NTB"], E["aT"], E["AB"]
        T = ntiles * 128
        if not skip_norm:
            norm_transpose(E, xt, XB, g_rep, GB, nT, NTB, ntiles, pre=pre)
        else:
            assert pre is None
        for fg in range(6):
            nchk = 4 if fg < 5 else 2
            sg_v, SGB = ws.next("up")
            su_v, SUB = ws.next("up")
            for jj in range(nchk):
                j = fg * 4 + jj
                bg, BG = ps.get()
                for dc in range(8):
                    P.op("pe", lambda e, dc=dc, jj=jj, bg=bg, sg_v=sg_v: e.matmul(bg[:, 0:T], lhsT=sg_v[:, dc, jj * 128:(jj + 1) * 128], rhs=nT[:, dc, 0:T],
                                                                                  start=(dc == 0), stop=(dc == 7)),
                         reads=NTB[0:ntiles] + [SGB], writes=[BG] if dc == 0 else [])
                BG.w = {"pe": ("e", "pe", len(P.q["pe"]) - 1)}
                bu, BU = ps.get()
                for dc in range(8):
                    P.op("pe", lambda e, dc=dc, jj=jj, bu=bu, su_v=su_v: e.matmul(bu[:, 0:T], lhsT=su_v[:, dc, jj * 128:(jj + 1) * 128], rhs=nT[:, dc, 0:T],
                                                                                  start=(dc == 0), stop=(dc == 7)),
                         reads=NTB[0:ntiles] + [SUB], writes=[BU] if dc == 0 else [])
                BU.w = {"pe": ("e", "pe", len(P.q["pe"]) - 1)}
                sgt, SB_ = E["sg"][j % 2], E["SGB"][j % 2]
                P.op("act", lambda e, bg=bg, sgt=sgt: e.activation(out=sgt[:, 0:T], in_=bg[:, 0:T], func=AF.Silu), reads=[BG], writes=[SB_])
                P.op("dve", lambda e, j=j, bu=bu, sgt=sgt: e.tensor_tensor(out=aT[:, j, 0:T], in0=bu[:, 0:T], in1=sgt[:, 0:T], op=ALU.mult),
                     reads=[BU, SB_], writes=[AB[j]])
        if debug and not _dbg_done[0]:
            _dbg_done[0] = True
            P.dma("pool", "dbg", [(dbg_nT[:, :], nT[:, :, :].rearrange("p c t -> p (c t)")), (dbg_aT[:, :], aT[:, :, :].rearrange("p c t -> p (c t)")),
                                  (dbg_rs[:, :], E["rstd"][:, :])], reads=NTB[0:ntiles] + AB + E["RSB"])
        if mid is not None:
            mid()
        banks = [[ps.get() for _ in range(2)] for _ in range(ntiles)]
        for dg in range(6):
            nj = 4 if dg < 5 else 2
            sd_v, SDB = ws.next("down")
            for jj in range(nj):
                j = dg * 4 + jj
                for t in range(ntiles):
                    for hf in range(2):
                        bk, BK = banks[t][hf]
                        P.op("pe", lambda e, j=j, jj=jj, t=t, hf=hf, bk=bk, sd_v=sd_v: e.matmul(
                            bk[:, :], lhsT=aT[:, j, t * 128:(t + 1) * 128], rhs=sd_v[:, jj, hf * 512:(hf + 1) * 512], start=(j == 0), stop=(j == NFC - 1)),
                             reads=[AB[j], SDB], writes=[BK] if j == 0 else [])
        lastpe = ("e", "pe", len(P.q["pe"]) - 1)
        for t in range(ntiles):
            for hf in range(2):
                banks[t][hf][1].w = {"pe": lastpe}

        def evac(t):
            for hf in range(2):
                bk, BK = banks[t][hf]
                P.op("dve", lambda e, t=t, hf=hf, bk=bk: e.scalar_tensor_tensor(out=xt[:, t, hf * 512:(hf + 1) * 512], in0=bk[:, :], scalar=0.5,
                                                                                 in1=xt[:, t, hf * 512:(hf + 1) * 512], op0=ALU.mult, op1=ALU.add),
                     reads=[BK, XB[t]], writes=[XB[t]])
        if defer:
            return evac
        for t in range(ntiles):
            evac(t)
        return None

    def common_env(st, ntiles, with_ffn=True):
        E = {}
        sb = lambda n, s, d: st.enter_context(_sb(n, s, d))
        E["ps"] = PS(st)
        E["ident"] = sb("ident", [128, 128], BF16)
        E["IB"] = Buf()
        identf = sb("identf", [128, 128], F32)
        IFB = Buf()
        P.dma("sp", "c_ident", [(identf[:], ident_in[:, :])], writes=[IFB])
        P.op("dve", lambda e: e.tensor_copy(out=E["ident"][:], in_=identf[:]), reads=[IFB], writes=[E["IB"]])
        E["ss"] = sb("ss", [128, 8], F32)
        E["SSB"] = bufs(4)
        E["SSB2"] = bufs(4)
        E["RSB2"] = bufs(4)
        E["nth"] = [sb("nth%d" % i, [128, 1024], BF16) for i in range(4)]
        E["NHB"] = bufs(4)
        E["rstd"] = sb("rstd", [128, 8], F32)
        E["RSB"] = bufs(4)
        E["junk"] = sb("junk", [128, 1024], BF16)
        E["JB"] = Buf()
        E["epsc"] = sb("epsc", [128, 1], F32)
        E["EPSB"] = Buf()
        P.op("dve", lambda e: e.memset(E["epsc"][:], EPS), writes=[E["EPSB"]])
        E["nt"] = [sb("nt%d" % i, [128, 1024], BF16) for i in range(2)]
        E["NB"] = bufs(2)
        E["nT"] = sb("nT", [128, 8, ntiles * 128], BF16)
        E["NTB"] = bufs(ntiles)
        if with_ffn:
            E["aT"] = sb("aT", [128, NFC, ntiles * 128], BF16)
            E["AB"] = bufs(NFC)
            E["sg"] = [sb("sg%d" % i, [128, ntiles * 128], BF16) for i in range(2)]
            E["SGB"] = bufs(2)
        return E, sb

    def load_grep(sb, idxs):
        out = []
        for i in idxs:
            t = sb("grep%d" % i, [128, 1024], F32)
            B = Buf()
            src = bass.AP(norms.tensor, norms[i:i + 1, :].offset, [[0, 128], [1, 1024]])
            P.dma("sp", "c_grep%d" % i, [(t[:], src)], writes=[B])
            out.append((t[:], B))
        return out

    def loop_A(l, hin, pre_ffn2):
        with ExitStack() as st:
            NT = 4
            T = 512
            E, sb = common_env(st, NT)
            ps = E["ps"]
            ws = WStream(st, NWSLOT, "wA")
            nsteps = NTOK // T
            gi = [3 * l, 3 * l + 1] + ([3 * (l - 1) + 2] if pre_ffn2 else [])
            greps = load_grep(sb, gi)
            xt = [sb("xt%d" % i, [128, NT, 1024], F32) for i in range(2)]
            XB = [bufs(NT) for _ in range(2)]
            aT, AB = E["aT"], E["AB"]
            uT = aT[:, 0:8, :]; UB = AB[0:8]
            gaT = aT[:, 8:16, :]; GAB = AB[8:16]
            yaT = sb("yaT", [128, 8, T], BF16); YAB = bufs(NT)
            vg = [sb("vg%d" % i, [128, 1024], F32) for i in range(2)]; VGB = bufs(2)
            vh = sb("vh", [128, NT, 1024], BF16); VHB = bufs(NT)
            NSTG = 3
            stg = [sb("stg%d" % i, [128, 8 * T], BF16) for i in range(NSTG)]; STB = bufs(NSTG)
            stg_n = [0]
            tmp = [sb("sgut%d" % i, [128, 4, 128], F32) for i in range(2)]; TMB = bufs(2)
            bst = sb("bst", [128, 12], F32); BSTB = bufs(2)
            mv = sb("mv", [128, 4], F32); MVB = bufs(2)
            lng_t = sb("lng_t", [128, 8], F32); LNGB = Buf()
            P.dma("sp", "c_lng", [(lng_t[:], lngT[l])], writes=[LNGB])
            bg_t = sb("bg_t", [128, 2, 8], F32); BGB = Buf()
            P.dma("sp", "c_bg", [(bg_t[:, 0, :], bgT[l, 0]), (bg_t[:, 1, :], bgT[l, 1])], writes=[BGB])
            wsf = sb("wsf", [128, 8, 128], F32); WSFB = Buf()
            P.dma("sp", "c_wsf", [(wsf[:], wsT[l].rearrange("g q p -> q g p"))], writes=[WSFB])
            wst = sb("wst", [128, 8, 128], BF16); WSTB = Buf()
            P.op("dve", lambda e: e.tensor_copy(out=wst[:], in_=wsf[:]), reads=[WSFB], writes=[WSTB])
            onesf = sb("onesf", [128, 1], F32); ONB = Buf()
            P.op("dve", lambda e: e.memset(onesf[:], 1.0), writes=[ONB])
            l2 = sb("l2", [2, 1024], F32); L2B = Buf()
            P.op("dve", lambda e: e.memset(l2[:], 1.0), writes=[L2B])
            P.dma("sp", "c_l2", [(l2[0:1, :], lnb[l:l + 1, :])], writes=[L2B])
            r2 = sb("r2", [2, 8, 128], F32); R2B = Buf()
            P.dma("sp", "c_r2", [(r2[1:2, :, :], bs[l:l + 1, :, :])], writes=[R2B])
            bias2 = sb("bias2", [128, 8, 128], F32); B2B = Buf()
            for half in range(2):
                bk, BK = ps.get()
                for g4 in range(4):
                    g = half * 4 + g4
                    P.op("pe", lambda e, g=g, g4=g4, bk=bk: e.matmul(bk[0:1, g4 * 128:(g4 + 1) * 128], lhsT=onesf[:, 0:1], rhs=wsf[:, g, :], start=True, stop=True),
                         reads=[ONB, WSFB], writes=[BK] if g4 == 0 else [])
                BK.w = {"pe": ("e", "pe", len(P.q["pe"]) - 1)}
                P.op("dve", lambda e, half=half, bk=bk: e.tensor_copy(out=r2[0:1, half * 4:(half + 1) * 4, :], in_=bk[0:1, :].rearrange("p (a b) -> p a b", a=4)),
                     reads=[BK], writes=[])
            R2B.w["dve"] = ("e", "dve", len(P.q["dve"]) - 1)
            for half in range(2):
                bk, BK = ps.get()
                for g4 in range(4):
                    g = half * 4 + g4
                    P.op("pe", lambda e, g=g, g4=g4, bk=bk: e.matmul(bk[:, g4 * 128:(g4 + 1) * 128], lhsT=l2[0:2, g * 128:(g + 1) * 128], rhs=r2[0:2, g, :], start=True, stop=True),
                         reads=[L2B, R2B], writes=[BK] if g4 == 0 else [])
                BK.w = {"pe": ("e", "pe", len(P.q["pe"]) - 1)}
                P.op("dve", lambda e, half=half, bk=bk: e.tensor_copy(out=bias2[:, half * 4:(half + 1) * 4, :], in_=bk[:, :].rearrange("p (a b) -> p a b", a=4)),
                     reads=[BK], writes=[])
            B2B.w = {"dve": ("e", "dve", len(P.q["dve"]) - 1)}

            win_l = wb["w_in"][l]
            for s in range(nsteps):
                if pre_ffn2:
                    plan_ffn(ws, wb["ffn2_w_gate"][l - 1], wb["ffn2_w_up"][l - 1], wb["ffn2_w_down"][l - 1])
                plan_ffn(ws, wb["ffn1_w_gate"][l], wb["ffn1_w_up"][l], wb["ffn1_w_down"][l])
                for cg in (2, 3, 8, 9, 0, 1, 4, 5, 6, 7, 10, 11, 12, 13):
                    ws.add("up", w_up_src(win_l, cg * 512, 512))
                for cg in range(2):
                    ws.add("up", w_up_src(wb["w_branch_a"][l], cg * 512, 512))

            def load_x(s):
                sl = s % 2
                tok0 = s * T
                pairs = [(xt[sl][:, t, :], hin[tok0 + t * 128: tok0 + (t + 1) * 128, :]) for t in range(NT)]
                P.dma("sp", "ldx%d" % sl, pairs, writes=XB[sl])

            def stage():
                i = stg_n[0] % NSTG
                stg_n[0] += 1
                return stg[i], STB[i], "st_stg%d" % i

            def lastref(eng):
                return ("e", eng, len(P.q[eng]) - 1)

            load_x(0)
            for s in range(nsteps):
                sl = s % 2
                tok0 = s * T
                x, XBs = xt[sl], XB[sl]
                if s + 1 < nsteps:
                    load_x(s + 1)
                nT, NTB = E["nT"], E["NTB"]
                hoisted = s > 0
                pv = None
                if pre_ffn2:
                    pv = ffn(E, x, XBs, greps[2][0], greps[2][1], ws, NT, skip_norm=hoisted, defer=True)
                    pv = ffn(E, x, XBs, greps[0][0], greps[0][1], ws, NT, pre=pv, defer=True)
                else:
                    pv = ffn(E, x, XBs, greps[0][0], greps[0][1], ws, NT, skip_norm=hoisted, defer=True)
                norm_transpose(E, x, XBs, greps[1][0], greps[1][1], nT, NTB, NT, pre=pv)
                P.dma("pool", "st_h1_%d" % sl, [(h1s[tok0 + t * 128: tok0 + (t + 1) * 128, :], x[:, t, :]) for t in range(NT)], reads=XBs)
                gfirst = greps[2] if pre_ffn2 else greps[0]
                if s + 1 < nsteps:
                    norm_stats_early(E, xt[(s + 1) % 2], XB[(s + 1) % 2], gfirst[0], gfirst[1], NT)

                def fm_group(cgi, evac):
                    sv, SVB = ws.next("up")
                    for jj in range(4):
                        bk, BK = ps.get()
                        for dc in range(8):
                            P.op("pe", lambda e, dc=dc, jj=jj, bk=bk, sv=sv: e.matmul(bk[:, :], lhsT=sv[:, dc, jj * 128:(jj + 1) * 128], rhs=nT[:, dc, :],
                                                                                      start=(dc == 0), stop=(dc == 7)),
                                 reads=NTB + [SVB], writes=[BK] if dc == 0 else [])
                        BK.w = {"pe": lastref("pe")}
                        evac(cgi * 4 + jj, bk, BK)

                def tm_group(half, evac):
                    sv, SVB = ws.next("up")
                    for t in range(NT):
                        bk, BK = ps.get()
                        for dc in range(8):
                            P.op("pe", lambda e, dc=dc, t=t, bk=bk, sv=sv: e.matmul(bk[:, :], lhsT=nT[:, dc, t * 128:(t + 1) * 128], rhs=sv[:, dc, :],
                                                                                    start=(dc == 0), stop=(dc == 7)),
                                 reads=[NTB[t], SVB], writes=[BK] if dc == 0 else [])
                        BK.w = {"pe": lastref("pe")}
                        evac(t, half, bk, BK)

                sv0, SVB0 = ws.next("up")
                sv1, SVB1 = ws.next("up")
                for t in range(NT):
                    i = t % 2
                    for half, (sv, SVB) in enumerate(((sv0, SVB0), (sv1, SVB1))):
                        bk, BK = ps.get()
                        for dc in range(8):
                            P.op("pe", lambda e, dc=dc, t=t, bk=bk, sv=sv: e.matmul(bk[:, :], lhsT=nT[:, dc, t * 128:(t + 1) * 128], rhs=sv[:, dc, :],
                                                                                    start=(dc == 0), stop=(dc == 7)),
                                 reads=[NTB[t], SVB], writes=[BK] if dc == 0 else [])
                        BK.w = {"pe": lastref("pe")}
                        if half == 0:
                            P.op("act", lambda e, i=i, bk=bk: e.activation(out=vg[i][:, 0:512], in_=bk[:, :], func=AF.Gelu), reads=[BK], writes=[VGB[i]])
                        else:
                            P.op("act", lambda e, i=i, bk=bk: e.activation(out=vg[i][:, 512:1024], in_=bk[:, :], func=AF.Gelu), reads=[BK], writes=[])
                            VGB[i].w = {"act": lastref("act")}
                    P.op("dve", lambda e, i=i: e.bn_stats(out=bst[:, 0:6], in_=vg[i][:, 0:512]), reads=[VGB[i]], writes=[BSTB[0]])
                    P.op("dve", lambda e, i=i: e.bn_stats(out=bst[:, 6:12], in_=vg[i][:, 512:1024]), reads=[VGB[i]], writes=[BSTB[1]])
                    P.op("dve", lambda e: e.bn_aggr(out=mv[:, 0:2], in_=bst[:, 0:12]), reads=BSTB, writes=[MVB[0]])
                    P.op("act", lambda e: e.activation(out=mv[:, 2:3], in_=mv[:, 1:2], func=AF.Sqrt, scale=1.0, bias=E["epsc"][:, 0:1]),
                         reads=[MVB[0], E["EPSB"]], writes=[MVB[1]])
                    P.op("dve", lambda e: e.reciprocal(out=mv[:, 3:4], in_=mv[:, 2:3]), reads=[MVB[1]], writes=[MVB[1]])
                    P.op("dve", lambda e, t=t, i=i: e.tensor_scalar(out=vh[:, t, :], in0=vg[i][:, :], scalar1=mv[:, 0:1], scalar2=mv[:, 3:4],
                                                                    op0=ALU.subtract, op1=ALU.mult),
                         reads=[VGB[i], MVB[0], MVB[1]], writes=[VHB[t]])
                vst, VSB, vsem = stage()
                vv = vst[:].rearrange("p (t f) -> p t f", t=NT)
                first = [True]

                def evac_vv(t, half, bk, BK):
                    P.op("dve", lambda e, t=t, half=half, bk=bk, vv=vv: e.tensor_copy(out=vv[:, t, half * 512:(half + 1) * 512], in_=bk[:, :]),
                         reads=[BK], writes=[VSB] if first[0] else [])
                    first[0] = False
                for half in (0, 1):
                    tm_group(half, evac_vv)
                VSB.w = {"dve": lastref("dve")}
                P.dma("pool", vsem, [(vs[tok0 // 128: tok0 // 128 + NT, :, :].rearrange("t p f -> p t f"), vv)], reads=[VSB])
                for cgi in (0, 1):
                    fm_group(cgi, lambda c, bk, BK: P.op("act", lambda e, c=c, bk=bk: e.activation(out=uT[:, c, :], in_=bk[:, :], func=AF.Gelu),
                                                         reads=[BK], writes=[UB[c]]))
                qst, QSB, qsem = stage()
                qv = qst[:].rearrange("p (c t) -> p c t", c=8)
                for cgi in (0, 1):
                    fm_group(cgi, lambda c, bk, BK: P.op("dve", lambda e, c=c, bk=bk, qv=qv: e.tensor_scalar(out=qv[:, c, :], in0=bk[:, :], scalar1=0.125, scalar2=None, op0=ALU.mult),
                                                         reads=[BK], writes=[QSB] if c == 0 else []))
                QSB.w = {"dve": lastref("dve")}
                P.dma("pool", qsem, [(qs[:, :, tok0:tok0 + T].rearrange("c p t -> p c t"), qv)], reads=[QSB])
                kst, KSB, ksem = stage()
                kv = kst[:].rearrange("p (t c k) -> p t c k", t=NT, c=8)
                for cgi in (0, 1):
                    fm_group(cgi, lambda c, bk, BK: P.op("dve", lambda e, c=c, bk=bk, kv=kv: e.tensor_copy(out=kv[:, :, c, :], in_=bk[:, :].rearrange("p (t k) -> p t k", t=NT)),
                                                         reads=[BK], writes=[KSB] if c == 0 else []))
                KSB.w = {"dve": lastref("dve")}
                P.dma("pool", ksem, [(ks[tok0 // 128: tok0 // 128 + NT, :, :].rearrange("t p f -> p t f"), kst[:].rearrange("p (t f) -> p t f", t=NT))], reads=[KSB])
                for cgi in (0, 1):
                    fm_group(cgi, lambda c, bk, BK: P.op("act", lambda e, c=c, bk=bk: e.activation(out=gaT[:, c, :], in_=bk[:, :], func=AF.Sigmoid, bias=bg_t[:, 0, c:c + 1], scale=1.0),
                                                         reads=[BK, BGB], writes=[GAB[c]]))
                gst, GSB, gsem = stage()
                gv = gst[:].rearrange("p (c t) -> p c t", c=8)
                for cgi in (0, 1):
                    fm_group(cgi, lambda c, bk, BK: P.op("act", lambda e, c=c, bk=bk, gv=gv: e.activation(out=gv[:, c, :], in_=bk[:, :], func=AF.Sigmoid, bias=bg_t[:, 1, c:c + 1], scale=1.0),
                                                         reads=[BK, BGB], writes=[GSB] if c == 0 else []))
                GSB.w = {"act": lastref("act")}
                P.dma("pool", gsem, [(gbs[:, :, tok0:tok0 + T].rearrange("c p t -> p c t"), gv)], reads=[GSB])
                for t in range(NT):
                    for half in range(2):
                        bk, BK = ps.get()
                        for g4 in range(4):
                            g = half * 4 + g4
                            P.op("pe", lambda e, g=g, g4=g4, t=t, bk=bk: e.matmul(bk[:, g4 * 128:(g4 + 1) * 128], lhsT=vh[:, t, g * 128:(g + 1) * 128], rhs=wst[:, g, :],
                                                                                  start=True, stop=True),
                                 reads=[VHB[t], WSTB], writes=[BK] if g4 == 0 else [])
                        BK.w = {"pe": lastref("pe")}
                        tm, TB = tmp[half], TMB[half]
                        for g4 in range(4):
                            g = half * 4 + g4
                            P.op("dve", lambda e, g=g, g4=g4, bk=bk, tm=tm: e.scalar_tensor_tensor(out=tm[:, g4, :], in0=bk[:, g4 * 128:(g4 + 1) * 128], scalar=lng_t[:, g:g + 1],
                                                                                                  in1=bias2[:, g, :], op0=ALU.mult, op1=ALU.add),
                                 reads=[BK, LNGB, B2B], writes=[TB] if g4 == 0 else [])
                        TB.w = {"dve": lastref("dve")}
                        P.op("dve", lambda e, half=half, t=t, tm=tm: e.tensor_tensor(out=yaT[:, half * 4:(half + 1) * 4, t * 128:(t + 1) * 128], in0=tm[:, :, :],
                                                                                     in1=uT[:, half * 4:(half + 1) * 4, t * 128:(t + 1) * 128], op=ALU.mult),
                             reads=[TB] + UB[half * 4:(half + 1) * 4], writes=[YAB[t]] if half == 0 else [])
                    YAB[t].w = {"dve": lastref("dve")}
                if s + 1 < nsteps:
                    norm_finish_late(E, nT, NTB, NT)
                mst, MSB, msem = stage()
                mvv = mst[:].rearrange("p (c t) -> p c t", c=8)
                for cg in range(2):
                    sv, SVB = ws.next("up")
                    for jj in range(4):
                        c = cg * 4 + jj
                        bk, BK = ps.get()
                        for cc in range(8):
                            P.op("pe", lambda e, cc=cc, jj=jj, bk=bk, sv=sv: e.matmul(bk[:, :], lhsT=sv[:, cc, jj * 128:(jj + 1) * 128], rhs=yaT[:, cc, :],
                                                                                      start=(cc == 0), stop=(cc == 7)),
                                 reads=YAB + [SVB], writes=[BK] if cc == 0 else [])
                        BK.w = {"pe": lastref("pe")}
                        P.op("dve", lambda e, c=c, bk=bk, mvv=mvv: e.tensor_tensor(out=mvv[:, c, :], in0=bk[:, :], in1=gaT[:, c, :], op=ALU.mult),
                             reads=[BK, GAB[c]], writes=[MSB] if c == 0 else [])
                MSB.w = {"dve": lastref("dve")}
                P.dma("pool", msem, [(mas[:, :, tok0:tok0 + T].rearrange("c p t -> p c t"), mvv)], reads=[MSB])
            assert ws.cur == len(ws.plan)
            P.barrier()

    def loop_B(l, hout):
        with ExitStack() as st:
            sb = lambda n, s, d: st.enter_context(_sb(n, s, d))
            ps = PS(st)
            NQR = 4
            T = 256
            nsteps = NROW // NQR
            NSLOT = 8
            wbt = sb("wbt", [128, 8, 1024], BF16); WBB = Buf()
            wot = sb("wot", [128, 8, 1024], BF16); WOB = Buf()
            P.dma("sp", "c_wb", [(wbt[:], wb["w_branch_b"][l].rearrange("(a p) n -> p a n", p=128))], writes=[WBB], waits=[cast_ref])
            P.dma("sp", "c_wo", [(wot[:], wb["w_out"][l].rearrange("(a p) n -> p a n", p=128))], writes=[WOB], waits=[cast_ref])
            mm_t = sb("mm_t", [128, 2], F32); MMB = Buf()
            P.dma("sp", "c_mm", [(mm_t[:], mm_in[:, :])], writes=[MMB])
            cm_t = sb("cm_t", [128, 64], F32); CMB = Buf()
            P.dma("sp", "c_cm", [(cm_t[:], colmask_in[:, :])], writes=[CMB])
            etab = sb("etab", [128, NH * 15 * 64], BF16); ETB = Buf()
            with ExitStack() as st2:
                hk = [st2.enter_context(_sb("hk%d" % i, [128, 4 * 15 * 64], F32)) for i in range(2)]; HKB = bufs(2)
                for hg in range(4):
                    i = hg % 2
                    base = rpbpad[l:l + 1, hg * 4 * 465: hg * 4 * 465 + 1].offset
                    src = bass.AP(rpbpad.tensor, base, [[1, 64], [465, 4], [31, 15], [1, 64]])
                    dst0 = hk[i][0:64, :].rearrange("p (h r q) -> p h r q", h=4, r=15)
                    dst1 = hk[i][64:128, :].rearrange("p (h r q) -> p h r q", h=4, r=15)
                    src_hi = bass.AP(rpbpad.tensor, base + 31, [[1, 64], [465, 4], [31, 15], [1, 64]])
                    P.dma("sp", "c_hk%d" % i, [(dst0, src), (dst1, src_hi)], writes=[HKB[i]])
                    P.op("act", lambda e, i=i: e.activation(out=hk[i][:, :], in_=hk[i][:, :], func=AF.Exp), reads=[HKB[i]], writes=[HKB[i]])
                    a = hk[i][:, 0:64]
                    rev = bass.AP(a.tensor, a.offset + 63, [list(a.ap[0]), [64, 60], [-1, 64]])
                    c0_ = cm_t[:, :]
                    cmb = bass.AP(c0_.tensor, c0_.offset, [list(c0_.ap[0]), [0, 60], [1, 64]])
                    P.op("dve", lambda e, hg=hg, rev=rev, cmb=cmb: e.tensor_tensor(out=etab[:, hg * 3840:(hg + 1) * 3840].rearrange("p (a q) -> p a q", q=64), in0=rev, in1=cmb, op=ALU.mult),
                         reads=[HKB[i], CMB], writes=[])
                    HKB[i].r["dve"] = ("e", "dve", len(P.q["dve"]) - 1)
                ETB.w = {"dve": ("e", "dve", len(P.q["dve"]) - 1)}
                P.barrier()
            kring = sb("kring", [128, NSLOT, 1024], BF16); KRB = bufs(NSLOT)
            vring = sb("vring", [128, NSLOT, 8, 192], BF16); VRB = bufs(NSLOT); VONE = Buf()
            P.op("dve", lambda e: e.memset(vring[:, :, :, 64:128], 1.0), writes=[VONE])
            qt = [sb("qt%d" % i, [128, 8, T], BF16) for i in range(2)]; QB = bufs(2)
            gbt = [sb("gbt%d" % i, [128, 8, T], BF16) for i in range(2)]; GBB = bufs(2)
            mat = [sb("mat%d" % i, [128, 8, T], BF16) for i in range(2)]; MAB = bufs(2)
            ht = [sb("ht%d" % i, [128, 2, 1024], F32) for i in range(2)]; HB = [bufs(2) for _ in range(2)]
            NG = 3
            NPX = 2 * NG
            pexp = [sb("pexp%d" % i, [128, 512], BF16) for i in range(NPX)]; PXB = bufs(NPX)
            pm = [sb("pm%d" % i, [128, 512], BF16) for i in range(NPX)]; PMB = bufs(NPX)
            pmzero = [set() for _ in range(NPX)]
            rr = [sb("rr%d" % i, [128, T], F32) for i in range(2)]; RRB = bufs(2)
            ybT = sb("ybT", [128, 8, T], BF16); YBB = bufs(8)
            mT = sb("mT", [128, 8, T], BF16); MTB = bufs(8)
            loaded = [-1]
            plans = [attn_plan(s * NQR, NQR, segs_modes) for s in range(nsteps)]

            def load_chunks(upto):
                while loaded[0] < upto:
                    c = loaded[0] + 1
                    sl = c % NSLOT
                    P.dma("sp", "ldk%d" % sl, [(kring[:, sl, :], ks[c])], writes=[KRB[sl]])
                    vsrc = vs[c].rearrange("p (a b f) -> p a b f", a=8, b=2)
                    P.dma("sp", "ldv%d" % sl, [(vring[:, sl, :, 0:64], vsrc[:, :, 0, :]), (vring[:, sl, :, 128:192], vsrc[:, :, 1, :])], writes=[VRB[sl]])
                    loaded[0] = c

            def load_step(s):
                i = s % 2
                tok0 = s * T
                P.dma("sp", "ldq%d" % i, [(qt[i][:], qs[:, :, tok0:tok0 + T].rearrange("c p t -> p c t"))], writes=[QB[i]])
                load_chunks(max(p["c"] for p in plans[s]))
                P.dma("sp", "ldgb%d" % i, [(gbt[i][:], gbs[:, :, tok0:tok0 + T].rearrange("c p t -> p c t"))], writes=[GBB[i]])
                P.dma("sp", "ldma%d" % i, [(mat[i][:], mas[:, :, tok0:tok0 + T].rearrange("c p t -> p c t"))], writes=[MAB[i]])
                P.dma("sp", "ldh%d" % i, [(ht[i][:, t, :], h1s[tok0 + t * 128: tok0 + (t + 1) * 128, :]) for t in range(2)], writes=HB[i])

            load_step(0)
            pxn = [0]
            for s in range(nsteps):
                i = s % 2
                tok0 = s * T
                r0 = s * NQR
                if s + 1 < nsteps:
                    load_step(s + 1)
                plan = plans[s]
                groups = []
                cur, used = [], 0
                for p in plan:
                    nqc = (p["qb"] - p["qa"] + 1) * 64
                    if used + nqc > 512:
                        groups.append(cur)
                        cur, used = [], 0
                    cur.append((p, used, nqc))
                    used += nqc
                groups.append(cur)

                has_special = any(r[3] != "b" for p in plan for r in p["runs"])

                def emit_scores(h):
                    hp, dc = h % 2, h // 2
                    res = []
                    for grp in groups:
                        bk, BK = ps.get()
                        for gi_, (p, off, nqc) in enumerate(grp):
                            sl = p["c"] % NSLOT
                            c0 = (p["qa"] - r0) * 64
                            P.op("pe", lambda e, bk=bk, off=off, nqc=nqc, sl=sl, c0=c0, hp=hp, dc=dc, i=i: e.matmul(
                                bk[:, off:off + nqc], lhsT=kring[hp * 64:(hp + 1) * 64, sl, dc * 128:(dc + 1) * 128],
                                rhs=qt[i][hp * 64:(hp + 1) * 64, dc, c0:c0 + nqc], start=True, stop=True),
                                 reads=[KRB[sl], QB[i]], writes=[BK] if gi_ == 0 else [])
                        BK.w = {"pe": ("e", "pe", len(P.q["pe"]) - 1)}
                        res.append((bk, BK, grp))
                    return res

                def emit_x(h, sc):
                    eng = "dve"
                    assert len(sc) <= NG
                    for gix, (bk, BK, grp) in enumerate(sc):
                        k = (h % 2) * NG + gix
                        tot = grp[-1][1] + grp[-1][2]
                        P.op("act", lambda e, bk=bk, k=k, tot=tot: e.activation(out=pexp[k][:, 0:tot], in_=bk[:, 0:tot], func=AF.Exp), reads=[BK], writes=[PXB[k]])
                        nops = 0
                        for (p, off, nqc) in grp:
                            cls_of = ({}, {})
                            for (hf, qf, ql, cls) in p["runs"]:
                                for qr in range(qf, ql + 1):
                                    cls_of[hf][qr] = cls
                            oplist = []
                            qr = p["qa"]
                            while qr <= p["qb"]:
                                if cls_of[0].get(qr) == "b" and cls_of[1].get(qr) == "b":
                                    q2 = qr
                                    while cls_of[0].get(q2 + 1) == "b" and cls_of[1].get(q2 + 1) == "b":
                                        q2 += 1
                                    oplist.append((0, 128, qr, q2, "b"))
                                    for qq in range(qr, q2 + 1):
                                        del cls_of[0][qq]
                                        del cls_of[1][qq]
                                    qr = q2 + 1
                                else:
                                    qr += 1
                            for hf in range(2):
                                rows = sorted(cls_of[hf])
                                j0 = 0
                                while j0 < len(rows):
                                    j1 = j0
                                    while j1 + 1 < len(rows) and rows[j1 + 1] == rows[j1] + 1 and cls_of[hf][rows[j1 + 1]] == cls_of[hf][rows[j0]]:
                                        j1 += 1
                                    oplist.append((hf * 64, hf * 64 + 64, rows[j0], rows[j1], cls_of[hf][rows[j0]]))
                                    j0 = j1 + 1
                            for (plo, phi, qf, ql, cls) in oplist:
                                n = ql - qf + 1
                                ri0 = 2 * p["c"] - qf + 7
                                assert 0 <= ri0 - (n - 1) and ri0 <= 14
                                co = off + (qf - p["qa"]) * 64
                                eb = etab[plo:phi, (h * 15 + ri0) * 64:(h * 15 + ri0) * 64 + 64]
                                ein = bass.AP(eb.tensor, eb.offset, [list(eb.ap[0]), [-64, n], [1, 64]])
                                o_ap = pm[k][plo:phi, co:co + n * 64].rearrange("p (a q) -> p a q", q=64)
                                for hf in range(plo // 64, phi // 64):
                                    for bb in range(n):
                                        pmzero[k].discard((hf, co // 64 + bb))
                                i_ap = pexp[k][plo:phi, co:co + n * 64].rearrange("p (a q) -> p a q", q=64)
                                if cls == "b":
                                    P.op(eng, lambda e, o_ap=o_ap, i_ap=i_ap, ein=ein: e.tensor_tensor(out=o_ap, in0=i_ap, in1=ein, op=ALU.mult),
                                         reads=[PXB[k], ETB], writes=[PMB[k]] if nops == 0 else [])
                                else:
                                    P.op(eng, lambda e, o_ap=o_ap, i_ap=i_ap, ein=ein, cls=cls, plo=plo, phi=phi: e.scalar_tensor_tensor(
                                        out=o_ap, in0=i_ap, scalar=mm_t[plo:phi, cls:cls + 1], in1=ein, op0=ALU.mult, op1=ALU.mult),
                                         reads=[PXB[k], ETB, MMB], writes=[PMB[k]] if nops == 0 else [])
                                nops += 1
                            for (hf, qr) in p["zero"]:
                                co = off + (qr - p["qa"]) * 64
                                if (hf, co // 64) in pmzero[k]:
                                    continue
                                pmzero[k].add((hf, co // 64))
                                P.op(eng, lambda e, k=k, hf=hf, co=co: e.memset(pm[k][hf * 64:(hf + 1) * 64, co:co + 64], 0.0),
                                     reads=[], writes=[PMB[k]] if nops == 0 else [])
                                nops += 1
                        PMB[k].w = {eng: ("e", eng, len(P.q[eng]) - 1)}
                        PXB[k].r[eng] = ("e", eng, len(P.q[eng]) - 1)

                def emit_v(h, sc):
                    hp, dc = h % 2, h // 2
                    ob, OB = ps.get()
                    firstpv = True
                    for gix, (bk, BK, grp) in enumerate(sc):
                        k = (h % 2) * NG + gix
                        for gi_, (p, off, nqc) in enumerate(grp):
                            sl = p["c"] % NSLOT
                            lh = vring[:, sl, dc, 0:128] if hp == 0 else vring[:, sl, dc, 64:192]
                            c0 = (p["qa"] - r0) * 64
                            lastpv = (gix == len(sc) - 1) and (gi_ == len(grp) - 1)
                            P.op("pe", lambda e, ob=ob, c0=c0, nqc=nqc, lh=lh, k=k, off=off, fp=firstpv, lp=lastpv: e.matmul(
                                ob[:, c0:c0 + nqc], lhsT=lh, rhs=pm[k][:, off:off + nqc], start=fp, stop=lp),
                                 reads=[PMB[k], VRB[sl], VONE], writes=[OB] if firstpv else [])
                            firstpv = False
                    OB.w = {"pe": ("e", "pe", len(P.q["pe"]) - 1)}
                    return ob, OB

                def emit_n(h, ob, OB):
                    hp, dc = h % 2, h // 2
                    j = h % 2
                    slo, shi = (64, 128) if hp == 0 else (0, 64)
                    nlo, nhi = (0, 64) if hp == 0 else (64, 128)
                    P.op("act", lambda e, ob=ob, j=j, slo=slo, shi=shi: e.activation(out=rr[j][slo:shi, :], in_=ob[slo:shi, 0:T], func=AF.Ln), reads=[OB], writes=[RRB[j]])
                    P.op("act", lambda e, j=j, slo=slo, shi=shi: e.activation(out=rr[j][slo:shi, :], in_=rr[j][slo:shi, :], func=AF.Exp, scale=-1.0), reads=[RRB[j]], writes=[RRB[j]])
                    P.op("dve", lambda e, ob=ob, j=j, slo=slo, shi=shi, nlo=nlo, nhi=nhi, dc=dc: e.tensor_tensor(
                        out=ybT[nlo:nhi, dc, :], in0=ob[nlo:nhi, 0:T], in1=rr[j][slo:shi, :], op=ALU.mult),
                         reads=[OB, RRB[j]], writes=[YBB[dc]] if hp == 0 else [])
                    YBB[dc].w = {"dve": ("e", "dve", len(P.q["dve"]) - 1)}

                scs = {0: emit_scores(0)}
                if NH > 1:
                    scs[1] = emit_scores(1)
                emit_x(0, scs[0])
                for h in range(NH):
                    if h + 2 < NH:
                        scs[h + 2] = emit_scores(h + 2)
                    if h + 1 < NH:
                        emit_x(h + 1, scs[h + 1])
                    ob, OB = emit_v(h, scs[h])
                    emit_n(h, ob, OB)
                    del scs[h]
                for c in range(8):
                    bk, BK = ps.get()
                    for cc in range(8):
                        P.op("pe", lambda e, c=c, cc=cc, bk=bk: e.matmul(bk[:, 0:T], lhsT=wbt[:, cc, c * 128:(c + 1) * 128], rhs=ybT[:, cc, :], start=(cc == 0), stop=(cc == 7)),
                             reads=YBB + [WBB], writes=[BK] if cc == 0 else [])
                    BK.w = {"pe": ("e", "pe", len(P.q["pe"]) - 1)}
                    P.op("dve", lambda e, c=c, bk=bk, i=i: e.tensor_tensor(out=mT[:, c, :], in0=bk[:, 0:T], in1=gbt[i][:, c, :], op=ALU.mult), reads=[BK, GBB[i]], writes=[MTB[c]])
                    P.op("dve", lambda e, c=c, i=i: e.tensor_tensor(out=mT[:, c, :], in0=mT[:, c, :], in1=mat[i][:, c, :], op=ALU.add), reads=[MTB[c], MAB[i]], writes=[MTB[c]])
                for t in range(2):
                    for hf in range(2):
                        bk, BK = ps.get()
                        for cc in range(8):
                            P.op("pe", lambda e, t=t, hf=hf, cc=cc, bk=bk: e.matmul(bk[:, :], lhsT=mT[:, cc, t * 128:(t + 1) * 128], rhs=wot[:, cc, hf * 512:(hf + 1) * 512],
                                                                                    start=(cc == 0), stop=(cc == 7)),
                                 reads=MTB + [WOB], writes=[BK] if cc == 0 else [])
                        BK.w = {"pe": ("e", "pe", len(P.q["pe"]) - 1)}
                        P.op("dve", lambda e, t=t, hf=hf, bk=bk, i=i: e.tensor_tensor(out=ht[i][:, t, hf * 512:(hf + 1) * 512], in0=bk[:, :], in1=ht[i][:, t, hf * 512:(hf + 1) * 512], op=ALU.add),
                             reads=[BK, HB[i][t]], writes=[HB[i][t]])
                P.dma("pool", "st_h2_%d" % i, [(hout[tok0 + t * 128: tok0 + (t + 1) * 128, :], ht[i][:, t, :]) for t in range(2)], reads=HB[i])
            P.barrier()

    def loop_C(hin):
        with ExitStack() as st:
            NT = 4
            T = 512
            E, sb = common_env(st, NT)
            ws = WStream(st, NWSLOT, "wC")
            greps = load_grep(sb, [5, 6])
            xt = [sb("xc%d" % i, [128, NT, 1024], F32) for i in range(2)]
            XB = [bufs(NT) for _ in range(2)]
            yt = [sb("yc%d" % i, [128, NT, 1024], F32) for i in range(2)]
            YB = [bufs(NT) for _ in range(2)]
            nsteps = NTOK // T
            for s in range(nsteps):
                plan_ffn(ws, wb["ffn2_w_gate"][NL - 1], wb["ffn2_w_up"][NL - 1], wb["ffn2_w_down"][NL - 1])

            def load_x(s):
                sl = s % 2
                tok0 = s * T
                P.dma("sp", "ldc%d" % sl, [(xt[sl][:, t, :], hin[tok0 + t * 128: tok0 + (t + 1) * 128, :]) for t in range(NT)], writes=XB[sl])
            load_x(0)
            for s in range(nsteps):
                sl = s % 2
                tok0 = s * T
                if s + 1 < nsteps:
                    load_x(s + 1)
                x, XBs = xt[sl], XB[sl]
                nT, NTB = E["nT"], E["NTB"]
                mid = None
                if s + 1 < nsteps:
                    norm_stats_early(E, xt[(s + 1) % 2], XB[(s + 1) % 2], greps[0][0], greps[0][1], NT)
                    mid = (lambda: norm_finish_late(E, nT, NTB, NT))
                pv = ffn(E, x, XBs, greps[0][0], greps[0][1], ws, NT, skip_norm=(s > 0), defer=True, mid=mid)
                ss, SSB, rstd, RSB, junk, JB = E["ss"], E["SSB"], E["rstd"], E["RSB"], E["junk"], E["JB"]
                for t in range(NT):
                    pv(t)
                    P.op("act", lambda e, t=t, x=x: e.activation(out=junk[:], in_=x[:, t, :], func=AF.Square, accum_out=ss[:, t:t + 1]), reads=[XBs[t]], writes=[JB, SSB[t]])
                    P.op("act", lambda e, t=t: e.activation(out=rstd[:, t:t + 1], in_=ss[:, t:t + 1], func=AF.Sqrt, scale=1.0 / D, bias=E["epsc"][:, 0:1]),
                         reads=[SSB[t], E["EPSB"]], writes=[RSB[t]])
                for t in range(NT):
                    P.op("dve", lambda e, t=t: e.reciprocal(out=rstd[:, t:t + 1], in_=rstd[:, t:t + 1]), reads=[RSB[t]], writes=[RSB[t]])
                    P.op("dve", lambda e, t=t, x=x, sl=sl: e.scalar_tensor_tensor(out=yt[sl][:, t, :], in0=x[:, t, :], scalar=rstd[:, t:t + 1], in1=greps[1][0], op0=ALU.mult, op1=ALU.mult),
                         reads=[XBs[t], RSB[t], greps[1][1]], writes=[YB[sl][t]])
                P.dma("pool", "st_y%d" % sl, [(y_out[tok0 + t * 128: tok0 + (t + 1) * 128, :], yt[sl][:, t, :]) for t in range(NT)], reads=YB[sl])
            P.barrier()

    P.barrier()
    stop = cfg.get("stop", 5)
    loop_A(0, x_in, False)
    if stop >= 2:
        loop_B(0, h2s)
    if stop >= 3:
        loop_A(1, h2s, True)
    if stop >= 4:
        loop_B(1, h2s)
    if stop >= 5:
        loop_C(h2s)
    P.run()
    return nc


def _consts():
    ident = np.eye(128, dtype=np.float32)
    qc = np.arange(64)
    cs = np.clip(qc - 8, 0, 48)
    kc = np.arange(64)
    cm = ((kc[:, None] >= cs[None, :]) & (kc[:, None] < cs[None, :] + 16)).astype(np.float32)
    colmask = np.concatenate([cm, cm], axis=0)
    return ident, colmask


def _shared_inputs(inp):
    f = lambda a: np.ascontiguousarray(np.asarray(a, dtype=np.float32))
    sh = {}
    for k in ("ffn1_w_gate", "ffn1_w_up", "ffn1_w_down", "ffn2_w_gate", "ffn2_w_up", "ffn2_w_down", "w_in", "w_branch_a", "w_branch_b", "w_out"):
        sh[k] = f(inp[k])
    n = [inp["ffn1_norm"][0], inp["mix_norm"][0], inp["ffn2_norm"][0], inp["ffn1_norm"][1], inp["mix_norm"][1], inp["ffn2_norm"][1], inp["final_norm"]]
    sh["norms"] = f(np.stack([np.asarray(a) for a in n]))
    sh["lngT"] = f(np.asarray(inp["sgu_ln_g"]).reshape(NL, 8, 128).transpose(0, 2, 1))
    sh["lnb"] = f(inp["sgu_ln_b"])
    sh["bgT"] = f(np.asarray(inp["b_gate"]).reshape(NL, 2, 8, 128).transpose(0, 1, 3, 2))
    sh["wsT"] = f(np.asarray(inp["sgu_w_s"]).transpose(0, 1, 3, 2))
    sh["bs"] = f(inp["sgu_b_s"])
    rp = np.asarray(inp["nat_rpb"], dtype=np.float32).reshape(NL, -1)
    sh["rpbpad"] = f(np.pad(rp, ((0, 0), (48, 48 + 64))))
    ident, colmask = _consts()
    sh["ident"] = ident
    sh["colmask"] = colmask
    return sh


_NC_CACHE = {}


def run_cfg(cfg, xs, modes, inp, debug=False):
    key = (cfg["nrow"], cfg["segs"], debug, cfg.get("stop", 5))
    if key not in _NC_CACHE:
        _NC_CACHE[key] = build_program(cfg, debug=debug)
    nc = _NC_CACHE[key]
    sh = _shared_inputs(inp)
    in_maps = []
    for c in range(8):
        m = dict(sh)
        m["x"] = np.ascontiguousarray(xs[c], dtype=np.float32)
        mm = np.zeros((128, 2), np.float32)
        mm[:, modes[c]] = 1.0
        m["modemask"] = mm
        in_maps.append(m)
    res = run_bass_kernel_spmd(nc, in_maps, core_ids=list(range(8)))
    return res.results


def kernel(**inp):
    xp = np.asarray(inp["x_prompt"], dtype=np.float32)
    xsm = np.asarray(inp["x_sample"], dtype=np.float32)
    xs, modes = [], []
    for c in range(4):
        xs.append(np.concatenate([xp[c], xsm[c]], axis=0))
        modes.append(0)
    for c in range(4):
        xs.append(np.concatenate([xsm[4 + 3 * c + i] for i in range(3)], axis=0))
        modes.append(1)
    res = run_cfg(PROD_CFG, xs, modes, inp)
    yp = np.empty_like(xp)
    ys = np.empty_like(xsm)
    for c in range(4):
        y = res[c]["y"]
        yp[c] = y[0:8192]
        ys[c] = y[8192:]
    for c in range(4):
        y = res[4 + c]["y"]
        for i in range(3):
            ys[4 + 3 * c + i] = y[i * 4096:(i + 1) * 4096]
    return yp, ys
```

```python
import bisect
from contextlib import ExitStack

import numpy as np
import concourse.bass as bass
import concourse.mybir as mybir
from concourse.bass_utils import run_bass_kernel_spmd

F32 = mybir.dt.float32
BF16 = mybir.dt.bfloat16
AF = mybir.ActivationFunctionType
ALU = mybir.AluOpType

D = 1024
DFF = 2816
NFC = DFF // 128
NWSLOT = 5
DIN = 7168
NL = 2
NH = 16
EPS = 1e-6
ENG = ("pe", "act", "dve", "pool", "sp")

PROD_CFG = dict(
    nrow=192,
    segs=(((0, 128), (128, 192)), ((0, 64), (64, 128), (128, 192))),
)


class Buf:
    __slots__ = ("w", "r")

    def __init__(self):
        self.w = {}
        self.r = {}


def bufs(n):
    return [Buf() for _ in range(n)]


class Prog:
    def __init__(self, nc):
        self.nc = nc
        self.q = {e: [] for e in ENG}
        self.sigidx = {e: [] for e in ENG}
        self.seen = {e: {} for e in ENG}
        self.dmasem = {}

    def _token(self, ref):
        if ref[0] == "s":
            return ref[1], ref[2]
        _, eng, idx = ref
        si = self.sigidx[eng]
        j = bisect.bisect_left(si, idx)
        if j < len(si):
            return eng, j + 1
        q = self.q[eng]
        last = len(q) - 1
        while q[last][0] != "op":
            last -= 1
        assert last >= idx
        q[last][3] = (eng, 1)
        si.append(last)
        return eng, len(si)

    def _resolve(self, eng, refs):
        wl = []
        for ref in refs:
            if ref is None:
                continue
            key, val = self._token(ref)
            if self.seen[eng].get(key, 0) >= val:
                continue
            self.seen[eng][key] = val
            wl.append((key, val))
        return wl

    def _deps(self, eng, reads, writes, waits):
        refs = list(waits)
        for b in reads:
            refs += list(b.w.values())
        for b in writes:
            for k, r in b.w.items():
                if not (r[0] == "e" and r[1] == eng):
                    refs.append(r)
            for k, r in b.r.items():
                if not (r[0] == "e" and r[1] == eng):
                    refs.append(r)
        if eng == "pe":
            refs = [r for r in refs if not (r[0] == "e" and r[1] == "pe")]
        return refs

    def op(self, eng, fn, reads=(), writes=(), waits=()):
        wl = self._resolve(eng, self._deps(eng, reads, writes, waits))
        idx = len(self.q[eng])
        self.q[eng].append(["op", fn, wl, None])
        ref = ("e", eng, idx)
        for b in reads:
            b.r[eng] = ref
        for b in writes:
            b.w = {eng: ref}
            b.r = {}
        return ref

    def dma(self, queue, sem, pairs, reads=(), writes=(), waits=()):
        wl = self._resolve(queue, self._deps(queue, reads, writes, waits))
        cur = self.dmasem.get(sem, 0)
        first = True
        for o, i in pairs:
            cur += 16
            self.q[queue].append(["dma", (lambda e, o=o, i=i: e.dma_start(out=o, in_=i)), wl if first else [], (sem, 16)])
            first = False
        self.dmasem[sem] = cur
        ref = ("s", sem, cur)
        for b in reads:
            b.r[sem] = ref
        for b in writes:
            b.w = {sem: ref}
            b.r = {}
        return ref

    def barrier(self):
        refs = []
        for e in ENG:
            q = self.q[e]
            for i in range(len(q) - 1, -1, -1):
                if q[i][0] == "op":
                    refs.append(("e", e, i))
                    break
        for s, v in self.dmasem.items():
            refs.append(("s", s, v))
        toks = [self._token(r) for r in refs]
        for e in ENG:
            wl = []
            for key, val in toks:
                if key == e:
                    continue
                if self.seen[e].get(key, 0) >= val:
                    continue
                self.seen[e][key] = val
                wl.append((key, val))
            if wl:
                self.q[e].append(["wait", None, wl, None])

    def run(self):
        nc = self.nc
        with ExitStack() as st:
            sems = {}
            for n in list(ENG) + list(self.dmasem.keys()):
                sems[n] = st.enter_context(nc.semaphore("s_" + n))
            block = st.enter_context(nc.Block())

            def replay(name):
                def f(e):
                    for kind, fn, wl, sig in self.q[name]:
                        for key, val in wl:
                            e.wait_ge(sems[key], val)
                        if fn is None:
                            continue
                        ins = fn(e)
                        if sig is not None:
                            ins.then_inc(sems[sig[0]], sig[1])

                return f

            block.tensor(replay("pe"))
            block.scalar(replay("act"))
            block.vector(replay("dve"))
            block.gpsimd(replay("pool"))
            block.sync(replay("sp"))


def _window(r, segs):
    for (s, e) in segs:
        if s <= r < e:
            rows = e - s
            rs = s + min(max((r - s) - 4, 0), rows - 8)
            return rs, rs + 8
    raise AssertionError


def attn_plan(r0, nq, segs_modes):
    allowed = {}
    for qr in range(r0, r0 + nq):
        wins = [_window(qr, segs) for segs in segs_modes]
        for kr in range(min(w[0] for w in wins), max(w[1] for w in wins)):
            inm = [w[0] <= kr < w[1] for w in wins]
            if any(inm):
                allowed[(kr, qr)] = "b" if all(inm) else (0 if inm[0] else 1)
    chunks = sorted(set(kr // 2 for (kr, qr) in allowed))
    plan = []
    for c in chunks:
        qrs = [qr for (kr, qr) in allowed if kr // 2 == c]
        qa, qb = min(qrs), max(qrs)
        runs, zero = [], []
        for hf in range(2):
            kr = 2 * c + hf
            cur = None
            for qr in range(qa, qb + 1):
                cls = allowed.get((kr, qr))
                if cls is None:
                    zero.append((hf, qr))
                    cur = None
                    continue
                if cur is not None and cur[3] == cls and cur[2] == qr - 1:
                    cur[2] = qr
                else:
                    cur = [hf, qr, qr, cls]
                    runs.append(cur)
        plan.append(dict(c=c, qa=qa, qb=qb, runs=[tuple(r) for r in runs], zero=zero))
    full = [p for p in plan if p["qa"] == r0 and p["qb"] == r0 + nq - 1]
    assert full
    pref = [p for p in full if p["c"] == r0 // 2]
    f0 = pref[0] if pref else full[0]
    plan.remove(f0)
    plan.insert(0, f0)
    return plan


def build_program(cfg, debug=False):
    NROW = cfg["nrow"]
    NTOK = NROW * 64
    NCH = NTOK // 128
    segs_modes = cfg["segs"]
    nc = bass.Bass("TRN2", target_bir_lowering=False)
    P = Prog(nc)

    _uid = [0]
    _sbuf_tensor, _psum_tensor = nc.sbuf_tensor, nc.psum_tensor

    def _sb(name, shape, dt):
        _uid[0] += 1
        return _sbuf_tensor("%s_u%d" % (name, _uid[0]), shape, dt)

    def _pp(name, shape, dt):
        _uid[0] += 1
        return _psum_tensor("%s_u%d" % (name, _uid[0]), shape, dt)

    def din(name, shape, dt=F32):
        return nc.dram_tensor(name, list(shape), dt, kind="ExternalInput").ap()

    def dscr(name, shape, dt, out=False):
        return nc.dram_tensor(name, list(shape), dt, kind=("ExternalOutput" if (out and debug) else "Internal")).ap()

    x_in = din("x", [NTOK, D])
    y_out = nc.dram_tensor("y", [NTOK, D], F32, kind="ExternalOutput").ap()
    wsrc = {}
    for pre in ("ffn1", "ffn2"):
        wsrc[pre + "_w_gate"] = din(pre + "_w_gate", [NL, D, DFF])
        wsrc[pre + "_w_up"] = din(pre + "_w_up", [NL, D, DFF])
        wsrc[pre + "_w_down"] = din(pre + "_w_down", [NL, DFF, D])
    wsrc["w_in"] = din("w_in", [NL, D, DIN])
    for n in ("w_branch_a", "w_branch_b", "w_out"):
        wsrc[n] = din(n, [NL, D, D])
    norms = din("norms", [7, D])
    lngT = din("lngT", [NL, 128, 8])
    lnb = din("lnb", [NL, D])
    bgT = din("bgT", [NL, 2, 128, 8])
    wsT = din("wsT", [NL, 8, 128, 128])
    bs = din("bs", [NL, 8, 128])
    rpbpad = din("rpbpad", [NL, 7600])
    ident_in = din("ident", [128, 128])
    colmask_in = din("colmask", [128, 64])
    mm_in = din("modemask", [128, 2])

    wb = {k: dscr(k + "_b", v.shape, BF16) for k, v in wsrc.items()}
    h1s = dscr("h1s", [NTOK, D], F32, out=True)
    h2s = dscr("h2s", [NTOK, D], F32, out=True)
    qs = dscr("qs", [8, 128, NTOK], BF16, out=True)
    gbs = dscr("gbs", [8, 128, NTOK], BF16, out=True)
    mas = dscr("mas", [8, 128, NTOK], BF16, out=True)
    ks = dscr("ks", [NCH, 128, 1024], BF16, out=True)
    vs = dscr("vs", [NCH, 128, 1024], BF16, out=True)

    _dbg_done = [False]
    if debug:
        dbg_nT = nc.dram_tensor("dbg_nT", [128, 8 * 512], BF16, kind="ExternalOutput").ap()
        dbg_aT = nc.dram_tensor("dbg_aT", [128, NFC * 512], BF16, kind="ExternalOutput").ap()
        dbg_nT2 = nc.dram_tensor("dbg_nT2", [128, 8 * 512], BF16, kind="ExternalOutput").ap()
        dbg_uT = nc.dram_tensor("dbg_uT", [128, 8 * 512], BF16, kind="ExternalOutput").ap()
        dbg_rs = nc.dram_tensor("dbg_rs", [128, 4], F32, kind="ExternalOutput").ap()
    cast_pairs = []
    for k, src in wsrc.items():
        K = src.shape[1]
        for l in range(NL):
            for r in range(0, K, 128):
                cast_pairs.append((wb[k][l, r:r + 128, :], src[l, r:r + 128, :]))
    cast_ref = P.dma("pool", "cast", cast_pairs)

    class PS:
        def __init__(self, st):
            self.t = [st.enter_context(_pp("psb%d" % i, [128, 512], F32)) for i in range(8)]
            self.b = bufs(8)
            self.n = 0

        def get(self):
            i = self.n % 8
            self.n += 1
            return self.t[i], self.b[i]

    class WStream:
        def __init__(self, st, nslot, name):
            self.ns = nslot
            self.t = [st.enter_context(_sb("%s%d" % (name, i), [128, 4096], BF16)) for i in range(nslot)]
            self.b = bufs(nslot)
            self.name = name
            self.plan = []
            self.issued = 0
            self.cur = 0

        def add(self, kind, src):
            self.plan.append((kind, src))

        def view(self, i, kind):
            t = self.t[i % self.ns]
            if kind == "up":
                return t[:].rearrange("p (a n) -> p a n", a=8)
            return t[:].rearrange("p (a n) -> p a n", a=4)

        def ensure(self, upto):
            upto = min(len(self.plan), upto)
            while self.issued < upto:
                j = self.issued
                kind, src = self.plan[j]
                v = self.view(j, kind)
                if src.shape[1] != v.shape[1]:
                    v = v[:, 0:src.shape[1], :]
                if src.shape[2] != v.shape[2]:
                    v = v[:, :, 0:src.shape[2]]
                P.dma("sp", "%s%d" % (self.name, j % self.ns), [(v, src)], writes=[self.b[j % self.ns]], waits=[cast_ref])
                self.issued += 1

        def next(self, kind):
            i = self.cur
            self.cur += 1
            assert self.plan[i][0] == kind, (i, kind, self.plan[i][0])
            self.ensure(i + self.ns - 1)
            return self.view(i, kind), self.b[i % self.ns]

    def plan_ffn(ws, wg, wu, wd):
        for fg in range(6):
            ncol = 512 if fg < 5 else 256
            ws.add("up", w_up_src(wg, fg * 512, ncol))
            ws.add("up", w_up_src(wu, fg * 512, ncol))
        for dg in range(6):
            nj = 4 if dg < 5 else 2
            ws.add("down", w_down_src(wd, dg * 4, nj))

    def w_up_src(w_l, c0, ncol):
        return w_l[:, c0:c0 + ncol].rearrange("(a p) n -> p a n", p=128)

    def w_down_src(w_l, j0, nj):
        return w_l[j0 * 128:(j0 + nj) * 128, :].rearrange("(a p) n -> p a n", p=128)

    def norm_transpose(E, xt, XB, g_rep, GB, nT, NTB, ntiles, pre=None):
        ps, ident, IB = E["ps"], E["ident"], E["IB"]
        ss, SSB, rstd, RSB = E["ss"], E["SSB"], E["rstd"], E["RSB"]
        junk, JB = E["junk"], E["JB"]
        state = {}

        def stage1(t):
            if pre is not None:
                pre(t)
            P.op("act", lambda e, t=t: e.activation(out=junk[:], in_=xt[:, t, :], func=AF.Square, accum_out=ss[:, t:t + 1]),
                 reads=[XB[t]], writes=[JB, SSB[t]])
            P.op("act", lambda e, t=t: e.activation(out=rstd[:, t:t + 1], in_=ss[:, t:t + 1], func=AF.Sqrt, scale=1.0 / D, bias=E["epsc"][:, 0:1]),
                 reads=[SSB[t], E["EPSB"]], writes=[RSB[t]])

        def stage2(t):
            P.op("dve", lambda e, t=t: e.reciprocal(out=rstd[:, t:t + 1], in_=rstd[:, t:t + 1]), reads=[RSB[t]], writes=[RSB[t]])
            nt, NB = E["nt"][t % 2], E["NB"][t % 2]
            P.op("dve", lambda e, t=t, nt=nt: e.scalar_tensor_tensor(out=nt[:], in0=xt[:, t, :], scalar=rstd[:, t:t + 1], in1=g_rep,
                                                                     op0=ALU.mult, op1=ALU.mult),
                 reads=[XB[t], RSB[t], GB], writes=[NB])
            bank, BB = ps.get()
            bv = bank[:].bitcast(BF16)
            for c in range(8):
                P.op("pe", lambda e, c=c, bv=bv, nt=nt: e.transpose(bv[:, c * 128:(c + 1) * 128], nt[:, c * 128:(c + 1) * 128], ident[:]),
                     reads=[NB, IB], writes=[BB] if c == 0 else [])
            BB.w = {"pe": ("e", "pe", len(P.q["pe"]) - 1)}
            state[t] = (bv, BB)

        def stage3(t):
            bv, BB = state.pop(t)
            P.op("act", lambda e, t=t, bv=bv: e.activation(out=nT[:, :, t * 128:(t + 1) * 128], in_=bv.rearrange("p (c k) -> p c k", c=8), func=AF.Copy),
                 reads=[BB], writes=[NTB[t]])

        for t in range(ntiles):
            stage1(t)
            if t >= 1:
                stage2(t - 1)
            if t >= 2:
                stage3(t - 2)
        stage2(ntiles - 1)
        if ntiles >= 2:
            stage3(ntiles - 2)
        stage3(ntiles - 1)

    def norm_stats_early(E, xt, XB, g_rep, GB, ntiles):
        ss, rstd, junk, JB = E["ss"], E["rstd"], E["junk"], E["JB"]
        for t in range(ntiles):
            P.op("act", lambda e, t=t: e.activation(out=junk[:], in_=xt[:, t, :], func=AF.Square, accum_out=ss[:, 4 + t:5 + t]),
                 reads=[XB[t]], writes=[JB, E["SSB2"][t]])
            P.op("act", lambda e, t=t: e.activation(out=rstd[:, 4 + t:5 + t], in_=ss[:, 4 + t:5 + t], func=AF.Sqrt, scale=1.0 / D, bias=E["epsc"][:, 0:1]),
                 reads=[E["SSB2"][t], E["EPSB"]], writes=[E["RSB2"][t]])
        for t in range(ntiles):
            P.op("dve", lambda e, t=t: e.reciprocal(out=rstd[:, 4 + t:5 + t], in_=rstd[:, 4 + t:5 + t]), reads=[E["RSB2"][t]], writes=[E["RSB2"][t]])
            P.op("dve", lambda e, t=t: e.scalar_tensor_tensor(out=E["nth"][t][:], in0=xt[:, t, :], scalar=rstd[:, 4 + t:5 + t], in1=g_rep, op0=ALU.mult, op1=ALU.mult),
                 reads=[XB[t], E["RSB2"][t], GB], writes=[E["NHB"][t]])

    def norm_finish_late(E, nT, NTB, ntiles):
        ps, ident, IB = E["ps"], E["ident"], E["IB"]
        st_ = []
        for t in range(ntiles):
            bank, BB = ps.get()
            bv = bank[:].bitcast(BF16)
            nth = E["nth"][t]
            for c in range(8):
                P.op("pe", lambda e, c=c, bv=bv, nth=nth: e.transpose(bv[:, c * 128:(c + 1) * 128], nth[:, c * 128:(c + 1) * 128], ident[:]),
                     reads=[E["NHB"][t], IB], writes=[BB] if c == 0 else [])
            BB.w = {"pe": ("e", "pe", len(P.q["pe"]) - 1)}
            st_.append((bv, BB))
        for t in range(ntiles):
            bv, BB = st_[t]
            P.op("act", lambda e, t=t, bv=bv: e.activation(out=nT[:, :, t * 128:(t + 1) * 128], in_=bv.rearrange("p (c k) -> p c k", c=8), func=AF.Copy),
                 reads=[BB], writes=[NTB[t]])

    def ffn(E, xt, XB, g_rep, GB, ws, ntiles, skip_norm=False, pre=None, defer=False, mid=None):
        ps = E["ps"]
        nT, NTB, aT, AB = E["nT"], E["NTB"], E["aT"], E["AB"]
        T = ntiles * 128
        if not skip_norm:
            norm_transpose(E, xt, XB, g_rep, GB, nT, NTB, ntiles, pre=pre)
        else:
            assert pre is None
        for fg in range(6):
            nchk = 4 if fg < 5 else 2
            sg_v, SGB = ws.next("up")
            su_v, SUB = ws.next("up")
            for jj in range(nchk):
                j = fg * 4 + jj
                bg, BG = ps.get()
                for dc in range(8):
                    P.op("pe", lambda e, dc=dc, jj=jj, bg=bg, sg_v=sg_v: e.matmul(bg[:, 0:T], lhsT=sg_v[:, dc, jj * 128:(jj + 1) * 128], rhs=nT[:, dc, 0:T],
                                                                                  start=(dc == 0), stop=(dc == 7)),
                         reads=NTB[0:ntiles] + [SGB], writes=[BG] if dc == 0 else [])
                BG.w = {"pe": ("e", "pe", len(P.q["pe"]) - 1)}
                bu, BU = ps.get()
                for dc in range(8):
                    P.op("pe", lambda e, dc=dc, jj=jj, bu=bu, su_v=su_v: e.matmul(bu[:, 0:T], lhsT=su_v[:, dc, jj * 128:(jj + 1) * 128], rhs=nT[:, dc, 0:T],
                                                                                  start=(dc == 0), stop=(dc == 7)),
                         reads=NTB[0:ntiles] + [SUB], writes=[BU] if dc == 0 else [])
                BU.w = {"pe": ("e", "pe", len(P.q["pe"]) - 1)}
                sgt, SB_ = E["sg"][j % 2], E["SGB"][j % 2]
                P.op("act", lambda e, bg=bg, sgt=sgt: e.activation(out=sgt[:, 0:T], in_=bg[:, 0:T], func=AF.Silu), reads=[BG], writes=[SB_])
                P.op("dve", lambda e, j=j, bu=bu, sgt=sgt: e.tensor_tensor(out=aT[:, j, 0:T], in0=bu[:, 0:T], in1=sgt[:, 0:T], op=ALU.mult),
                     reads=[BU, SB_], writes=[AB[j]])
        if debug and not _dbg_done[0]:
            _dbg_done[0] = True
            P.dma("pool", "dbg", [(dbg_nT[:, :], nT[:, :, :].rearrange("p c t -> p (c t)")), (dbg_aT[:, :], aT[:, :, :].rearrange("p c t -> p (c t)")),
                                  (dbg_rs[:, :], E["rstd"][:, :])], reads=NTB[0:ntiles] + AB + E["RSB"])
        if mid is not None:
            mid()
        banks = [[ps.get() for _ in range(2)] for _ in range(ntiles)]
        for dg in range(6):
            nj = 4 if dg < 5 else 2
            sd_v, SDB = ws.next("down")
            for jj in range(nj):
                j = dg * 4 + jj
                for t in range(ntiles):
                    for hf in range(2):
                        bk, BK = banks[t][hf]
                        P.op("pe", lambda e, j=j, jj=jj, t=t, hf=hf, bk=bk, sd_v=sd_v: e.matmul(
                            bk[:, :], lhsT=aT[:, j, t * 128:(t + 1) * 128], rhs=sd_v[:, jj, hf * 512:(hf + 1) * 512], start=(j == 0), stop=(j == NFC - 1)),
                             reads=[AB[j], SDB], writes=[BK] if j == 0 else [])
        lastpe = ("e", "pe", len(P.q["pe"]) - 1)
        for t in range(ntiles):
            for hf in range(2):
                banks[t][hf][1].w = {"pe": lastpe}

        def evac(t):
            for hf in range(2):
                bk, BK = banks[t][hf]
                P.op("dve", lambda e, t=t, hf=hf, bk=bk: e.scalar_tensor_tensor(out=xt[:, t, hf * 512:(hf + 1) * 512], in0=bk[:, :], scalar=0.5,
                                                                                 in1=xt[:, t, hf * 512:(hf + 1) * 512], op0=ALU.mult, op1=ALU.add),
                     reads=[BK, XB[t]], writes=[XB[t]])
        if defer:
            return evac
        for t in range(ntiles):
            evac(t)
        return None

    def common_env(st, ntiles, with_ffn=True):
        E = {}
        sb = lambda n, s, d: st.enter_context(_sb(n, s, d))
        E["ps"] = PS(st)
        E["ident"] = sb("ident", [128, 128], BF16)
        E["IB"] = Buf()
        identf = sb("identf", [128, 128], F32)
        IFB = Buf()
        P.dma("sp", "c_ident", [(identf[:], ident_in[:, :])], writes=[IFB])
        P.op("dve", lambda e: e.tensor_copy(out=E["ident"][:], in_=identf[:]), reads=[IFB], writes=[E["IB"]])
        E["ss"] = sb("ss", [128, 8], F32)
        E["SSB"] = bufs(4)
        E["SSB2"] = bufs(4)
        E["RSB2"] = bufs(4)
        E["nth"] = [sb("nth%d" % i, [128, 1024], BF16) for i in range(4)]
        E["NHB"] = bufs(4)
        E["rstd"] = sb("rstd", [128, 8], F32)
        E["RSB"] = bufs(4)
        E["junk"] = sb("junk", [128, 1024], BF16)
        E["JB"] = Buf()
        E["epsc"] = sb("epsc", [128, 1], F32)
        E["EPSB"] = Buf()
        P.op("dve", lambda e: e.memset(E["epsc"][:], EPS), writes=[E["EPSB"]])
        E["nt"] = [sb("nt%d" % i, [128, 1024], BF16) for i in range(2)]
        E["NB"] = bufs(2)
        E["nT"] = sb("nT", [128, 8, ntiles * 128], BF16)
        E["NTB"] = bufs(ntiles)
        if with_ffn:
            E["aT"] = sb("aT", [128, NFC, ntiles * 128], BF16)
            E["AB"] = bufs(NFC)
            E["sg"] = [sb("sg%d" % i, [128, ntiles * 128], BF16) for i in range(2)]
            E["SGB"] = bufs(2)
        return E, sb

    def load_grep(sb, idxs):
        out = []
        for i in idxs:
            t = sb("grep%d" % i, [128, 1024], F32)
            B = Buf()
            src = bass.AP(norms.tensor, norms[i:i + 1, :].offset, [[0, 128], [1, 1024]])
            P.dma("sp", "c_grep%d" % i, [(t[:], src)], writes=[B])
            out.append((t[:], B))
        return out

    def loop_A(l, hin, pre_ffn2):
        with ExitStack() as st:
            NT = 4
            T = 512
            E, sb = common_env(st, NT)
            ps = E["ps"]
            ws = WStream(st, NWSLOT, "wA")
            nsteps = NTOK // T
            gi = [3 * l, 3 * l + 1] + ([3 * (l - 1) + 2] if pre_ffn2 else [])
            greps = load_grep(sb, gi)
            xt = [sb("xt%d" % i, [128, NT, 1024], F32) for i in range(2)]
            XB = [bufs(NT) for _ in range(2)]
            aT, AB = E["aT"], E["AB"]
            uT = aT[:, 0:8, :]; UB = AB[0:8]
            gaT = aT[:, 8:16, :]; GAB = AB[8:16]
            yaT = sb("yaT", [128, 8, T], BF16); YAB = bufs(NT)
            vg = [sb("vg%d" % i, [128, 1024], F32) for i in range(2)]; VGB = bufs(2)
            vh = sb("vh", [128, NT, 1024], BF16); VHB = bufs(NT)
            NSTG = 3
            stg = [sb("stg%d" % i, [128, 8 * T], BF16) for i in range(NSTG)]; STB = bufs(NSTG)
            stg_n = [0]
            tmp = [sb("sgut%d" % i, [128, 4, 128], F32) for i in range(2)]; TMB = bufs(2)
            bst = sb("bst", [128, 12], F32); BSTB = bufs(2)
            mv = sb("mv", [128, 4], F32); MVB = bufs(2)
            lng_t = sb("lng_t", [128, 8], F32); LNGB = Buf()
            P.dma("sp", "c_lng", [(lng_t[:], lngT[l])], writes=[LNGB])
            bg_t = sb("bg_t", [128, 2, 8], F32); BGB = Buf()
            P.dma("sp", "c_bg", [(bg_t[:, 0, :], bgT[l, 0]), (bg_t[:, 1, :], bgT[l, 1])], writes=[BGB])
            wsf = sb("wsf", [128, 8, 128], F32); WSFB = Buf()
            P.dma("sp", "c_wsf", [(wsf[:], wsT[l].rearrange("g q p -> q g p"))], writes=[WSFB])
            wst = sb("wst", [128, 8, 128], BF16); WSTB = Buf()
            P.op("dve", lambda e: e.tensor_copy(out=wst[:], in_=wsf[:]), reads=[WSFB], writes=[WSTB])
            onesf = sb("onesf", [128, 1], F32); ONB = Buf()
            P.op("dve", lambda e: e.memset(onesf[:], 1.0), writes=[ONB])
            l2 = sb("l2", [2, 1024], F32); L2B = Buf()
            P.op("dve", lambda e: e.memset(l2[:], 1.0), writes=[L2B])
            P.dma("sp", "c_l2", [(l2[0:1, :], lnb[l:l + 1, :])], writes=[L2B])
            r2 = sb("r2", [2, 8, 128], F32); R2B = Buf()
            P.dma("sp", "c_r2", [(r2[1:2, :, :], bs[l:l + 1, :, :])], writes=[R2B])
            bias2 = sb("bias2", [128, 8, 128], F32); B2B = Buf()
            for half in range(2):
                bk, BK = ps.get()
                for g4 in range(4):
                    g = half * 4 + g4
                    P.op("pe", lambda e, g=g, g4=g4, bk=bk: e.matmul(bk[0:1, g4 * 128:(g4 + 1) * 128], lhsT=onesf[:, 0:1], rhs=wsf[:, g, :], start=True, stop=True),
                         reads=[ONB, WSFB], writes=[BK] if g4 == 0 else [])
                BK.w = {"pe": ("e", "pe", len(P.q["pe"]) - 1)}
                P.op("dve", lambda e, half=half, bk=bk: e.tensor_copy(out=r2[0:1, half * 4:(half + 1) * 4, :], in_=bk[0:1, :].rearrange("p (a b) -> p a b", a=4)),
                     reads=[BK], writes=[])
            R2B.w["dve"] = ("e", "dve", len(P.q["dve"]) - 1)
            for half in range(2):
                bk, BK = ps.get()
                for g4 in range(4):
                    g = half * 4 + g4
                    P.op("pe", lambda e, g=g, g4=g4, bk=bk: e.matmul(bk[:, g4 * 128:(g4 + 1) * 128], lhsT=l2[0:2, g * 128:(g + 1) * 128], rhs=r2[0:2, g, :], start=True, stop=True),
                         reads=[L2B, R2B], writes=[BK] if g4 == 0 else [])
                BK.w = {"pe": ("e", "pe", len(P.q["pe"]) - 1)}
                P.op("dve", lambda e, half=half, bk=bk: e.tensor_copy(out=bias2[:, half * 4:(half + 1) * 4, :], in_=bk[:, :].rearrange("p (a b) -> p a b", a=4)),
                     reads=[BK], writes=[])
            B2B.w = {"dve": ("e", "dve", len(P.q["dve"]) - 1)}

            win_l = wb["w_in"][l]
            for s in range(nsteps):
                if pre_ffn2:
                    plan_ffn(ws, wb["ffn2_w_gate"][l - 1], wb["ffn2_w_up"][l - 1], wb["ffn2_w_down"][l - 1])
                plan_ffn(ws, wb["ffn1_w_gate"][l], wb["ffn1_w_up"][l], wb["ffn1_w_down"][l])
                for cg in (2, 3, 8, 9, 0, 1, 4, 5, 6, 7, 10, 11, 12, 13):
                    ws.add("up", w_up_src(win_l, cg * 512, 512))
                for cg in range(2):
                    ws.add("up", w_up_src(wb["w_branch_a"][l], cg * 512, 512))

            def load_x(s):
                sl = s % 2
                tok0 = s * T
                pairs = [(xt[sl][:, t, :], hin[tok0 + t * 128: tok0 + (t + 1) * 128, :]) for t in range(NT)]
                P.dma("sp", "ldx%d" % sl, pairs, writes=XB[sl])

            def stage():
                i = stg_n[0] % NSTG
                stg_n[0] += 1
                return stg[i], STB[i], "st_stg%d" % i

            def lastref(eng):
                return ("e", eng, len(P.q[eng]) - 1)

            load_x(0)
            for s in range(nsteps):
                sl = s % 2
                tok0 = s * T
                x, XBs = xt[sl], XB[sl]
                if s + 1 < nsteps:
                    load_x(s + 1)
                nT, NTB = E["nT"], E["NTB"]
                hoisted = s > 0
                pv = None
                if pre_ffn2:
                    pv = ffn(E, x, XBs, greps[2][0], greps[2][1], ws, NT, skip_norm=hoisted, defer=True)
                    pv = ffn(E, x, XBs, greps[0][0], greps[0][1], ws, NT, pre=pv, defer=True)
                else:
                    pv = ffn(E, x, XBs, greps[0][0], greps[0][1], ws, NT, skip_norm=hoisted, defer=True)
                norm_transpose(E, x, XBs, greps[1][0], greps[1][1], nT, NTB, NT, pre=pv)
                P.dma("pool", "st_h1_%d" % sl, [(h1s[tok0 + t * 128: tok0 + (t + 1) * 128, :], x[:, t, :]) for t in range(NT)], reads=XBs)
                gfirst = greps[2] if pre_ffn2 else greps[0]
                if s + 1 < nsteps:
                    norm_stats_early(E, xt[(s + 1) % 2], XB[(s + 1) % 2], gfirst[0], gfirst[1], NT)

                def fm_group(cgi, evac):
                    sv, SVB = ws.next("up")
                    for jj in range(4):
                        bk, BK = ps.get()
                        for dc in range(8):
                            P.op("pe", lambda e, dc=dc, jj=jj, bk=bk, sv=sv: e.matmul(bk[:, :], lhsT=sv[:, dc, jj * 128:(jj + 1) * 128], rhs=nT[:, dc, :],
                                                                                      start=(dc == 0), stop=(dc == 7)),
                                 reads=NTB + [SVB], writes=[BK] if dc == 0 else [])
                        BK.w = {"pe": lastref("pe")}
                        evac(cgi * 4 + jj, bk, BK)

                def tm_group(half, evac):
                    sv, SVB = ws.next("up")
                    for t in range(NT):
                        bk, BK = ps.get()
                        for dc in range(8):
                            P.op("pe", lambda e, dc=dc, t=t, bk=bk, sv=sv: e.matmul(bk[:, :], lhsT=nT[:, dc, t * 128:(t + 1) * 128], rhs=sv[:, dc, :],
                                                                                    start=(dc == 0), stop=(dc == 7)),
                                 reads=[NTB[t], SVB], writes=[BK] if dc == 0 else [])
                        BK.w = {"pe": lastref("pe")}
                        evac(t, half, bk, BK)

                sv0, SVB0 = ws.next("up")
                sv1, SVB1 = ws.next("up")
                for t in range(NT):
                    i = t % 2
                    for half, (sv, SVB) in enumerate(((sv0, SVB0), (sv1, SVB1))):
                        bk, BK = ps.get()
                        for dc in range(8):
                            P.op("pe", lambda e, dc=dc, t=t, bk=bk, sv=sv: e.matmul(bk[:, :], lhsT=nT[:, dc, t * 128:(t + 1) * 128], rhs=sv[:, dc, :],
                                                                                    start=(dc == 0), stop=(dc == 7)),
                                 reads=[NTB[t], SVB], writes=[BK] if dc == 0 else [])
                        BK.w = {"pe": lastref("pe")}
                        if half == 0:
                            P.op("act", lambda e, i=i, bk=bk: e.activation(out=vg[i][:, 0:512], in_=bk[:, :], func=AF.Gelu), reads=[BK], writes=[VGB[i]])
                        else:
                            P.op("act", lambda e, i=i, bk=bk: e.activation(out=vg[i][:, 512:1024], in_=bk[:, :], func=AF.Gelu), reads=[BK], writes=[])
                            VGB[i].w = {"act": lastref("act")}
                    P.op("dve", lambda e, i=i: e.bn_stats(out=bst[:, 0:6], in_=vg[i][:, 0:512]), reads=[VGB[i]], writes=[BSTB[0]])
                    P.op("dve", lambda e, i=i: e.bn_stats(out=bst[:, 6:12], in_=vg[i][:, 512:1024]), reads=[VGB[i]], writes=[BSTB[1]])
                    P.op("dve", lambda e: e.bn_aggr(out=mv[:, 0:2], in_=bst[:, 0:12]), reads=BSTB, writes=[MVB[0]])
                    P.op("act", lambda e: e.activation(out=mv[:, 2:3], in_=mv[:, 1:2], func=AF.Sqrt, scale=1.0, bias=E["epsc"][:, 0:1]),
                         reads=[MVB[0], E["EPSB"]], writes=[MVB[1]])
                    P.op("dve", lambda e: e.reciprocal(out=mv[:, 3:4], in_=mv[:, 2:3]), reads=[MVB[1]], writes=[MVB[1]])
                    P.op("dve", lambda e, t=t, i=i: e.tensor_scalar(out=vh[:, t, :], in0=vg[i][:, :], scalar1=mv[:, 0:1], scalar2=mv[:, 3:4],
                                                                    op0=ALU.subtract, op1=ALU.mult),
                         reads=[VGB[i], MVB[0], MVB[1]], writes=[VHB[t]])
                vst, VSB, vsem = stage()
                vv = vst[:].rearrange("p (t f) -> p t f", t=NT)
                first = [True]

                def evac_vv(t, half, bk, BK):
                    P.op("dve", lambda e, t=t, half=half, bk=bk, vv=vv: e.tensor_copy(out=vv[:, t, half * 512:(half + 1) * 512], in_=bk[:, :]),
                         reads=[BK], writes=[VSB] if first[0] else [])
                    first[0] = False
                for half in (0, 1):
                    tm_group(half, evac_vv)
                VSB.w = {"dve": lastref("dve")}
                P.dma("pool", vsem, [(vs[tok0 // 128: tok0 // 128 + NT, :, :].rearrange("t p f -> p t f"), vv)], reads=[VSB])
                for cgi in (0, 1):
                    fm_group(cgi, lambda c, bk, BK: P.op("act", lambda e, c=c, bk=bk: e.activation(out=uT[:, c, :], in_=bk[:, :], func=AF.Gelu),
                                                         reads=[BK], writes=[UB[c]]))
                for t in range(NT):
                    for half in range(2):
                        bk, BK = ps.get()
                        for g4 in range(4):
                            g = half * 4 + g4
                            P.op("pe", lambda e, g=g, g4=g4, t=t, bk=bk: e.matmul(bk[:, g4 * 128:(g4 + 1) * 128], lhsT=vh[:, t, g * 128:(g + 1) * 128], rhs=wst[:, g, :],
                                                                                  start=True, stop=True),
                                 reads=[VHB[t], WSTB], writes=[BK] if g4 == 0 else [])
                        BK.w = {"pe": lastref("pe")}
                        tm, TB = tmp[half], TMB[half]
                        for g4 in range(4):
                            g = half * 4 + g4
                            P.op("dve", lambda e, g=g, g4=g4, bk=bk, tm=tm: e.scalar_tensor_tensor(out=tm[:, g4, :], in0=bk[:, g4 * 128:(g4 + 1) * 128], scalar=lng_t[:, g:g + 1],
                                                                                                  in1=bias2[:, g, :], op0=ALU.mult, op1=ALU.add),
                                 reads=[BK, LNGB, B2B], writes=[TB] if g4 == 0 else [])
                        TB.w = {"dve": lastref("dve")}
                        P.op("dve", lambda e, half=half, t=t, tm=tm: e.tensor_tensor(out=yaT[:, half * 4:(half + 1) * 4, t * 128:(t + 1) * 128], in0=tm[:, :, :],
                                                                                     in1=uT[:, half * 4:(half + 1) * 4, t * 128:(t + 1) * 128], op=ALU.mult),
                             reads=[TB] + UB[half * 4:(half + 1) * 4], writes=[YAB[t]] if half == 0 else [])
                    YAB[t].w = {"dve": lastref("dve")}
                qst, QSB, qsem = stage()
                qv = qst[:].rearrange("p (c t) -> p c t", c=8)
                for cgi in (0, 1):
                    fm_group(cgi, lambda c, bk, BK: P.op("dve", lambda e, c=c, bk=bk, qv=qv: e.tensor_scalar(out=qv[:, c, :], in0=bk[:, :], scalar1=0.125, scalar2=None, op0=ALU.mult),
                                                         reads=[BK], writes=[QSB] if c == 0 else []))
                QSB.w = {"dve": lastref("dve")}
                P.dma("pool", qsem, [(qs[:, :, tok0:tok0 + T].rearrange("c p t -> p c t"), qv)], reads=[QSB])
                kst, KSB, ksem = stage()
                kv = kst[:].rearrange("p (t c k) -> p t c k", t=NT, c=8)
                for cgi in (0, 1):
                    fm_group(cgi, lambda c, bk, BK: P.op("dve", lambda e, c=c, bk=bk, kv=kv: e.tensor_copy(out=kv[:, :, c, :], in_=bk[:, :].rearrange("p (t k) -> p t k", t=NT)),
                                                         reads=[BK], writes=[KSB] if c == 0 else []))
                KSB.w = {"dve": lastref("dve")}
                P.dma("pool", ksem, [(ks[tok0 // 128: tok0 // 128 + NT, :, :].rearrange("t p f -> p t f"), kst[:].rearrange("p (t f) -> p t f", t=NT))], reads=[KSB])
                for cgi in (0, 1):
                    fm_group(cgi, lambda c, bk, BK: P.op("act", lambda e, c=c, bk=bk: e.activation(out=gaT[:, c, :], in_=bk[:, :], func=AF.Sigmoid, bias=bg_t[:, 0, c:c + 1], scale=1.0),
                                                         reads=[BK, BGB], writes=[GAB[c]]))
                gst, GSB, gsem = stage()
                gv = gst[:].rearrange("p (c t) -> p c t", c=8)
                for cgi in (0, 1):
                    fm_group(cgi, lambda c, bk, BK: P.op("act", lambda e, c=c, bk=bk, gv=gv: e.activation(out=gv[:, c, :], in_=bk[:, :], func=AF.Sigmoid, bias=bg_t[:, 1, c:c + 1], scale=1.0),
                                                         reads=[BK, BGB], writes=[GSB] if c == 0 else []))
                GSB.w = {"act": lastref("act")}
                P.dma("pool", gsem, [(gbs[:, :, tok0:tok0 + T].rearrange("c p t -> p c t"), gv)], reads=[GSB])
                if s + 1 < nsteps:
                    norm_finish_late(E, nT, NTB, NT)
                mst, MSB, msem = stage()
                mvv = mst[:].rearrange("p (c t) -> p c t", c=8)
                for cg in range(2):
                    sv, SVB = ws.next("up")
                    for jj in range(4):
                        c = cg * 4 + jj
                        bk, BK = ps.get()
                        for cc in range(8):
                            P.op("pe", lambda e, cc=cc, jj=jj, bk=bk, sv=sv: e.matmul(bk[:, :], lhsT=sv[:, cc, jj * 128:(jj + 1) * 128], rhs=yaT[:, cc, :],
                                                                                      start=(cc == 0), stop=(cc == 7)),
                                 reads=YAB + [SVB], writes=[BK] if cc == 0 else [])
                        BK.w = {"pe": lastref("pe")}
                        P.op("dve", lambda e, c=c, bk=bk, mvv=mvv: e.tensor_tensor(out=mvv[:, c, :], in0=bk[:, :], in1=gaT[:, c, :], op=ALU.mult),
                             reads=[BK, GAB[c]], writes=[MSB] if c == 0 else [])
                MSB.w = {"dve": lastref("dve")}
                P.dma("pool", msem, [(mas[:, :, tok0:tok0 + T].rearrange("c p t -> p c t"), mvv)], reads=[MSB])
            assert ws.cur == len(ws.plan)
            P.barrier()

    def loop_B(l, hout):
        with ExitStack() as st:
            sb = lambda n, s, d: st.enter_context(_sb(n, s, d))
            ps = PS(st)
            NQR = 4
            T = 256
            nsteps = NROW // NQR
            NSLOT = 8
            wbt = sb("wbt", [128, 8, 1024], BF16); WBB = Buf()
            wot = sb("wot", [128, 8, 1024], BF16); WOB = Buf()
            P.dma("sp", "c_wb", [(wbt[:], wb["w_branch_b"][l].rearrange("(a p) n -> p a n", p=128))], writes=[WBB], waits=[cast_ref])
            P.dma("sp", "c_wo", [(wot[:], wb["w_out"][l].rearrange("(a p) n -> p a n", p=128))], writes=[WOB], waits=[cast_ref])
            mm_t = sb("mm_t", [128, 2], F32); MMB = Buf()
            P.dma("sp", "c_mm", [(mm_t[:], mm_in[:, :])], writes=[MMB])
            cm_t = sb("cm_t", [128, 64], F32); CMB = Buf()
            P.dma("sp", "c_cm", [(cm_t[:], colmask_in[:, :])], writes=[CMB])
            etab = sb("etab", [128, NH * 15 * 64], BF16); ETB = Buf()
            with ExitStack() as st2:
                hk = [st2.enter_context(_sb("hk%d" % i, [128, 4 * 15 * 64], F32)) for i in range(2)]; HKB = bufs(2)
                for hg in range(4):
                    i = hg % 2
                    base = rpbpad[l:l + 1, hg * 4 * 465: hg * 4 * 465 + 1].offset
                    src = bass.AP(rpbpad.tensor, base, [[1, 64], [465, 4], [31, 15], [1, 64]])
                    dst0 = hk[i][0:64, :].rearrange("p (h r q) -> p h r q", h=4, r=15)
                    dst1 = hk[i][64:128, :].rearrange("p (h r q) -> p h r q", h=4, r=15)
                    src_hi = bass.AP(rpbpad.tensor, base + 31, [[1, 64], [465, 4], [31, 15], [1, 64]])
                    P.dma("sp", "c_hk%d" % i, [(dst0, src), (dst1, src_hi)], writes=[HKB[i]])
                    P.op("act", lambda e, i=i: e.activation(out=hk[i][:, :], in_=hk[i][:, :], func=AF.Exp), reads=[HKB[i]], writes=[HKB[i]])
                    a = hk[i][:, 0:64]
                    rev = bass.AP(a.tensor, a.offset + 63, [list(a.ap[0]), [64, 60], [-1, 64]])
                    c0_ = cm_t[:, :]
                    cmb = bass.AP(c0_.tensor, c0_.offset, [list(c0_.ap[0]), [0, 60], [1, 64]])
                    P.op("dve", lambda e, hg=hg, rev=rev, cmb=cmb: e.tensor_tensor(out=etab[:, hg * 3840:(hg + 1) * 3840].rearrange("p (a q) -> p a q", q=64), in0=rev, in1=cmb, op=ALU.mult),
                         reads=[HKB[i], CMB], writes=[])
                    HKB[i].r["dve"] = ("e", "dve", len(P.q["dve"]) - 1)
                ETB.w = {"dve": ("e", "dve", len(P.q["dve"]) - 1)}
                P.barrier()
            kring = sb("kring", [128, NSLOT, 1024], BF16); KRB = bufs(NSLOT)
            vring = sb("vring", [128, NSLOT, 8, 192], BF16); VRB = bufs(NSLOT); VONE = Buf()
            P.op("dve", lambda e: e.memset(vring[:, :, :, 64:128], 1.0), writes=[VONE])
            qt = [sb("qt%d" % i, [128, 8, T], BF16) for i in range(2)]; QB = bufs(2)
            gbt = [sb("gbt%d" % i, [128, 8, T], BF16) for i in range(2)]; GBB = bufs(2)
            mat = [sb("mat%d" % i, [128, 8, T], BF16) for i in range(2)]; MAB = bufs(2)
            ht = [sb("ht%d" % i, [128, 2, 1024], F32) for i in range(2)]; HB = [bufs(2) for _ in range(2)]
            NG = 3
            NPX = 3 * NG
            pexp = [sb("pexp%d" % i, [128, 512], BF16) for i in range(NPX)]; PXB = bufs(NPX)
            pm = [sb("pm%d" % i, [128, 512], BF16) for i in range(NPX)]; PMB = bufs(NPX)
            pmzero = [set() for _ in range(NPX)]
            rr = [sb("rr%d" % i, [128, T], F32) for i in range(2)]; RRB = bufs(2)
            ybT = sb("ybT", [128, 8, T], BF16); YBB = bufs(8)
            mT = sb("mT", [128, 8, T], BF16); MTB = bufs(8)
            loaded = [-1]
            plans = [attn_plan(s * NQR, NQR, segs_modes) for s in range(nsteps)]

            def load_chunks(upto):
                while loaded[0] < upto:
                    c = loaded[0] + 1
                    sl = c % NSLOT
                    P.dma("sp", "ldk%d" % sl, [(kring[:, sl, :], ks[c])], writes=[KRB[sl]])
                    vsrc = vs[c].rearrange("p (a b f) -> p a b f", a=8, b=2)
                    P.dma("sp", "ldv%d" % sl, [(vring[:, sl, :, 0:64], vsrc[:, :, 0, :]), (vring[:, sl, :, 128:192], vsrc[:, :, 1, :])], writes=[VRB[sl]])
                    loaded[0] = c

            def load_step(s):
                i = s % 2
                tok0 = s * T
                P.dma("sp", "ldq%d" % i, [(qt[i][:], qs[:, :, tok0:tok0 + T].rearrange("c p t -> p c t"))], writes=[QB[i]])
                load_chunks(max(p["c"] for p in plans[s]))
                P.dma("sp", "ldgb%d" % i, [(gbt[i][:], gbs[:, :, tok0:tok0 + T].rearrange("c p t -> p c t"))], writes=[GBB[i]])
                P.dma("sp", "ldma%d" % i, [(mat[i][:], mas[:, :, tok0:tok0 + T].rearrange("c p t -> p c t"))], writes=[MAB[i]])
                P.dma("sp", "ldh%d" % i, [(ht[i][:, t, :], h1s[tok0 + t * 128: tok0 + (t + 1) * 128, :]) for t in range(2)], writes=HB[i])

            load_step(0)
            pxn = [0]
            for s in range(nsteps):
                i = s % 2
                tok0 = s * T
                r0 = s * NQR
                if s + 1 < nsteps:
                    load_step(s + 1)
                plan = plans[s]
                groups = []
                cur, used = [], 0
                for p in plan:
                    nqc = (p["qb"] - p["qa"] + 1) * 64
                    if used + nqc > 512:
                        groups.append(cur)
                        cur, used = [], 0
                    cur.append((p, used, nqc))
                    used += nqc
                groups.append(cur)

                has_special = any(r[3] != "b" for p in plan for r in p["runs"])

                def emit_scores(h):
                    hp, dc = h % 2, h // 2
                    res = []
                    for grp in groups:
                        bk, BK = ps.get()
                        for gi_, (p, off, nqc) in enumerate(grp):
                            sl = p["c"] % NSLOT
                            c0 = (p["qa"] - r0) * 64
                            P.op("pe", lambda e, bk=bk, off=off, nqc=nqc, sl=sl, c0=c0, hp=hp, dc=dc, i=i: e.matmul(
                                bk[:, off:off + nqc], lhsT=kring[hp * 64:(hp + 1) * 64, sl, dc * 128:(dc + 1) * 128],
                                rhs=qt[i][hp * 64:(hp + 1) * 64, dc, c0:c0 + nqc], start=True, stop=True),
                                 reads=[KRB[sl], QB[i]], writes=[BK] if gi_ == 0 else [])
                        BK.w = {"pe": ("e", "pe", len(P.q["pe"]) - 1)}
                        res.append((bk, BK, grp))
                    return res

                def emit_x(h, sc):
                    eng = "dve"
                    assert len(sc) <= NG
                    for gix, (bk, BK, grp) in enumerate(sc):
                        k = (h % 3) * NG + gix
                        tot = grp[-1][1] + grp[-1][2]
                        P.op("act", lambda e, bk=bk, k=k, tot=tot: e.activation(out=pexp[k][:, 0:tot], in_=bk[:, 0:tot], func=AF.Exp), reads=[BK], writes=[PXB[k]])
                        nops = 0
                        for (p, off, nqc) in grp:
                            cls_of = ({}, {})
                            for (hf, qf, ql, cls) in p["runs"]:
                                for qr in range(qf, ql + 1):
                                    cls_of[hf][qr] = cls
                            oplist = []
                            qr = p["qa"]
                            while qr <= p["qb"]:
                                if cls_of[0].get(qr) == "b" and cls_of[1].get(qr) == "b":
                                    q2 = qr
                                    while cls_of[0].get(q2 + 1) == "b" and cls_of[1].get(q2 + 1) == "b":
                                        q2 += 1
                                    oplist.append((0, 128, qr, q2, "b"))
                                    for qq in range(qr, q2 + 1):
                                        del cls_of[0][qq]
                                        del cls_of[1][qq]
                                    qr = q2 + 1
                                else:
                                    qr += 1
                            for hf in range(2):
                                rows = sorted(cls_of[hf])
                                j0 = 0
                                while j0 < len(rows):
                                    j1 = j0
                                    while j1 + 1 < len(rows) and rows[j1 + 1] == rows[j1] + 1 and cls_of[hf][rows[j1 + 1]] == cls_of[hf][rows[j0]]:
                                        j1 += 1
                                    oplist.append((hf * 64, hf * 64 + 64, rows[j0], rows[j1], cls_of[hf][rows[j0]]))
                                    j0 = j1 + 1
                            for (plo, phi, qf, ql, cls) in oplist:
                                n = ql - qf + 1
                                ri0 = 2 * p["c"] - qf + 7
                                assert 0 <= ri0 - (n - 1) and ri0 <= 14
                                co = off + (qf - p["qa"]) * 64
                                eb = etab[plo:phi, (h * 15 + ri0) * 64:(h * 15 + ri0) * 64 + 64]
                                ein = bass.AP(eb.tensor, eb.offset, [list(eb.ap[0]), [-64, n], [1, 64]])
                                o_ap = pm[k][plo:phi, co:co + n * 64].rearrange("p (a q) -> p a q", q=64)
                                for hf in range(plo // 64, phi // 64):
                                    for bb in range(n):
                                        pmzero[k].discard((hf, co // 64 + bb))
                                i_ap = pexp[k][plo:phi, co:co + n * 64].rearrange("p (a q) -> p a q", q=64)
                                if cls == "b":
                                    P.op(eng, lambda e, o_ap=o_ap, i_ap=i_ap, ein=ein: e.tensor_tensor(out=o_ap, in0=i_ap, in1=ein, op=ALU.mult),
                                         reads=[PXB[k], ETB], writes=[PMB[k]] if nops == 0 else [])
                                else:
                                    P.op(eng, lambda e, o_ap=o_ap, i_ap=i_ap, ein=ein, cls=cls, plo=plo, phi=phi: e.scalar_tensor_tensor(
                                        out=o_ap, in0=i_ap, scalar=mm_t[plo:phi, cls:cls + 1], in1=ein, op0=ALU.mult, op1=ALU.mult),
                                         reads=[PXB[k], ETB, MMB], writes=[PMB[k]] if nops == 0 else [])
                                nops += 1
                            for (hf, qr) in p["zero"]:
                                co = off + (qr - p["qa"]) * 64
                                if (hf, co // 64) in pmzero[k]:
                                    continue
                                pmzero[k].add((hf, co // 64))
                                P.op(eng, lambda e, k=k, hf=hf, co=co: e.memset(pm[k][hf * 64:(hf + 1) * 64, co:co + 64], 0.0),
                                     reads=[], writes=[PMB[k]] if nops == 0 else [])
                                nops += 1
                        PMB[k].w = {eng: ("e", eng, len(P.q[eng]) - 1)}
                        PXB[k].r[eng] = ("e", eng, len(P.q[eng]) - 1)

                def emit_v(h, sc):
                    hp, dc = h % 2, h // 2
                    ob, OB = ps.get()
                    firstpv = True
                    for gix, (bk, BK, grp) in enumerate(sc):
                        k = (h % 3) * NG + gix
                        for gi_, (p, off, nqc) in enumerate(grp):
                            sl = p["c"] % NSLOT
                            lh = vring[:, sl, dc, 0:128] if hp == 0 else vring[:, sl, dc, 64:192]
                            c0 = (p["qa"] - r0) * 64
                            lastpv = (gix == len(sc) - 1) and (gi_ == len(grp) - 1)
                            P.op("pe", lambda e, ob=ob, c0=c0, nqc=nqc, lh=lh, k=k, off=off, fp=firstpv, lp=lastpv: e.matmul(
                                ob[:, c0:c0 + nqc], lhsT=lh, rhs=pm[k][:, off:off + nqc], start=fp, stop=lp),
                                 reads=[PMB[k], VRB[sl], VONE], writes=[OB] if firstpv else [])
                            firstpv = False
                    OB.w = {"pe": ("e", "pe", len(P.q["pe"]) - 1)}
                    return ob, OB

                def emit_n(h, ob, OB):
                    hp, dc = h % 2, h // 2
                    j = h % 2
                    slo, shi = (64, 128) if hp == 0 else (0, 64)
                    nlo, nhi = (0, 64) if hp == 0 else (64, 128)
                    P.op("act", lambda e, ob=ob, j=j, slo=slo, shi=shi: e.activation(out=rr[j][slo:shi, :], in_=ob[slo:shi, 0:T], func=AF.Ln), reads=[OB], writes=[RRB[j]])
                    P.op("act", lambda e, j=j, slo=slo, shi=shi: e.activation(out=rr[j][slo:shi, :], in_=rr[j][slo:shi, :], func=AF.Exp, scale=-1.0), reads=[RRB[j]], writes=[RRB[j]])
                    P.op("dve", lambda e, ob=ob, j=j, slo=slo, shi=shi, nlo=nlo, nhi=nhi, dc=dc: e.tensor_tensor(
                        out=ybT[nlo:nhi, dc, :], in0=ob[nlo:nhi, 0:T], in1=rr[j][slo:shi, :], op=ALU.mult),
                         reads=[OB, RRB[j]], writes=[YBB[dc]] if hp == 0 else [])
                    YBB[dc].w = {"dve": ("e", "dve", len(P.q["dve"]) - 1)}

                scs = {}
                scs[0] = emit_scores(0)
                scs[1] = emit_scores(1)
                emit_x(0, scs[0])
                scs[2] = emit_scores(2)
                emit_x(1, scs[1])
                for h in range(NH):
                    if h + 3 < NH:
                        scs[h + 3] = emit_scores(h + 3)
                    if h + 2 < NH:
                        emit_x(h + 2, scs[h + 2])
                    ob, OB = emit_v(h, scs[h])
                    emit_n(h, ob, OB)
                    del scs[h]
                for c in range(8):
                    bk, BK = ps.get()
                    for cc in range(8):
                        P.op("pe", lambda e, c=c, cc=cc, bk=bk: e.matmul(bk[:, 0:T], lhsT=wbt[:, cc, c * 128:(c + 1) * 128], rhs=ybT[:, cc, :], start=(cc == 0), stop=(cc == 7)),
                             reads=YBB + [WBB], writes=[BK] if cc == 0 else [])
                    BK.w = {"pe": ("e", "pe", len(P.q["pe"]) - 1)}
                    P.op("dve", lambda e, c=c, bk=bk, i=i: e.tensor_tensor(out=mT[:, c, :], in0=bk[:, 0:T], in1=gbt[i][:, c, :], op=ALU.mult), reads=[BK, GBB[i]], writes=[MTB[c]])
                    P.op("dve", lambda e, c=c, i=i: e.tensor_tensor(out=mT[:, c, :], in0=mT[:, c, :], in1=mat[i][:, c, :], op=ALU.add), reads=[MTB[c], MAB[i]], writes=[MTB[c]])
                for t in range(2):
                    for hf in range(2):
                        bk, BK = ps.get()
                        for cc in range(8):
                            P.op("pe", lambda e, t=t, hf=hf, cc=cc, bk=bk: e.matmul(bk[:, :], lhsT=mT[:, cc, t * 128:(t + 1) * 128], rhs=wot[:, cc, hf * 512:(hf + 1) * 512],
                                                                                    start=(cc == 0), stop=(cc == 7)),
                                 reads=MTB + [WOB], writes=[BK] if cc == 0 else [])
                        BK.w = {"pe": ("e", "pe", len(P.q["pe"]) - 1)}
                        P.op("dve", lambda e, t=t, hf=hf, bk=bk, i=i: e.tensor_tensor(out=ht[i][:, t, hf * 512:(hf + 1) * 512], in0=bk[:, :], in1=ht[i][:, t, hf * 512:(hf + 1) * 512], op=ALU.add),
                             reads=[BK, HB[i][t]], writes=[HB[i][t]])
                P.dma("pool", "st_h2_%d" % i, [(hout[tok0 + t * 128: tok0 + (t + 1) * 128, :], ht[i][:, t, :]) for t in range(2)], reads=HB[i])
            P.barrier()

    def loop_C(hin):
        with ExitStack() as st:
            NT = 4
            T = 512
            E, sb = common_env(st, NT)
            ws = WStream(st, NWSLOT, "wC")
            greps = load_grep(sb, [5, 6])
            xt = [sb("xc%d" % i, [128, NT, 1024], F32) for i in range(2)]
            XB = [bufs(NT) for _ in range(2)]
            yt = [sb("yc%d" % i, [128, NT, 1024], F32) for i in range(2)]
            YB = [bufs(NT) for _ in range(2)]
            nsteps = NTOK // T
            for s in range(nsteps):
                plan_ffn(ws, wb["ffn2_w_gate"][NL - 1], wb["ffn2_w_up"][NL - 1], wb["ffn2_w_down"][NL - 1])

            def load_x(s):
                sl = s % 2
                tok0 = s * T
                P.dma("sp", "ldc%d" % sl, [(xt[sl][:, t, :], hin[tok0 + t * 128: tok0 + (t + 1) * 128, :]) for t in range(NT)], writes=XB[sl])
            load_x(0)
            for s in range(nsteps):
                sl = s % 2
                tok0 = s * T
                if s + 1 < nsteps:
                    load_x(s + 1)
                x, XBs = xt[sl], XB[sl]
                nT, NTB = E["nT"], E["NTB"]
                mid = None
                if s + 1 < nsteps:
                    norm_stats_early(E, xt[(s + 1) % 2], XB[(s + 1) % 2], greps[0][0], greps[0][1], NT)
                    mid = (lambda: norm_finish_late(E, nT, NTB, NT))
                pv = ffn(E, x, XBs, greps[0][0], greps[0][1], ws, NT, skip_norm=(s > 0), defer=True, mid=mid)
                ss, SSB, rstd, RSB, junk, JB = E["ss"], E["SSB"], E["rstd"], E["RSB"], E["junk"], E["JB"]
                for t in range(NT):
                    pv(t)
                    P.op("act", lambda e, t=t, x=x: e.activation(out=junk[:], in_=x[:, t, :], func=AF.Square, accum_out=ss[:, t:t + 1]), reads=[XBs[t]], writes=[JB, SSB[t]])
                    P.op("act", lambda e, t=t: e.activation(out=rstd[:, t:t + 1], in_=ss[:, t:t + 1], func=AF.Sqrt, scale=1.0 / D, bias=E["epsc"][:, 0:1]),
                         reads=[SSB[t], E["EPSB"]], writes=[RSB[t]])
                for t in range(NT):
                    P.op("dve", lambda e, t=t: e.reciprocal(out=rstd[:, t:t + 1], in_=rstd[:, t:t + 1]), reads=[RSB[t]], writes=[RSB[t]])
                    P.op("dve", lambda e, t=t, x=x, sl=sl: e.scalar_tensor_tensor(out=yt[sl][:, t, :], in0=x[:, t, :], scalar=rstd[:, t:t + 1], in1=greps[1][0], op0=ALU.mult, op1=ALU.mult),
                         reads=[XBs[t], RSB[t], greps[1][1]], writes=[YB[sl][t]])
                P.dma("pool", "st_y%d" % sl, [(y_out[tok0 + t * 128: tok0 + (t + 1) * 128, :], yt[sl][:, t, :]) for t in range(NT)], reads=YB[sl])
            P.barrier()

    P.barrier()
    stop = cfg.get("stop", 5)
    loop_A(0, x_in, False)
    if stop >= 2:
        loop_B(0, h2s)
    if stop >= 3:
        loop_A(1, h2s, True)
    if stop >= 4:
        loop_B(1, h2s)
    if stop >= 5:
        loop_C(h2s)
    P.run()
    return nc


def _consts():
    ident = np.eye(128, dtype=np.float32)
    qc = np.arange(64)
    cs = np.clip(qc - 8, 0, 48)
    kc = np.arange(64)
    cm = ((kc[:, None] >= cs[None, :]) & (kc[:, None] < cs[None, :] + 16)).astype(np.float32)
    colmask = np.concatenate([cm, cm], axis=0)
    return ident, colmask


def _shared_inputs(inp):
    f = lambda a: np.ascontiguousarray(np.asarray(a, dtype=np.float32))
    sh = {}
    for k in ("ffn1_w_gate", "ffn1_w_up", "ffn1_w_down", "ffn2_w_gate", "ffn2_w_up", "ffn2_w_down", "w_in", "w_branch_a", "w_branch_b", "w_out"):
        sh[k] = f(inp[k])
    n = [inp["ffn1_norm"][0], inp["mix_norm"][0], inp["ffn2_norm"][0], inp["ffn1_norm"][1], inp["mix_norm"][1], inp["ffn2_norm"][1], inp["final_norm"]]
    sh["norms"] = f(np.stack([np.asarray(a) for a in n]))
    sh["lngT"] = f(np.asarray(inp["sgu_ln_g"]).reshape(NL, 8, 128).transpose(0, 2, 1))
    sh["lnb"] = f(inp["sgu_ln_b"])
    sh["bgT"] = f(np.asarray(inp["b_gate"]).reshape(NL, 2, 8, 128).transpose(0, 1, 3, 2))
    sh["wsT"] = f(np.asarray(inp["sgu_w_s"]).transpose(0, 1, 3, 2))
    sh["bs"] = f(inp["sgu_b_s"])
    rp = np.asarray(inp["nat_rpb"], dtype=np.float32).reshape(NL, -1)
    sh["rpbpad"] = f(np.pad(rp, ((0, 0), (48, 48 + 64))))
    ident, colmask = _consts()
    sh["ident"] = ident
    sh["colmask"] = colmask
    return sh


_NC_CACHE = {}


def run_cfg(cfg, xs, modes, inp, debug=False):
    key = (cfg["nrow"], cfg["segs"], debug, cfg.get("stop", 5))
    if key not in _NC_CACHE:
        _NC_CACHE[key] = build_program(cfg, debug=debug)
    nc = _NC_CACHE[key]
    sh = _shared_inputs(inp)
    in_maps = []
    for c in range(8):
        m = dict(sh)
        m["x"] = np.ascontiguousarray(xs[c], dtype=np.float32)
        mm = np.zeros((128, 2), np.float32)
        mm[:, modes[c]] = 1.0
        m["modemask"] = mm
        in_maps.append(m)
    res = run_bass_kernel_spmd(nc, in_maps, core_ids=list(range(8)))
    return res.results


def kernel(**inp):
    xp = np.asarray(inp["x_prompt"], dtype=np.float32)
    xsm = np.asarray(inp["x_sample"], dtype=np.float32)
    xs, modes = [], []
    for c in range(4):
        xs.append(np.concatenate([xp[c], xsm[c]], axis=0))
        modes.append(0)
    for c in range(4):
        xs.append(np.concatenate([xsm[4 + 3 * c + i] for i in range(3)], axis=0))
        modes.append(1)
    res = run_cfg(PROD_CFG, xs, modes, inp)
    yp = np.empty_like(xp)
    ys = np.empty_like(xsm)
    for c in range(4):
        y = res[c]["y"]
        yp[c] = y[0:8192]
        ys[c] = y[8192:]
    for c in range(4):
        y = res[4 + c]["y"]
        for i in range(3):
            ys[4 + 3 * c + i] = y[i * 4096:(i + 1) * 4096]
    return yp, ys
```
